# Optimizing a Trainium2 kernel written in Bass

```python
import jax, jax.numpy as jnp
from jax import lax
import numpy as np

D_MODEL = 1024
BATCH = 16
SEQ = 4096
DEPTH = 2

GRID_W = 64
EPS = 1e-6
D_MIX = D_MODEL
NA_HEAD_DIM = 64
NA_WIDTH = D_MIX // 2
NA_HEADS = NA_WIDTH // NA_HEAD_DIM
NA_KH = 8
NA_KW = 16
DN_HEAD_DIM = 128
DN_WIDTH = D_MIX - NA_WIDTH
DN_HEADS = DN_WIDTH // DN_HEAD_DIM
DN_CONV = 5
DN_CHUNK = 64
IN_SIZES = (NA_WIDTH,) * 4 + (DN_WIDTH,) * 4 + (DN_HEADS,) * 4
D_IN = sum(IN_SIZES)

kernel_name = 'hymba_natten_gdn_encoder'


def _rmsnorm(x, w):
    xf = x.astype(jnp.float32)
    y = xf * lax.rsqrt(jnp.mean(xf * xf, axis=-1, keepdims=True) + EPS) * w.astype(jnp.float32)
    return y.astype(x.dtype)


def _l2norm(t):
    return t * lax.rsqrt(jnp.sum(t * t, axis=-1, keepdims=True) + EPS)


def _split_proj(p):
    offsets = [int(o) for o in np.cumsum(IN_SIZES)[:-1]]
    return jnp.split(p, offsets, axis=-1)


def _neighbourhood_attention(q, k, v, rpb):
    B, L, H, Dh = q.shape
    rows = L // GRID_W
    kh = min(NA_KH, rows)
    to_grid = lambda t: t.reshape(B, rows, GRID_W, H, Dh).transpose(0, 3, 1, 2, 4)
    qg, kg, vg = to_grid(q), to_grid(k), to_grid(v)
    cols = np.arange(GRID_W)
    cs = np.clip(cols - NA_KW // 2, 0, GRID_W - NA_KW)
    col_idx = cs[:, None] + np.arange(NA_KW)[None, :]
    col_bias = rpb[:, :, col_idx - cols[:, None] + NA_KW - 1]
    scale = Dh ** -0.5

    def row_fn(r):
        rs = jnp.clip(r - kh // 2, 0, rows - kh)
        q_r = lax.dynamic_index_in_dim(qg, r, axis=2, keepdims=False)
        k_sel = jnp.take(lax.dynamic_slice_in_dim(kg, rs, kh, axis=2), col_idx, axis=3)
        v_sel = jnp.take(lax.dynamic_slice_in_dim(vg, rs, kh, axis=2), col_idx, axis=3)
        dr = rs + jnp.arange(kh) - r + NA_KH - 1
        bias = jnp.take(col_bias, dr, axis=1).transpose(0, 2, 1, 3)
        s = jnp.einsum('bhwd,bhiwjd->bhwij', q_r, k_sel).astype(jnp.float32) * scale + bias.astype(jnp.float32)
        p = jax.nn.softmax(s.reshape(B, H, GRID_W, kh * NA_KW), axis=-1).reshape(s.shape).astype(v.dtype)
        return jnp.einsum('bhwij,bhiwjd->bhwd', p, v_sel)

    o = lax.map(row_fn, jnp.arange(rows))
    return o.transpose(1, 0, 3, 2, 4).reshape(B, L, H * Dh)


def _centred_depthwise_conv(x, w):
    c = x.shape[-1]
    return lax.conv_general_dilated(
        x, w[:, None, :].astype(x.dtype), window_strides=(1,),
        padding=[(DN_CONV // 2, DN_CONV // 2)],
        dimension_numbers=('NWC', 'WIO', 'NWC'), feature_group_count=c)


def _gated_delta_chunked(q, k, v, g, beta):
    B, H, L, Dk = q.shape
    Dv = v.shape[-1]
    C = DN_CHUNK
    N = L // C
    q = q * (Dk ** -0.5)
    ch = lambda t: t.reshape((B, H, N, C) + t.shape[3:])
    q, k, v, g, beta = ch(q), ch(k), ch(v), ch(g), ch(beta)
    g = jnp.cumsum(g, axis=-1)
    tri = np.tril(np.ones((C, C), dtype=bool))
    strict = np.tril(np.ones((C, C), dtype=bool), -1)
    diff = g[..., :, None] - g[..., None, :]
    decay = jnp.where(tri, jnp.exp(jnp.where(tri, diff, 0.0)), 0.0)
    kb = k * beta[..., None]
    a_mat = jnp.where(strict, jnp.einsum('bhncd,bhnsd->bhncs', kb, k) * decay, 0.0) + jnp.eye(C, dtype=jnp.float32)
    rhs = jnp.concatenate([v * beta[..., None], kb * jnp.exp(g)[..., None]], axis=-1)
    sol = lax.linalg.triangular_solve(a_mat, rhs, left_side=True, lower=True, unit_diagonal=True)
    u, w = sol[..., :Dv], sol[..., Dv:]
    intra = jnp.einsum('bhncd,bhnsd->bhncs', q, k) * decay

    def step(S, inp):
        q_c, k_c, u_c, w_c, g_c, a_c = inp
        v_new = u_c - jnp.einsum('bhcd,bhde->bhce', w_c, S)
        o = jnp.einsum('bhcd,bhde->bhce', q_c * jnp.exp(g_c)[..., None], S) + jnp.einsum('bhcs,bhse->bhce', a_c, v_new)
        g_last = g_c[..., -1]
        S = S * jnp.exp(g_last)[..., None, None] + jnp.einsum(
            'bhcd,bhce->bhde', k_c * jnp.exp(g_last[..., None] - g_c)[..., None], v_new)
        return S, o

    xs = tuple(jnp.moveaxis(t, 2, 0) for t in (q, k, u, w, g, intra))
    S0 = jnp.zeros((B, H, Dk, Dv), jnp.float32)
    _, o = lax.scan(step, S0, xs)
    return jnp.moveaxis(o, 0, 2).reshape(B, H, L, Dv)


def _layer(x, norm_w, w_in, qk_gain_q, qk_gain_k, rpb, conv_w, a_log, dt_bias, dn_norm_w, w_out):
    B, L, _ = x.shape
    h = _rmsnorm(x, norm_w)
    proj = jnp.einsum('bld,de->ble', h, w_in)
    aq, ak, av, az, dq, dk, dv, dz, b_f, a_f, b_b, a_b = _split_proj(proj)
    heads = lambda t, n: t.reshape(B, L, n, -1)

    aq = _rmsnorm(heads(aq, NA_HEADS), qk_gain_q)
    ak = _rmsnorm(heads(ak, NA_HEADS), qk_gain_k)
    attn = _neighbourhood_attention(aq, ak, heads(av, NA_HEADS), rpb) * jax.nn.silu(az)

    qkv = jax.nn.silu(_centred_depthwise_conv(jnp.concatenate([dq, dk, dv], axis=-1), conv_w))
    dq, dk, dv = jnp.split(qkv, 3, axis=-1)
    to_bhld = lambda t: heads(t, DN_HEADS).astype(jnp.float32).transpose(0, 2, 1, 3)
    dq, dk, dv = _l2norm(to_bhld(dq)), _l2norm(to_bhld(dk)), to_bhld(dv)

    def gates(b, a, d):
        beta = jax.nn.sigmoid(b.astype(jnp.float32))
        g = -jnp.exp(a_log[d].astype(jnp.float32)) * jax.nn.softplus(a.astype(jnp.float32) + dt_bias[d].astype(jnp.float32))
        return g.transpose(0, 2, 1), beta.transpose(0, 2, 1)

    g_f, beta_f = gates(b_f, a_f, 0)
    g_b, beta_b = gates(b_b, a_b, 1)
    flip = lambda t: jnp.flip(t, axis=2)
    o_f = _gated_delta_chunked(dq, dk, dv, g_f, beta_f)
    o_b = flip(_gated_delta_chunked(flip(dq), flip(dk), flip(dv), flip(g_b), flip(beta_b)))
    o = (o_f + o_b).transpose(0, 2, 1, 3)
    o = o * lax.rsqrt(jnp.mean(o * o, axis=-1, keepdims=True) + EPS) * dn_norm_w.astype(jnp.float32)
    o = o * jax.nn.silu(heads(dz, DN_HEADS).astype(jnp.float32))
    delta = o.reshape(B, L, DN_WIDTH).astype(x.dtype)

    y = jnp.einsum('ble,ed->bld', jnp.concatenate([attn, delta], axis=-1), w_out)
    return x + y


def setup_inputs(seed: int = 0) -> dict:
    key = jax.random.key(seed)
    ks = jax.random.split(key, 12)
    f32 = jnp.float32
    x = jax.random.normal(ks[0], (BATCH, SEQ, D_MODEL), f32)
    norm_w = 1.0 + 0.02 * jax.random.normal(ks[1], (DEPTH, D_MODEL), f32)
    w_in = jax.random.normal(ks[2], (DEPTH, D_MODEL, D_IN), f32) * D_MODEL ** -0.5
    qk_gain_q = 1.0 + 0.02 * jax.random.normal(ks[3], (DEPTH, NA_HEAD_DIM), f32)
    qk_gain_k = 1.0 + 0.02 * jax.random.normal(ks[4], (DEPTH, NA_HEAD_DIM), f32)
    rpb = 0.02 * jax.random.normal(ks[5], (DEPTH, NA_HEADS, 2 * NA_KH - 1, 2 * NA_KW - 1), f32)
    conv_w = jax.random.normal(ks[6], (DEPTH, DN_CONV, 3 * DN_WIDTH), f32) * DN_CONV ** -0.5
    a_log = jnp.log(jax.random.uniform(ks[7], (DEPTH, 2, DN_HEADS), f32, 1.0, 16.0))
    u = jax.random.uniform(ks[8], (DEPTH, 2, DN_HEADS), f32)
    dt = jnp.exp(u * (jnp.log(0.1) - jnp.log(0.001)) + jnp.log(0.001))
    dt_bias = dt + jnp.log(-jnp.expm1(-dt))
    dn_norm_w = 1.0 + 0.02 * jax.random.normal(ks[9], (DEPTH, DN_HEAD_DIM), f32)
    w_out = jax.random.normal(ks[10], (DEPTH, D_MIX, D_MODEL), f32) * D_MIX ** -0.5
    return {'x': x, 'norm_w': norm_w, 'w_in': w_in, 'qk_gain_q': qk_gain_q, 'qk_gain_k': qk_gain_k,
            'rpb': rpb, 'conv_w': conv_w, 'a_log': a_log, 'dt_bias': dt_bias,
            'dn_norm_w': dn_norm_w, 'w_out': w_out}


def reference(x, norm_w, w_in, qk_gain_q, qk_gain_k, rpb, conv_w, a_log, dt_bias, dn_norm_w, w_out):
    for l in range(DEPTH):
        x = _layer(x, norm_w[l], w_in[l], qk_gain_q[l], qk_gain_k[l], rpb[l], conv_w[l],
                   a_log[l], dt_bias[l], dn_norm_w[l], w_out[l])
    return x
```

```python
import numpy as np
from contextlib import ExitStack
import concourse.bass as bass
import concourse.mybir as mybir
from concourse.bass_utils import run_bass_kernel_spmd

F32 = mybir.dt.float32
BF16 = mybir.dt.bfloat16
ALU = mybir.AluOpType
AF = mybir.ActivationFunctionType
AX = mybir.AxisListType

D = 1024
L = 4096
DIN = 4112
NT = L // 128
EPS = 1e-6
NCORES = 8
SEQ_PER_CORE = 2


class Res:
    __slots__ = ("name", "w", "r")

    def __init__(self, name=""):
        self.name = name
        self.w = None
        self.r = []


class Sched:
    NRING = 8

    def __init__(self, nc):
        self.nc = nc
        self.E = {"pe": nc.tensor, "act": nc.scalar, "dve": nc.vector,
                  "pool": nc.gpsimd, "sp": nc.sync}
        self.sems = {}
        self.cnt = {}
        for e in self.E:
            self.sems[e] = nc.alloc_semaphore(name=f"s_{e}")
            self.cnt[e] = 0
        self.ring = {}
        self.ring_i = {}
        for q in ("sp", "pool", "act"):
            self.ring[q] = []
            for i in range(self.NRING):
                k = f"d_{q}{i}"
                self.sems[k] = nc.alloc_semaphore(name=k)
                self.cnt[k] = 0
                self.ring[q].append(k)
            self.ring_i[q] = 0
        self.seen = {e: {} for e in self.E}
        self.grp = None
        self.n_ins = 0
        self.n_wait = 0

    def _need(self, eng, reads, writes):
        need = {}

        def add(tok):
            if tok is None:
                return
            k, v = tok
            if k == eng and eng == "pe":
                return
            if need.get(k, 0) < v:
                need[k] = v
        for r in reads:
            add(r.w)
        for w in writes:
            add(w.w)
            for t in w.r:
                if t[0] == eng:
                    continue
                add(t)
        return need

    def _emit_waits(self, eng, need):
        seen = self.seen[eng]
        for k, v in need.items():
            if seen.get(k, 0) < v:
                if self.grp is not None and self.grp[0] == eng and k == eng and v >= self.grp[1][1]:
                    raise RuntimeError("dependency inside open group on same engine")
                self.E[eng].wait_ge(self.sems[k], v)
                seen[k] = v
                self.n_wait += 1

    def _mark(self, tok, reads, writes):
        for r in reads:
            r.r.append(tok)
        for w in writes:
            w.w = tok
            w.r = []

    def op(self, eng, fn, reads=(), writes=()):
        need = self._need(eng, reads, writes)
        self._emit_waits(eng, need)
        ins = fn(self.E[eng])
        self.n_ins += 1
        if self.grp is not None and self.grp[0] == eng:
            tok = self.grp[1]
            self.grp[2] = ins
        else:
            self.cnt[eng] += 1
            tok = (eng, self.cnt[eng])
            ins.then_inc(self.sems[eng], 1)
        self._mark(tok, reads, writes)
        return ins

    def group(self, eng):
        s = self

        class G:
            def __enter__(self_):
                assert s.grp is None
                s.cnt[eng] += 1
                s.grp = [eng, (eng, s.cnt[eng]), None]

            def __exit__(self_, *a):
                g = s.grp
                s.grp = None
                if a[0] is not None:
                    return False
                assert g[2] is not None
                g[2].then_inc(s.sems[eng], 1)
                return False
        return G()

    def dma(self, q, out, in_, reads=(), writes=(), **kw):
        ring = self.ring[q]
        i = self.ring_i[q]
        self.ring_i[q] = i + 1
        k = ring[i % self.NRING]
        need = self._need(q, reads, writes)
        if self.cnt[k] > 0 and need.get(k, 0) < self.cnt[k]:
            need[k] = self.cnt[k]
        self._emit_waits(q, need)
        ins = self.E[q].dma_start(out=out, in_=in_, **kw)
        self.cnt[k] += 16
        ins.then_inc(self.sems[k], 16)
        tok = (k, self.cnt[k])
        self._mark(tok, reads, writes)
        self.n_ins += 1
        return ins

    def barrier(self):
        for e in self.E:
            need = {k: v for k, v in self.cnt.items() if v > 0 and not (k == e and e == "pe")}
            self._emit_waits(e, need)

    def finish(self):
        need = {k: v for k, v in self.cnt.items() if v > 0}
        self._emit_waits("sp", need)


class Ctx:
    pass


_uid = [0]
CPARTS = "012"


def _u(name):
    _uid[0] += 1
    return f"{name}_{_uid[0]}"


def build_program(n_layers=2, n_seq=SEQ_PER_CORE, phases="ABCD", debug=False):
    nc = bass.Bass("TRN2", target_bir_lowering=False)
    s = Sched(nc)
    c = Ctx()
    c.nc, c.s = nc, s
    c.n_seq = n_seq

    def din(name, shape, dt=F32):
        return nc.dram_tensor(name, list(shape), dt, kind="ExternalInput").ap()

    def dscr(name, shape, dt=F32):
        return nc.dram_tensor(name, list(shape), dt, kind="ExternalOutput" if debug else "Internal").ap()

    c.x = din("x", [n_seq, L, D])
    c.norm_w = din("norm_w", [2, D])
    c.w_in = din("w_in", [2, D, DIN])
    c.gq = din("qk_gain_q", [2, 64])
    c.gk = din("qk_gain_k", [2, 64])
    c.rpbg = din("rpb_g", [2, 8, 128, 1024])
    c.conv_w = din("conv_wT", [2, 1536, 5])
    c.a_log = din("a_log", [2, 8])
    c.dt_bias = din("dt_bias", [2, 8])
    c.dn_w = din("dn_norm_w", [2, 128])
    c.w_out = din("w_out", [2, D, D])
    c.c_ident = din("c_ident", [128, 128])
    c.c_masks = din("c_masks", [128, 16, 128])
    c.c_navalid = din("c_navalid", [128, 1024])
    c.out = nc.dram_tensor("out", [n_seq, L, D], F32, kind="ExternalOutput").ap()

    c.xmid = dscr("xmid", [n_seq, L, D])
    c.cin = dscr("cin", [n_seq, 1536, L + 4], BF16)
    c.qT = dscr("qT", [n_seq, 512, L], BF16)
    c.kT = dscr("kT", [n_seq, 512, L], BF16)
    c.v1 = dscr("v1", [n_seq, L, 520], BF16)
    c.mix = dscr("mix", [n_seq, L, D], BF16)
    c.gates = dscr("gates", [n_seq, L, 24])
    c.zst = dscr("zst", [n_seq, NT, 2, 128, 2048], BF16)
    c.ktd = dscr("ktd", [n_seq, L, 512], BF16)
    c.vtd = dscr("vtd", [n_seq, L, 512], BF16)
    c.ost = dscr("ost", [n_seq, 2, L, 512])
    c.r_xmid = [Res() for _ in range(n_seq)]
    c.r_cin = [Res() for _ in range(n_seq)]
    c.r_qT = [Res() for _ in range(n_seq)]
    c.r_kT = [Res() for _ in range(n_seq)]
    c.r_v1 = [Res() for _ in range(n_seq)]
    c.r_mix = [Res() for _ in range(n_seq)]
    c.r_gates = [Res() for _ in range(n_seq)]
    c.r_zst = [Res() for _ in range(n_seq)]
    c.r_ktd = [Res() for _ in range(n_seq)]
    c.r_vtd = [Res() for _ in range(n_seq)]
    c.r_ost = [Res() for _ in range(n_seq)]
    c.dbg = {}
    if debug:
        c.dbg_out = {}

    with ExitStack() as es0:
        c.pb = [es0.enter_context(nc.psum_tensor(f"pb{i}", [128, 512], F32)) for i in range(8)]
        c.pr = [Res(f"pb{i}") for i in range(8)]
        c.pi = 0
        c.ring_banks = list(range(8))

        def nextbank():
            i = c.ring_banks[c.pi % len(c.ring_banks)]
            c.pi += 1
            return c.pb[i], c.pr[i]
        c.nextbank = nextbank
        c.free_banks = list(range(8))

        def getbank():
            while not c.free_banks:
                yield
            i = c.free_banks.pop(0)
            return c.pb[i], c.pr[i]

        def relbank(pp):
            i = [k for k in range(8) if c.pb[k] is pp][0]
            assert i not in c.free_banks
            c.free_banks.append(i)
        c.getbank, c.relbank = getbank, relbank

        c.ident_f = es0.enter_context(nc.sbuf_tensor("ident_f", [128, 128], F32)); c.r_identf = Res()
        c.ident = es0.enter_context(nc.sbuf_tensor("ident", [128, 128], BF16)); c.r_ident = Res()
        s.dma("sp", c.ident_f[:], c.c_ident, writes=[c.r_identf])
        s.op("dve", lambda e: e.tensor_copy(c.ident[:], c.ident_f[:]), [c.r_identf], [c.r_ident])

        for l in range(n_layers):
            src = c.x if l == 0 else c.xmid
            dst = c.out if l == n_layers - 1 else c.xmid
            if "A" in phases:
                phase_A(c, l, src)
                s.barrier()
            if "B" in phases:
                phase_B(c, l)
                s.barrier()
            if "C" in phases:
                phase_C(c, l)
                s.barrier()
            if "D" in phases:
                phase_D(c, l, src, dst)
                s.barrier()
        s.finish()
    c.stats = (s.n_ins, s.n_wait)
    return nc, c


def phase_A(c, l, src):
    nc, s = c.nc, c.s
    with ExitStack() as es:
        def sb(name, shape, dt):
            return es.enter_context(nc.sbuf_tensor(_u(name), shape, dt))
        Wb = sb("Wb", [128, 8, DIN], BF16); r_Wb = Res()
        nw = sb("nw", [128, 8], F32); r_nw = Res()
        stage = [sb(f"wstage{i}", [128, 8, 512], F32) for i in range(2)]
        r_stage = [Res() for _ in range(2)]
        s.dma("sp", nw[:], c.norm_w[l].rearrange("(t p) -> p t", p=128), writes=[r_nw], allow_slow_non_contiguous=True)
        wv = c.w_in[l].rearrange("(t p) e -> p t e", p=128)
        chunks = [(i * 512, 512) for i in range(8)] + [(4096, 16)]
        for ci, (c0, cw) in enumerate(chunks):
            st, rs = stage[ci % 2], r_stage[ci % 2]
            s.dma("sp" if ci % 2 == 0 else "act", st[:, :, 0:cw], wv[:, :, c0:c0 + cw], writes=[rs])
            eng = "dve" if ci % 2 == 0 else "pool"
            s.op(eng, lambda e, st=st, c0=c0, cw=cw: e.tensor_tensor(
                out=Wb[:, :, c0:c0 + cw], in0=st[:, :, 0:cw], in1=nw[:].unsqueeze(2).to_broadcast([128, 8, cw]), op=ALU.mult),
                [rs, r_nw], [r_Wb])
        gqk = sb("gqk", [128, 2], F32); r_gqk = Res()
        s.dma("sp", gqk[0:64, 0:1], c.gq[l].rearrange("(p o) -> p o", o=1), writes=[r_gqk])
        s.dma("sp", gqk[64:128, 0:1], c.gq[l].rearrange("(p o) -> p o", o=1), writes=[r_gqk])
        s.dma("sp", gqk[0:64, 1:2], c.gk[l].rearrange("(p o) -> p o", o=1), writes=[r_gqk])
        s.dma("sp", gqk[64:128, 1:2], c.gk[l].rearrange("(p o) -> p o", o=1), writes=[r_gqk])
        gprod = sb("gprod", [128, 1], F32); r_gprod = Res()
        s.op("dve", lambda e: e.tensor_tensor(out=gprod[:], in0=gqk[:, 0:1], in1=gqk[:, 1:2], op=ALU.mult), [r_gqk], [r_gprod])
        bones = sb("bones", [128, 128], BF16); r_bones = Res()
        mk = sb("mk_a", [128, 128], F32); r_mk = Res()
        s.dma("sp", mk[:], c.c_masks[:, 0, :], writes=[r_mk])
        s.op("dve", lambda e: e.tensor_copy(bones[:], mk[:]), [r_mk], [r_bones])
        dtb = sb("dtb", [128, 8], F32); r_dtb = Res()
        nega = sb("nega", [128, 8], F32); r_nega = Res()
        s.dma("sp", dtb[:], c.dt_bias[l].partition_broadcast(128), writes=[r_dtb])
        s.dma("sp", nega[:], c.a_log[l].partition_broadcast(128), writes=[r_nega])
        s.op("act", lambda e: e.activation(out=nega[:], in_=nega[:], func=AF.Exp), [r_nega], [r_nega])
        s.op("dve", lambda e: e.tensor_scalar(out=nega[:], in0=nega[:], scalar1=-1.0, scalar2=None, op0=ALU.mult), [r_nega], [r_nega])

        xt = [sb(f"xt{i}", [128, D], F32) for i in range(4)]; r_xt = [Res() for _ in range(4)]
        junk = sb("junkA", [128, D], BF16); r_junk = Res()
        ss = [sb(f"ssA{k}", [128, NT], F32) for k in range(2)]; r_ss = [Res() for _ in range(2)]
        lnv = sb("lnvA", [128, 2 * NT, 2], F32); r_lnv = [Res() for _ in range(2 * NT)]
        hb = [sb(f"hb{i}", [128, D], BF16) for i in range(4)]; r_hb = [Res() for _ in range(4)]
        hT4 = [sb(f"hT4_{i}", [128, 8, 512], BF16) for i in range(2)]; r_hT4 = [Res() for _ in range(2)]
        vst = [sb(f"vst{i}", [128, 8, 65], BF16) for i in range(2)]; r_vst = [Res() for _ in range(2)]
        gz = [sb(f"gz{i}", [128, D], BF16) for i in range(2)]; r_gz = [Res() for _ in range(2)]
        gstash = [sb(f"gstash{k}", [128, NT, 16], F32) for k in range(2)]; r_gst = [Res() for _ in range(2)]
        gproc = sb("gproc", [128, NT, 24], F32); r_gproc = Res()
        fm = [sb(f"fm{i}", [128, 512], BF16) for i in range(4)]; r_fm = [Res() for _ in range(4)]
        sq = [sb(f"sq{i}", [128, 512], BF16) for i in range(2)]; r_sq = [Res() for _ in range(2)]
        rn = [sb(f"rn{i}", [128, 512], F32) for i in range(2)]; r_rn = [Res() for _ in range(2)]
        for i in range(2):
            s.op("pool", lambda e, i=i: e.memset(vst[i][:], 1.0), [], [r_vst[i]])
        cnt = {"fm": 0, "sq": 0, "ev": 0}

        def prepA(sq_i, st):
            sk = sq_i % 2
            h4, r_h4 = hT4[st % 2], r_hT4[st % 2]
            if st == 0:
                s.op("dve", lambda e: e.memset(ss[sk][:], 0.0), [r_ss[sk]], [r_ss[sk]])
            for ti in range(4):
                i = st * 4 + ti
                b = i % 4
                lv = lnv[:, sk * NT + i, :]
                r_lv = r_lnv[sk * NT + i]
                s.dma("sp", xt[b][:], src[sq_i, i * 128:(i + 1) * 128, :], reads=[c.r_xmid[sq_i]], writes=[r_xt[b]])
                yield
                s.op("act", lambda e: e.activation(out=junk[:], in_=xt[b][:], func=AF.Square, accum_out=ss[sk][:, i:i + 1]),
                     [r_xt[b], r_ss[sk]], [r_junk, r_ss[sk]])
                yield
                s.op("act", lambda e: e.activation(out=lv[:, 0:1], in_=ss[sk][:, i:i + 1], func=AF.Ln, bias=EPS, scale=1.0 / D),
                     [r_ss[sk], r_lv], [r_lv])
                yield
                s.op("act", lambda e: e.activation(out=lv[:, 1:2], in_=lv[:, 0:1], func=AF.Exp, scale=-0.5), [r_lv], [r_lv])
                yield
                s.op("dve", lambda e: e.tensor_scalar(out=hb[b][:], in0=xt[b][:], scalar1=lv[:, 1:2], scalar2=None, op0=ALU.mult),
                     [r_xt[b], r_lv], [r_hb[b]])
                yield
            for ti in range(4):
                i = st * 4 + ti
                b = i % 4
                pt, r_pt = c.nextbank()
                ptb = pt[:].bitcast(BF16)
                with s.group("pe"):
                    for dt in range(8):
                        s.op("pe", lambda e, dt=dt: e.transpose(ptb[:, dt * 128:(dt + 1) * 128], hb[b][:, dt * 128:(dt + 1) * 128], c.ident[:]),
                             [r_hb[b], c.r_ident], [r_pt])
                yield
                s.op("dve", lambda e: e.tensor_copy(h4[:, :, ti * 128:(ti + 1) * 128], ptb.rearrange("p (a b) -> p a b", a=8)), [r_pt], [r_h4])
                yield

        def mainA(sq_i, st):
            sk = sq_i % 2
            h4, r_h4 = hT4[st % 2], r_hT4[st % 2]
            for ti in range(4):
                i = st * 4 + ti
                vb = i % 2
                for (c0, cw, kind) in ((1024, 512, "v"), (1536, 512, "az"), (3584, 512, "dz"), (4096, 16, "g")):
                    pp, r_pp = c.nextbank()
                    with s.group("pe"):
                        for dt in range(8):
                            s.op("pe", lambda e, dt=dt: e.matmul(
                                pp[:, 0:cw], lhsT=h4[:, dt, ti * 128:(ti + 1) * 128], rhs=Wb[:, dt, c0:c0 + cw],
                                start=(dt == 0), stop=(dt == 7)), [r_h4, r_Wb], [r_pp])
                    yield
                    if kind == "v":
                        s.op("act", lambda e: e.copy(vst[vb][:, :, 0:64], pp[:].rearrange("p (h d) -> p h d", h=8)), [r_pp], [r_vst[vb]])
                        s.dma("pool", c.v1[sq_i, i * 128:(i + 1) * 128, :], vst[vb][:].rearrange("p h d -> p (h d)"),
                              reads=[r_vst[vb]], writes=[c.r_v1[sq_i]])
                    elif kind == "az":
                        s.op("act", lambda e: e.activation(out=gz[vb][:, 0:512], in_=pp[:], func=AF.Silu), [r_pp], [r_gz[vb]])
                    elif kind == "dz":
                        s.op("act", lambda e: e.activation(out=gz[vb][:, 512:1024], in_=pp[:], func=AF.Silu), [r_pp], [r_gz[vb]])
                        s.dma("pool", c.mix[sq_i, i * 128:(i + 1) * 128, :], gz[vb][:], reads=[r_gz[vb]], writes=[c.r_mix[sq_i]])
                    else:
                        s.op("dve", lambda e: e.tensor_copy(gstash[sk][:, i, :], pp[:, 0:16]), [r_pp], [r_gst[sk]])
                    yield
            for et in range(20):
                if et < 8:
                    c0 = et * 128
                else:
                    c0 = 2048 + (et - 8) * 128
                pp, r_pp = c.nextbank()
                with s.group("pe"):
                    for dt in range(8):
                        s.op("pe", lambda e, dt=dt: e.matmul(
                            pp[:], lhsT=Wb[:, dt, c0:c0 + 128], rhs=h4[:, dt, :], start=(dt == 0), stop=(dt == 7)),
                            [r_h4, r_Wb], [r_pp])
                yield
                fb = cnt["fm"] % 4
                cnt["fm"] += 1
                if et >= 8:
                    eng = "act" if cnt["ev"] % 2 == 0 else "dve"
                    cnt["ev"] += 1
                    if eng == "act":
                        s.op("act", lambda e: e.copy(fm[fb][:], pp[:]), [r_pp], [r_fm[fb]])
                    else:
                        s.op("dve", lambda e: e.tensor_copy(fm[fb][:], pp[:]), [r_pp], [r_fm[fb]])
                    ch0 = (et - 8) * 128
                    s.dma("pool", c.cin[sq_i, ch0:ch0 + 128, 2 + st * 512:2 + (st + 1) * 512], fm[fb][:],
                          reads=[r_fm[fb]], writes=[c.r_cin[sq_i]])
                    yield
                else:
                    qb = cnt["sq"] % 2
                    cnt["sq"] += 1
                    s.op("act", lambda e: e.activation(out=sq[qb][:], in_=pp[:], func=AF.Square), [r_pp], [r_sq[qb]])
                    yield
                    p2, r_p2 = c.nextbank()
                    s.op("pe", lambda e: e.matmul(p2[:], lhsT=bones[:], rhs=sq[qb][:], start=True, stop=True), [r_bones, r_sq[qb]], [r_p2])
                    yield
                    s.op("act", lambda e: e.activation(out=rn[qb][:], in_=p2[:], func=AF.Ln, bias=EPS, scale=1.0 / 64), [r_p2], [r_rn[qb]])
                    yield
                    s.op("act", lambda e: e.activation(out=rn[qb][:], in_=rn[qb][:], func=AF.Exp, scale=-0.5), [r_rn[qb]], [r_rn[qb]])
                    yield
                    if et < 4:
                        s.op("dve", lambda e: e.scalar_tensor_tensor(out=fm[fb][:], in0=pp[:], scalar=0.125, in1=rn[qb][:], op0=ALU.mult, op1=ALU.mult),
                             [r_pp, r_rn[qb]], [r_fm[fb]])
                        s.dma("pool", c.qT[sq_i, et * 128:(et + 1) * 128, st * 512:(st + 1) * 512], fm[fb][:], reads=[r_fm[fb]], writes=[c.r_qT[sq_i]])
                    else:
                        s.op("dve", lambda e: e.scalar_tensor_tensor(out=fm[fb][:], in0=pp[:], scalar=gprod[:, 0:1], in1=rn[qb][:], op0=ALU.mult, op1=ALU.mult),
                             [r_pp, r_rn[qb], r_gprod], [r_fm[fb]])
                        s.dma("pool", c.kT[sq_i, (et - 4) * 128:(et - 3) * 128, st * 512:(st + 1) * 512], fm[fb][:], reads=[r_fm[fb]], writes=[c.r_kT[sq_i]])
                    yield
            if st == 7:
                gs = gstash[sk]
                gv = gs[:].rearrange("p t (d k h) -> p t d k h", d=2, k=2)
                bview = gv[:, :, :, 0, :]
                aview = gv[:, :, :, 1, :]
                beta_o = gproc[:, :, 0:8].rearrange("p t (d h) -> p t d h", d=2)
                g_o = gproc[:, :, 8:16].rearrange("p t (d h) -> p t d h", d=2)
                lnb_o = gproc[:, :, 16:24].rearrange("p t (d h) -> p t d h", d=2)
                s.op("act", lambda e: e.activation(out=beta_o, in_=bview, func=AF.Sigmoid), [r_gst[sk], r_gproc], [r_gproc])
                yield
                s.op("act", lambda e: e.activation(out=lnb_o, in_=beta_o, func=AF.Ln), [r_gproc], [r_gproc])
                yield
                s.op("dve", lambda e: e.tensor_tensor(out=g_o, in0=aview, in1=dtb[:].rearrange("p (d h) -> p d h", d=2).unsqueeze(1).to_broadcast([128, NT, 2, 4]),
                                                      op=ALU.add), [r_gst[sk], r_dtb, r_gproc], [r_gproc])
                yield
                s.op("dve", lambda e: e.tensor_scalar(out=gproc[:, :, 8:16], in0=gproc[:, :, 8:16], scalar1=80.0, scalar2=None, op0=ALU.min), [r_gproc], [r_gproc])
                yield
                s.op("act", lambda e: e.activation(out=gproc[:, :, 8:16], in_=gproc[:, :, 8:16], func=AF.Exp), [r_gproc], [r_gproc])
                yield
                s.op("act", lambda e: e.activation(out=gproc[:, :, 8:16], in_=gproc[:, :, 8:16], func=AF.Ln, bias=1.0), [r_gproc], [r_gproc])
                yield
                s.op("dve", lambda e: e.tensor_tensor(out=gproc[:, :, 8:16], in0=gproc[:, :, 8:16], in1=nega[:].unsqueeze(1).to_broadcast([128, NT, 8]),
                                                      op=ALU.mult), [r_gproc, r_nega], [r_gproc])
                yield
                gdst = c.gates[sq_i].rearrange("(t p) c -> p t c", p=128)
                for qd in range(4):
                    s.dma("sp", gdst[:, qd * 8:(qd + 1) * 8, :], gproc[:, qd * 8:(qd + 1) * 8, :], reads=[r_gproc], writes=[c.r_gates[sq_i]])

        seqs = [(sq_i, st) for sq_i in range(c.n_seq) for st in range(8)]
        _interleave([prepA(*seqs[0])])
        for k, (sq_i, st) in enumerate(seqs):
            gens = [mainA(sq_i, st)]
            if k + 1 < len(seqs):
                gens.append(prepA(*seqs[k + 1]))
            _interleave(gens)


def phase_D(c, l, src, dst):
    nc, s = c.nc, c.s
    with ExitStack() as es:
        def sb(name, shape, dt):
            return es.enter_context(nc.sbuf_tensor(_u(name), shape, dt))
        Wo = sb("Wo", [128, 8, D], BF16); r_Wo = Res()
        stage = [sb(f"wostage{i}", [128, 8, 512], F32) for i in range(2)]
        r_stage = [Res() for _ in range(2)]
        wv = c.w_out[l].rearrange("(t p) e -> p t e", p=128)
        for ci in range(2):
            s.dma("sp", stage[ci][:], wv[:, :, ci * 512:(ci + 1) * 512], writes=[r_stage[ci]])
            s.op("dve" if ci == 0 else "pool", lambda e, ci=ci: e.tensor_copy(Wo[:, :, ci * 512:(ci + 1) * 512], stage[ci][:]), [r_stage[ci]], [r_Wo])
        xt = [sb(f"xtD{i}", [128, D], F32) for i in range(2)]; r_xt = [Res() for _ in range(2)]
        mt = [sb(f"mtD{i}", [128, D], BF16) for i in range(2)]; r_mt = [Res() for _ in range(2)]
        mT = [sb(f"mTD{i}", [128, 8, 128], BF16) for i in range(2)]; r_mT = [Res() for _ in range(2)]
        yo = [sb(f"yoD{i}", [128, D], F32) for i in range(2)]; r_yo = [Res() for _ in range(2)]
        n = 0
        for sq_i in range(c.n_seq):
            for i in range(NT):
                b = n % 2
                n += 1
                rows = slice(i * 128, (i + 1) * 128)
                s.dma("sp", xt[b][:], src[sq_i, rows, :], reads=[c.r_xmid[sq_i]], writes=[r_xt[b]])
                s.dma("act", mt[b][:], c.mix[sq_i, rows, :], reads=[c.r_mix[sq_i]], writes=[r_mt[b]])
                pt, r_pt = c.nextbank()
                ptb = pt[:].bitcast(BF16)
                with s.group("pe"):
                    for et in range(8):
                        s.op("pe", lambda e, et=et, b=b: e.transpose(ptb[:, et * 128:(et + 1) * 128], mt[b][:, et * 128:(et + 1) * 128], c.ident[:]),
                             [r_mt[b], c.r_ident], [r_pt])
                s.op("act", lambda e, b=b: e.copy(mT[b][:], ptb.rearrange("p (a b) -> p a b", a=8)), [r_pt], [r_mT[b]])
                for ch in range(2):
                    pp, r_pp = c.nextbank()
                    with s.group("pe"):
                        for et in range(8):
                            s.op("pe", lambda e, et=et, b=b, ch=ch, pp=pp: e.matmul(
                                pp[:], lhsT=mT[b][:, et, :], rhs=Wo[:, et, ch * 512:(ch + 1) * 512], start=(et == 0), stop=(et == 7)),
                                [r_mT[b], r_Wo], [r_pp])
                    s.op("dve", lambda e, b=b, ch=ch, pp=pp: e.tensor_tensor(
                        out=yo[b][:, ch * 512:(ch + 1) * 512], in0=pp[:], in1=xt[b][:, ch * 512:(ch + 1) * 512], op=ALU.add),
                        [r_pp, r_xt[b]], [r_yo[b]])
                wres = [c.r_xmid[sq_i]] if dst is c.xmid else []
                s.dma("pool", dst[sq_i, rows, :], yo[b][:], reads=[r_yo[b]], writes=wres)


def _rs(r):
    return min(max(r - 4, 0), 56)


def phase_B(c, l):
    nc, s = c.nc, c.s
    Rj = {}
    for j in range(32):
        rows = [r for r in range(64) if _rs(r) <= 2 * j + 1 and _rs(r) + 7 >= 2 * j]
        Rj[j] = (rows[0], rows[-1])
    c.ring_banks = [0, 1, 2, 3, 4, 5]
    with ExitStack() as es:
        def sb(name, shape, dt):
            return es.enter_context(nc.sbuf_tensor(_u(name), shape, dt))
        Mtab = sb("Mtab", [128, 8, 1024], BF16); r_Mtab = Res()
        with ExitStack() as es2:
            nav = es2.enter_context(nc.sbuf_tensor(_u("nav"), [128, 1024], F32)); r_nav = Res()
            stg = [es2.enter_context(nc.sbuf_tensor(_u(f"rstg{i}"), [128, 1024], F32)) for i in range(2)]
            r_stg = [Res() for _ in range(2)]
            s.dma("sp", nav[:], c.c_navalid, writes=[r_nav])
            for h in range(8):
                b = h % 2
                s.dma("sp", stg[b][:], c.rpbg[l, h], writes=[r_stg[b]])
                s.op("act", lambda e, b=b: e.activation(out=stg[b][:], in_=stg[b][:], func=AF.Exp), [r_stg[b]], [r_stg[b]])
                s.op("dve", lambda e, b=b, h=h: e.tensor_tensor(out=Mtab[:, h, :], in0=stg[b][:], in1=nav[:], op=ALU.mult),
                     [r_stg[b], r_nav], [r_Mtab])
            s.barrier()
        V1 = sb("V1all", [128, NT, 520], BF16); r_V1 = Res()
        mixA = sb("mixA", [128, NT, 512], BF16); r_mixA = Res()
        QT2 = [sb(f"QT2_{i}", [128, L], BF16) for i in range(2)]; r_QT2 = [Res() for _ in range(2)]
        KT2 = [sb(f"KT2_{i}", [128, L], BF16) for i in range(2)]; r_KT2 = [Res() for _ in range(2)]
        NSL = 6
        Es = [[sb(f"E{hh}_{k}", [128, 768], BF16) for k in range(NSL)] for hh in range(2)]
        r_Es = [[Res() for k in range(NSL)] for hh in range(2)]
        rdl = [sb(f"rdB{k}", [128, 2], F32) for k in range(2)]; r_rdl = [Res() for _ in range(2)]
        attl = [sb(f"attB{k}", [128, 2, 64], F32) for k in range(2)]; r_attl = [Res() for _ in range(2)]
        nq = 0
        for sq_i in range(c.n_seq):
            v1v = c.v1[sq_i].rearrange("(j p) c -> p j c", p=128)
            mxv = c.mix[sq_i].rearrange("(j p) c -> p j c", p=128)
            for qd in range(4):
                s.dma("sp", V1[:, qd * 8:(qd + 1) * 8, :], v1v[:, qd * 8:(qd + 1) * 8, :], reads=[c.r_v1[sq_i]], writes=[r_V1])
                s.dma("act", mixA[:, qd * 8:(qd + 1) * 8, :], mxv[:, qd * 8:(qd + 1) * 8, 0:512], reads=[c.r_mix[sq_i]], writes=[r_mixA])
            qbs = {}
            for hp in range(4):
                qbs[hp] = nq % 2
                nq += 1

            def qk_gen(hp, j):
                qb = qbs[hp]
                if j == 0:
                    s.dma("sp", QT2[qb][:], c.qT[sq_i, hp * 128:(hp + 1) * 128, :], reads=[c.r_qT[sq_i]], writes=[r_QT2[qb]])
                    s.dma("sp", KT2[qb][:], c.kT[sq_i, hp * 128:(hp + 1) * 128, :], reads=[c.r_kT[sq_i]], writes=[r_KT2[qb]])
                lo, hi = Rj[j]
                ncols = (hi - lo + 1) * 64
                t0 = lo - 2 * j + 7
                sl = (hp * 32 + j) % NSL
                for hh in range(2):
                    head = 2 * hp + hh
                    ph = slice(hh * 64, hh * 64 + 64)
                    c0 = 0
                    while c0 < ncols:
                        cw = min(512, ncols - c0)
                        pp, r_pp = c.nextbank()
                        s.op("pe", lambda e: e.matmul(
                            pp[:, 0:cw], lhsT=KT2[qb][ph, 128 * j:128 * j + 128], rhs=QT2[qb][ph, 64 * lo + c0:64 * lo + c0 + cw],
                            start=True, stop=True), [r_KT2[qb], r_QT2[qb]], [r_pp])
                        yield
                        s.op("act", lambda e: e.activation(out=Es[hh][sl][:, c0:c0 + cw], in_=pp[:, 0:cw], func=AF.Exp), [r_pp], [r_Es[hh][sl]])
                        yield
                        eng = "dve" if hh == 0 else "pool"
                        s.op(eng, lambda e: e.tensor_tensor(
                            out=Es[hh][sl][:, c0:c0 + cw], in0=Es[hh][sl][:, c0:c0 + cw],
                            in1=Mtab[:, head, t0 * 64 + c0:t0 * 64 + c0 + cw], op=ALU.mult), [r_Es[hh][sl], r_Mtab], [r_Es[hh][sl]])
                        yield
                        c0 += cw
                    for r in range(lo, hi + 1):
                        for a in range(2):
                            kr = 2 * j + a
                            if not (_rs(r) <= kr <= _rs(r) + 7):
                                s.op("pool", lambda e: e.memset(Es[hh][sl][a * 64:(a + 1) * 64, (r - lo) * 64:(r - lo + 1) * 64], 0.0),
                                     [r_Es[hh][sl]], [r_Es[hh][sl]])
                                yield

            def pv_gen(hp, j):
                for r in range(64):
                    rs = _rs(r)
                    if (rs + 7) // 2 != j:
                        continue
                    i = r // 2
                    ob, r_ob = c.pb[6 + (i % 2)], c.pr[6 + (i % 2)]
                    po = slice((r % 2) * 64, (r % 2) * 64 + 64)
                    parts = list(range(rs // 2, (rs + 7) // 2 + 1))
                    for hh in range(2):
                        head = 2 * hp + hh
                        with s.group("pe"):
                            for pi, jj in enumerate(parts):
                                blk = (r - Rj[jj][0]) * 64
                                s.op("pe", lambda e: e.matmul(
                                    ob[po, hh * 66:hh * 66 + 65], lhsT=Es[hh][(hp * 32 + jj) % NSL][:, blk:blk + 64],
                                    rhs=V1[:, jj, head * 65:head * 65 + 65], start=(pi == 0), stop=(pi == len(parts) - 1)),
                                    [r_Es[hh][(hp * 32 + jj) % NSL], r_V1], [r_ob])
                        yield
                    if r % 2 == 1:
                        rd, r_rd, att, r_att = rdl[i % 2], r_rdl[i % 2], attl[i % 2], r_attl[i % 2]
                        ov = ob[:, 0:132].rearrange("p (h d) -> p h d", h=2)
                        s.op("dve", lambda e: e.reciprocal(out=rd[:], in_=ov[:, :, 64]), [r_ob, r_rd], [r_rd])
                        yield
                        s.op("dve", lambda e: e.tensor_tensor(out=att[:], in0=ov[:, :, 0:64], in1=rd[:].unsqueeze(2).to_broadcast([128, 2, 64]), op=ALU.mult),
                             [r_ob, r_rd, r_att], [r_att])
                        yield
                        s.op("pool", lambda e: e.tensor_tensor(
                            out=mixA[:, i, hp * 128:(hp + 1) * 128], in0=att[:].rearrange("p h d -> p (h d)"),
                            in1=mixA[:, i, hp * 128:(hp + 1) * 128], op=ALU.mult), [r_att, r_mixA], [r_mixA])
                        yield

            units = [(hp, j) for hp in range(4) for j in range(32)]
            _interleave([qk_gen(*units[0])])
            for k, u in enumerate(units):
                gens = [pv_gen(*u)]
                if k + 1 < len(units):
                    gens.append(qk_gen(*units[k + 1]))
                _interleave(gens)
            for qd in range(4):
                s.dma("pool", mxv[:, qd * 8:(qd + 1) * 8, 0:512], mixA[:, qd * 8:(qd + 1) * 8, :], reads=[r_mixA], writes=[c.r_mix[sq_i]])
    c.ring_banks = list(range(8))


def phase_C(c, l):
    nc, s = c.nc, c.s
    NCH = 64
    with ExitStack() as es:
        def sb(name, shape, dt):
            return es.enter_context(nc.sbuf_tensor(_u(name), shape, dt))
        mskf = sb("mskf", [128, 7, 128], F32); r_mskf = Res()
        s.dma("sp", mskf[:], c.c_masks[:, 0:7, :], writes=[r_mskf])
        BD_ONES, BD_GT, BD_LE, BD_LT, BD_GE, CH_A, CH_B = [mskf[:, k, :] for k in range(7)]
        mske = sb("mske", [128, 4, 128], F32); r_mske = Res()
        s.dma("sp", mske[:], c.c_masks[:, 7:11, :], writes=[r_mske])
        Ma = mske[:, 0, :].rearrange("p (d c) -> p d c", d=2)
        Mb = mske[:, 1, :].rearrange("p (d c) -> p d c", d=2)
        Mc = mske[:, 2, :].rearrange("p (d c) -> p d c", d=2)
        EQ = mske[:, 3, 0:64]
        ones_b = sb("ones_b", [128, 128], BF16); r_onesb = Res()
        s.op("pool", lambda e: e.memset(ones_b[:], 1.0), [], [r_onesb])
        dnw = sb("dnw_bc", [128, 128], F32); r_dnw = Res()
        s.dma("sp", dnw[:], c.dn_w[l].partition_broadcast(128), writes=[r_dnw])
        cw = sb("cwC", [128, 12, 5], F32); r_cw = Res()
        s.dma("sp", cw[:], c.conv_w[l].rearrange("(t p) j -> p t j", p=128), writes=[r_cw])

        kqT = sb("kqT", [128, 4, NCH, 128], BF16); r_kqT = Res()
        GT = sb("GTall", [128, NT, 24], F32); r_GT = Res()
        sc = sb("scC", [128, NT, 32], F32); r_scl = [Res() for _ in range(NT)]

        for sq_i in range(c.n_seq):
            gsrc = c.gates[sq_i].rearrange("(t p) c -> p t c", p=128)
            for qd in range(4):
                s.dma("sp", GT[:, qd * 8:(qd + 1) * 8, :], gsrc[:, qd * 8:(qd + 1) * 8, :], reads=[c.r_gates[sq_i]], writes=[r_GT])
            with ExitStack() as e0:
              if "0" in CPARTS:
                  def sb0(name, shape, dt):
                      return e0.enter_context(nc.sbuf_tensor(_u(name), shape, dt))
                  dg = sb0("dgC", [128, 60, 128], BF16); r_dg = Res()
                  for t in range(12):
                      for jx in range(5):
                          eng = "dve" if (t * 5 + jx) % 2 == 0 else "pool"
                          s.op(eng, lambda e, t=t, jx=jx: e.tensor_scalar(out=dg[:, t * 5 + jx, :], in0=c.ident_f[:], scalar1=cw[:, t, jx:jx + 1],
                                                                          scalar2=None, op0=ALU.mult), [c.r_identf, r_cw], [r_dg])
                  NS0 = 6
                  T0 = []
                  for k in range(NS0):
                      T = {}
                      for nm, shp, dt in (("xin", [128, 516], BF16), ("yb", [128, 512], F32), ("sqb", [128, 512], BF16), ("rnb", [128, 512], F32),
                                          ("knb", [128, 512], BF16), ("tst", [128, 4, 128], BF16)):
                          T[nm] = sb0(f"{nm}0_{k}", shp, dt)
                          T["r_" + nm] = Res()
                      T0.append(T)

                  def c0_unit(tb, h, qkv, T):
                      xin, yb, sqb, rnb, knb = T["xin"], T["yb"], T["sqb"], T["rnb"], T["knb"]
                      r_xin, r_yb, r_sqb, r_rnb, r_knb = T["r_xin"], T["r_yb"], T["r_sqb"], T["r_rnb"], T["r_knb"]
                      t = qkv * 4 + h
                      ch0 = t * 128
                      lo_c, hi_c = tb * 512, tb * 512 + 516
                      d0, d1 = 0, 516
                      if tb == 0:
                          lo_c, d0 = 2, 2
                          s.op("pool", lambda e: e.memset(xin[:, 0:2], 0.0), [r_xin], [r_xin])
                      if tb == 7:
                          hi_c, d1 = L + 2, 514
                          s.op("pool", lambda e: e.memset(xin[:, 514:516], 0.0), [r_xin], [r_xin])
                      s.dma("sp", xin[:, d0:d1], c.cin[sq_i, ch0:ch0 + 128, lo_c:hi_c], reads=[c.r_cin[sq_i]], writes=[r_xin])
                      yield
                      pp, r_pp = c.nextbank()
                      with s.group("pe"):
                          for jx in range(5):
                              s.op("pe", lambda e, jx=jx: e.matmul(pp[:], lhsT=dg[:, t * 5 + jx, :], rhs=xin[:, jx:jx + 512], start=(jx == 0), stop=(jx == 4)),
                                   [r_dg, r_xin], [r_pp])
                      yield
                      if qkv == 2:
                          s.op("act", lambda e: e.activation(out=knb[:], in_=pp[:], func=AF.Silu), [r_pp, r_knb], [r_knb])
                          yield
                          yield
                          yield
                          yield
                          yield
                          yield
                          pt, r_pt = c.nextbank()
                          ptb = pt[:].bitcast(BF16)
                          with s.group("pe"):
                              for k in range(4):
                                  s.op("pe", lambda e, k=k: e.transpose(ptb[:, k * 128:(k + 1) * 128], knb[:, k * 128:(k + 1) * 128], c.ident[:]),
                                       [r_knb, c.r_ident], [r_pt])
                          yield
                          s.op("dve", lambda e: e.tensor_copy(T["tst"][:], ptb[:, 0:512].rearrange("p (a b) -> p a b", a=4)), [r_pt, T["r_tst"]], [T["r_tst"]])
                          yield
                          s.dma("pool", c.vtd[sq_i].rearrange("(t p) (h d) -> p t h d", p=128, h=4)[:, tb * 4:(tb + 1) * 4, h, :], T["tst"][:],
                                reads=[T["r_tst"]], writes=[c.r_vtd[sq_i]])
                      else:
                          s.op("act", lambda e: e.activation(out=yb[:], in_=pp[:], func=AF.Silu), [r_pp, r_yb], [r_yb])
                          yield
                          s.op("pool", lambda e: e.tensor_tensor(out=sqb[:], in0=yb[:], in1=yb[:], op=ALU.mult), [r_yb, r_sqb], [r_sqb])
                          yield
                          p2, r_p2 = c.nextbank()
                          s.op("pe", lambda e: e.matmul(p2[:], lhsT=ones_b[:], rhs=sqb[:], start=True, stop=True), [r_onesb, r_sqb], [r_p2])
                          yield
                          s.op("act", lambda e: e.activation(out=rnb[:], in_=p2[:], func=AF.Ln, bias=EPS), [r_p2, r_rnb], [r_rnb])
                          yield
                          s.op("act", lambda e: e.activation(out=rnb[:], in_=rnb[:], func=AF.Exp, scale=-0.5), [r_rnb], [r_rnb])
                          yield
                          if qkv == 0:
                              s.op("dve", lambda e: e.scalar_tensor_tensor(
                                  out=kqT[:, h, tb * 8:(tb + 1) * 8, 64:128], in0=yb[:].rearrange("p (n c) -> p n c", n=8), scalar=128 ** -0.5,
                                  in1=rnb[:].rearrange("p (n c) -> p n c", n=8), op0=ALU.mult, op1=ALU.mult), [r_yb, r_rnb], [r_kqT])
                          else:
                              s.op("dve", lambda e: e.tensor_tensor(out=knb[:], in0=yb[:], in1=rnb[:], op=ALU.mult), [r_yb, r_rnb, r_knb], [r_knb])
                              yield
                              s.op("pool", lambda e: e.tensor_copy(kqT[:, h, tb * 8:(tb + 1) * 8, 0:64], knb[:].rearrange("p (n c) -> p n c", n=8)),
                                   [r_knb], [r_kqT])
                              pt, r_pt = c.nextbank()
                              ptb = pt[:].bitcast(BF16)
                              with s.group("pe"):
                                  for k in range(4):
                                      s.op("pe", lambda e, k=k: e.transpose(ptb[:, k * 128:(k + 1) * 128], knb[:, k * 128:(k + 1) * 128], c.ident[:]),
                                           [r_knb, c.r_ident], [r_pt])
                              yield
                              s.op("act", lambda e: e.copy(T["tst"][:], ptb[:, 0:512].rearrange("p (a b) -> p a b", a=4)), [r_pt, T["r_tst"]], [T["r_tst"]])
                              yield
                              s.dma("pool", c.ktd[sq_i].rearrange("(t p) (h d) -> p t h d", p=128, h=4)[:, tb * 4:(tb + 1) * 4, h, :], T["tst"][:],
                                    reads=[T["r_tst"]], writes=[c.r_ktd[sq_i]])

                  for tb in range(8):
                      for hp in range(2):
                          units = [(tb, 2 * hp + hh, qkv) for hh in range(2) for qkv in range(3)]
                          _interleave([c0_unit(*u, T0[k]) for k, u in enumerate(units)])
                  s.barrier()
            with ExitStack() as e1:
              if "1" in CPARTS or "2" in CPARTS:
                  def sb1(name, shape, dt):
                      return e1.enter_context(nc.sbuf_tensor(_u(name), shape, dt))
                  NSLOT = 2
                  slots = []
                  for k in range(NSLOT):
                      T = {}
                      for nm, shp, dt in (("RG0", [128, 2, 4, 64], F32), ("RG1", [128, 2, 4, 64], F32), ("RB", [128, 2, 4, 64], F32),
                                          ("zc", [128, 32], F32), ("beg", [128, 8], F32),
                                          ("Nbd", [128, 8, 128], BF16), ("NTbd", [128, 8, 128], BF16), ("Pb", [128, 8, 128], BF16),
                                          ("Qb", [128, 8, 128], BF16), ("Zb0", [128, 8, 128], BF16), ("Zb1", [128, 8, 128], BF16),
                                          ("bv", [128, 8, 128], BF16), ("kgb", [128, 8, 128], BF16), ("rec", [128, 4096], BF16),
                                          ("kt", [128, 4, 128], BF16), ("vt", [128, 4, 128], BF16)):
                          T[nm] = sb1(f"{nm}_{k}", shp, dt)
                          T["r_" + nm] = Res()
                      s.op("pool", lambda e, T=T: e.memset(T["Nbd"][:], 0.0), [], [T["r_Nbd"]])
                      s.op("pool", lambda e, T=T: e.memset(T["NTbd"][:], 0.0), [], [T["r_NTbd"]])
                      slots.append(T)
                  bc = lambda ap: ap.unsqueeze(3).to_broadcast([128, 2, 4, 64])
                  bm = lambda m: m.unsqueeze(2).to_broadcast([128, 2, 4, 64])
                  LTm = (BD_GT, BD_LT)
                  LT2m = (BD_LE, BD_GE)
                  done_c1 = set()
                  done_o = [set(), set()]
                  r_zst_t = [Res() for _ in range(NT)]
                  r_ost_t = {(d, n): Res() for d in range(2) for n in range(NCH)}

                  def c1_tile(i, T):
                      RG0, RG1, RB, zc, beg = T["RG0"], T["RG1"], T["RB"], T["zc"], T["beg"]
                      r_RG0, r_RG1, r_RB, r_zc, r_beg = T["r_RG0"], T["r_RG1"], T["r_RB"], T["r_zc"], T["r_beg"]
                      Nbd, NTbd, r_Nbd, r_NTbd = T["Nbd"], T["NTbd"], T["r_Nbd"], T["r_NTbd"]
                      rec, r_rec = T["rec"], T["r_rec"]
                      r_sc = r_scl[i]
                      recv = rec[:].rearrange("p (d x) -> p d x", d=2)
                      kt, vt, r_kt, r_vt = T["kt"], T["vt"], T["r_kt"], T["r_vt"]
                      s.dma("sp", kt[:].rearrange("p h d -> p (h d)"), c.ktd[sq_i, i * 128:(i + 1) * 128, :], reads=[c.r_ktd[sq_i]], writes=[r_kt])
                      s.dma("sp", vt[:].rearrange("p h d -> p (h d)"), c.vtd[sq_i, i * 128:(i + 1) * 128, :], reads=[c.r_vtd[sq_i]], writes=[r_vt])
                      g8 = GT[:, i, 8:16]
                      gv = g8.rearrange("p (d h) -> p d h", d=2)
                      beta8 = GT[:, i, 0:8]
                      beta = beta8.rearrange("p (d h) -> p d h", d=2)
                      s.op("dve", lambda e: e.tensor_tensor(out=RG0[:], in0=bc(gv), in1=bm(Ma), op=ALU.mult), [r_GT, r_mske, r_RG0], [r_RG0])
                      s.op("pool", lambda e: e.tensor_tensor(out=RG1[:], in0=bc(gv), in1=bm(Mb), op=ALU.mult), [r_GT, r_mske, r_RG1], [r_RG1])
                      s.op("pool", lambda e: e.tensor_tensor(out=RB[:], in0=bc(beta), in1=EQ.unsqueeze(1).unsqueeze(1).to_broadcast([128, 2, 4, 64]), op=ALU.mult),
                           [r_GT, r_mske, r_RB], [r_RB])
                      yield
                      D1, r_D1, D2, r_D2, tI, r_tI = RG0, r_RG0, RG1, r_RG1, RB, r_RB
                      pX, r_pX = yield from c.getbank()
                      for d in range(2):
                          s.op("pe", lambda e, d=d: e.matmul(pX[:, d * 256:(d + 1) * 256], lhsT=LTm[d], rhs=RG0[:, d].rearrange("p h c -> p (h c)"),
                                                            start=True, stop=True), [r_mskf, r_RG0], [r_pX])
                      yield
                      s.op("act", lambda e: e.activation(out=D1[:].rearrange("p d h c -> p (d h c)"), in_=pX[:], func=AF.Exp), [r_pX, r_D1], [r_D1])
                      c.relbank(pX)
                      yield
                      pY, r_pY = yield from c.getbank()
                      for d in range(2):
                          s.op("pe", lambda e, d=d: e.matmul(pY[:, d * 256:(d + 1) * 256], lhsT=LT2m[d], rhs=RG1[:, d].rearrange("p h c -> p (h c)"),
                                                            start=True, stop=True), [r_mskf, r_RG1], [r_pY])
                      yield
                      s.op("act", lambda e: e.activation(out=D2[:].rearrange("p d h c -> p (d h c)"), in_=pY[:], func=AF.Exp), [r_pY, r_D2], [r_D2])
                      c.relbank(pY)
                      yield
                      pZ, r_pZ = yield from c.getbank()
                      with s.group("pe"):
                          s.op("pe", lambda e: e.matmul(pZ[:, 0:4], lhsT=BD_LE, rhs=g8[:, 0:4], start=True, stop=True), [r_mskf, r_GT], [r_pZ])
                          s.op("pe", lambda e: e.matmul(pZ[:, 4:8], lhsT=BD_GE, rhs=g8[:, 4:8], start=True, stop=True), [r_mskf, r_GT], [r_pZ])
                          s.op("pe", lambda e: e.matmul(pZ[:, 8:16], lhsT=BD_ONES, rhs=g8, start=True, stop=True), [r_mskf, r_GT], [r_pZ])
                          s.op("pe", lambda e: e.matmul(pZ[:, 16:24], lhsT=CH_A, rhs=g8, start=True, stop=True), [r_mskf, r_GT], [r_pZ])
                          s.op("pe", lambda e: e.matmul(pZ[:, 24:32], lhsT=CH_B, rhs=g8, start=True, stop=True), [r_mskf, r_GT], [r_pZ])
                      yield
                      s.op("act", lambda e: e.copy(zc[:], pZ[:, 0:32]), [r_pZ, r_zc], [r_zc])
                      c.relbank(pZ)
                      yield
                      s.op("dve", lambda e: e.tensor_tensor(out=zc[:, 8:16], in0=zc[:, 8:16], in1=zc[:, 0:8], op=ALU.subtract), [r_zc], [r_zc])
                      yield
                      s.op("act", lambda e: e.activation(out=sc[:, i, :], in_=zc[:], func=AF.Exp), [r_zc, r_sc], [r_sc])
                      yield
                      s.op("dve", lambda e: e.tensor_tensor(out=beg[:], in0=beta8, in1=sc[:, i, 0:8], op=ALU.mult), [r_GT, r_sc, r_beg], [r_beg])
                      s.op("pool", lambda e: e.tensor_tensor(out=recv[:, :, 1280:1536].rearrange("p d (h c) -> p d h c", h=4),
                                                             in0=bc(sc[:, i, 0:8].rearrange("p (d h) -> p d h", d=2)),
                                                             in1=EQ.unsqueeze(1).unsqueeze(1).to_broadcast([128, 2, 4, 64]), op=ALU.mult),
                           [r_sc, r_mske, r_rec], [r_rec])
                      yield
                      pB, r_pB = yield from c.getbank()
                      s.op("pe", lambda e: e.matmul(pB[:], lhsT=BD_ONES, rhs=RB[:].rearrange("p d h c -> p (d h c)"), start=True, stop=True), [r_mskf, r_RB], [r_pB])
                      yield
                      pW, r_pW = yield from c.getbank()
                      with s.group("pe"):
                          for h in range(4):
                              for a in range(2):
                                  n = 2 * i + a
                                  s.op("pe", lambda e, h=h, a=a, n=n: e.matmul(pW[a * 64:(a + 1) * 64, h * 128:(h + 1) * 128], lhsT=kqT[:, h, n, 0:64],
                                                                             rhs=kqT[:, h, n, :], start=True, stop=True), [r_kqT], [r_pW])
                      yield
                      Wv = pW[:].rearrange("p (h x) -> p h x", h=4)
                      KK = Wv[:, :, 0:64].unsqueeze(1).to_broadcast([128, 2, 4, 64])
                      KQ = Wv[:, :, 64:128].unsqueeze(1).to_broadcast([128, 2, 4, 64])
                      s.op("dve", lambda e: e.tensor_tensor(out=tI[:], in0=D1[:], in1=KQ, op=ALU.mult), [r_D1, r_pW, r_tI], [r_tI])
                      yield
                      s.op("dve", lambda e: e.tensor_tensor(out=D1[:], in0=D1[:], in1=KK, op=ALU.mult), [r_D1, r_pW], [r_D1])
                      yield
                      s.op("dve", lambda e: e.tensor_tensor(out=D2[:], in0=D2[:], in1=KK, op=ALU.mult), [r_D2, r_pW], [r_D2])
                      c.relbank(pW)
                      yield
                      s.op("dve", lambda e: e.tensor_tensor(out=D1[:], in0=D1[:], in1=pB[:].rearrange("p (d h c) -> p d h c", d=2, h=4), op=ALU.mult), [r_D1, r_pB], [r_D1])
                      c.relbank(pB)
                      yield
                      s.op("pool", lambda e: e.tensor_tensor(out=D2[:], in0=D2[:], in1=bc(beta), op=ALU.mult), [r_D2, r_GT], [r_D2])
                      yield
                      s.op("pool", lambda e: e.tensor_tensor(out=recv[:, :, 1024:1280].rearrange("p d (h c) -> p d h c", h=4), in0=tI[:], in1=bm(Ma), op=ALU.mult),
                           [r_tI, r_mske, r_rec], [r_rec])
                      for a in range(2):
                          pa = slice(a * 64, (a + 1) * 64)
                          ca = slice(a * 64, (a + 1) * 64)
                          s.op("dve" if a == 0 else "pool", lambda e, pa=pa, ca=ca: e.tensor_tensor(
                              out=Nbd[pa, :, ca].rearrange("p (d h) c -> p d h c", d=2), in0=D1[pa], in1=bm(Mc)[pa], op=ALU.mult),
                              [r_D1, r_mske, r_Nbd], [r_Nbd])
                          s.op("pool" if a == 0 else "dve", lambda e, pa=pa, ca=ca: e.tensor_tensor(
                              out=NTbd[pa, :, ca].rearrange("p (d h) c -> p d h c", d=2), in0=D2[pa], in1=bm(Mb)[pa], op=ALU.mult),
                              [r_D2, r_mske, r_NTbd], [r_NTbd])
                          yield
                      Zb = [T["Zb0"], T["Zb1"]]
                      r_Zb = [T["r_Zb0"], T["r_Zb1"]]
                      s.op("pool", lambda e: e.tensor_tensor(out=Zb[0][:], in0=c.ident[:].unsqueeze(1).to_broadcast([128, 8, 128]), in1=Nbd[:], op=ALU.subtract),
                           [c.r_ident, r_Nbd, r_Zb[0]], [r_Zb[0]])
                      bv, r_bv, kgb, r_kgb = T["bv"], T["r_bv"], T["kgb"], T["r_kgb"]
                      bc128 = lambda ap: ap.rearrange("p (d h) -> p d h", d=2).unsqueeze(3).to_broadcast([128, 2, 4, 128])
                      s.op("pool", lambda e: e.tensor_tensor(out=bv[:].rearrange("p (d h) x -> p d h x", d=2), in0=vt[:].unsqueeze(1).to_broadcast([128, 2, 4, 128]),
                                                             in1=bc128(beta8), op=ALU.mult), [r_vt, r_GT, r_bv], [r_bv])
                      yield
                      s.op("pool", lambda e: e.tensor_tensor(out=kgb[:].rearrange("p (d h) x -> p d h x", d=2), in0=kt[:].unsqueeze(1).to_broadcast([128, 2, 4, 128]),
                                                             in1=bc128(beg[:]), op=ALU.mult), [r_kt, r_beg, r_kgb], [r_kgb])
                      yield
                      s.op("pool", lambda e: e.tensor_tensor(out=recv[:, :, 1536:2048].rearrange("p d (h x) -> p d h x", h=4), in0=kt[:].unsqueeze(1).to_broadcast([128, 2, 4, 128]),
                                                             in1=bc128(sc[:, i, 8:16]), op=ALU.mult), [r_kt, r_sc, r_rec], [r_rec])
                      yield
                      Pseq = [(NTbd, r_NTbd), (T["Pb"], T["r_Pb"])]
                      Qseq = [(Nbd, r_Nbd), (T["Qb"], T["r_Qb"])]
                      zi = 0
                      for lev in range(1, 6):
                          Pc, r_Pc = Pseq[(lev - 1) % 2]
                          Qc, r_Qc = Qseq[(lev - 1) % 2]
                          Pn, r_Pn = Pseq[lev % 2]
                          Qn, r_Qn = Qseq[lev % 2]
                          def mmP(half, pp, r_pp):
                              with s.group("pe"):
                                  for k in range(4):
                                      p = half * 4 + k
                                      s.op("pe", lambda e, p=p, k=k: e.matmul(pp[:, k * 128:(k + 1) * 128], lhsT=Qc[:, p, :], rhs=Pc[:, p, :],
                                                                            start=True, stop=True), [r_Pc, r_Qc], [r_pp])

                          def mmQ(half, pp, r_pp):
                              with s.group("pe"):
                                  for k in range(4):
                                      p = half * 4 + k
                                      s.op("pe", lambda e, p=p, k=k: e.matmul(pp[:, k * 128:(k + 1) * 128], lhsT=Pc[:, p, :], rhs=Qc[:, p, :],
                                                                            start=True, stop=True), [r_Pc, r_Qc], [r_pp])

                          def evP(half, pp, r_pp):
                              s.op("act", lambda e: e.copy(Pn[:, half * 4:(half + 1) * 4, :], pp[:].rearrange("p (k x) -> p k x", k=4)), [r_pp, r_Pn], [r_Pn])
                              c.relbank(pp)

                          def evQ(half, pp, r_pp):
                              s.op("act", lambda e: e.copy(Qn[:, half * 4:(half + 1) * 4, :], pp[:].rearrange("p (k x) -> p k x", k=4)), [r_pp, r_Qn], [r_Qn])
                              c.relbank(pp)
                          pa0, r_pa0 = yield from c.getbank()
                          mmP(0, pa0, r_pa0)
                          yield
                          pa1, r_pa1 = yield from c.getbank()
                          mmP(1, pa1, r_pa1)
                          yield
                          evP(0, pa0, r_pa0)
                          yield
                          if lev < 5:
                              qa0, r_qa0 = yield from c.getbank()
                              mmQ(0, qa0, r_qa0)
                              yield
                          evP(1, pa1, r_pa1)
                          yield
                          if lev < 5:
                              qa1, r_qa1 = yield from c.getbank()
                              mmQ(1, qa1, r_qa1)
                              yield
                              evQ(0, qa0, r_qa0)
                              yield
                              evQ(1, qa1, r_qa1)
                              yield
                          Zc, r_Zc = Zb[zi], r_Zb[zi]
                          Zn, r_Zn = Zb[1 - zi], r_Zb[1 - zi]
                          for half in range(2):
                              pp, r_pp = yield from c.getbank()
                              with s.group("pe"):
                                  for k in range(4):
                                      p = half * 4 + k
                                      s.op("pe", lambda e, p=p, k=k, pp=pp, Zc=Zc, Pn=Pn: e.matmul(pp[:, k * 128:(k + 1) * 128], lhsT=Pn[:, p, :], rhs=Zc[:, p, :],
                                                                                            start=True, stop=True), [r_Pn, r_Zc], [r_pp])
                              yield
                              ppv = pp[:].rearrange("p (k x) -> p k x", k=4)
                              s.op("dve", lambda e, ppv=ppv, half=half, Zn=Zn, Zc=Zc: e.tensor_tensor(out=Zn[:, half * 4:(half + 1) * 4, :], in0=ppv,
                                                                                                     in1=Zc[:, half * 4:(half + 1) * 4, :], op=ALU.add),
                                   [r_pp, r_Zc, r_Zn], [r_Zn])
                              c.relbank(pp)
                              yield
                          zi = 1 - zi
                      Z2, r_Z2 = Zb[zi], r_Zb[zi]
                      for half in range(2):
                          pp, r_pp = yield from c.getbank()
                          with s.group("pe"):
                              for k in range(4):
                                  p = half * 4 + k
                                  s.op("pe", lambda e, p=p, k=k, pp=pp: e.matmul(pp[:, k * 128:(k + 1) * 128], lhsT=Z2[:, p, :], rhs=bv[:, p, :], start=True, stop=True),
                                       [r_Z2, r_bv], [r_pp])
                          yield
                          s.op("act", lambda e, pp=pp, half=half: e.copy(recv[:, half, 0:512], pp[:]), [r_pp, r_rec], [r_rec])
                          c.relbank(pp)
                          yield
                      for half in range(2):
                          pp, r_pp = yield from c.getbank()
                          with s.group("pe"):
                              for k in range(4):
                                  p = half * 4 + k
                                  s.op("pe", lambda e, p=p, k=k, pp=pp: e.matmul(pp[:, k * 128:(k + 1) * 128], lhsT=kgb[:, p, :], rhs=Z2[:, p, :], start=True, stop=True),
                                       [r_kgb, r_Z2], [r_pp])
                          yield
                          s.op("dve", lambda e, pp=pp, half=half: e.tensor_copy(recv[:, half, 512:1024], pp[:]), [r_pp, r_rec], [r_rec])
                          c.relbank(pp)
                          yield
                      s.dma("pool", c.zst[sq_i, i].rearrange("d p x -> p d x"), recv, reads=[r_rec], writes=[r_zst_t[i]])

                  sb2 = sb1
                  zb = [[sb2(f"zbuf{d}_{k}", [128, 2048], BF16) for k in range(2)] for d in range(2)]
                  r_zb = [[Res() for k in range(2)] for d in range(2)]
                  S = [sb2(f"S{d}", [128, 4, 128], F32) for d in range(2)]; r_S = [Res() for _ in range(2)]
                  Sb = [sb2(f"Sb{d}", [128, 4, 128], BF16) for d in range(2)]; r_Sb = [Res() for _ in range(2)]
                  vnew = [sb2(f"vnew_{d}", [128, 4, 128], BF16) for d in range(2)]; r_vnew = [Res() for _ in range(2)]
                  p1sb = [sb2(f"p1sb_{d}", [128, 4, 128], BF16) for d in range(2)]; r_p1sb = [Res() for _ in range(2)]
                  ot = [sb2(f"ot_{d}", [128, 4, 128], F32) for d in range(2)]; r_ot = [Res() for _ in range(2)]
                  for d in range(2):
                      s.op("pool", lambda e, d=d: e.memset(S[d][:], 0.0), [r_S[d]], [r_S[d]])
                      s.op("pool", lambda e, d=d: e.memset(Sb[d][:], 0.0), [r_Sb[d]], [r_Sb[d]])
                  bc4 = lambda ap: ap.unsqueeze(2).to_broadcast([64, 4, 128])

                  def c2_step(t, d):
                      n = t if d == 0 else NCH - 1 - t
                      i, a = n // 2, n % 2
                      pa = slice(a * 64, (a + 1) * 64)
                      rows = slice(n * 64, (n + 1) * 64)
                      zt, r_zt = zb[d][i % 2], r_zb[d][i % 2]
                      r_sc = r_scl[i]
                      first_of_tile = (a == 0) if d == 0 else (a == 1)
                      if first_of_tile:
                          if t == 0:
                              while i not in done_c1:
                                  yield
                              s.dma("sp", zt[:], c.zst[sq_i, i, d], reads=[r_zst_t[i]], writes=[r_zt])
                          inx = i + 1 if d == 0 else i - 1
                          if 0 <= inx < NT:
                              while inx not in done_c1:
                                  yield
                              s.dma("sp", zb[d][inx % 2][:], c.zst[sq_i, inx, d], reads=[r_zst_t[inx]], writes=[r_zb[d][inx % 2]])
                      U = zt[:, 0:512].rearrange("p (q x) -> p q x", q=4)
                      Wt = zt[:, 512:1024].rearrange("p (q x) -> p q x", q=4)
                      IT = zt[:, 1024:1280].rearrange("p (q x) -> p q x", q=4)
                      DG = zt[:, 1280:1536].rearrange("p (q x) -> p q x", q=4)
                      KD = zt[:, 1536:2048].rearrange("p (q x) -> p q x", q=4)
                      eGt = sc[:, i, 16 + a * 8 + d * 4:16 + a * 8 + (d + 1) * 4]
                      yield
                      pWS, r_pWS = yield from c.getbank()
                      pP1, r_pP1 = yield from c.getbank()
                      with s.group("pe"):
                          for h in range(4):
                              s.op("pe", lambda e, h=h: e.matmul(pWS[pa, h * 128:(h + 1) * 128], lhsT=Wt[:, h, a * 64:(a + 1) * 64], rhs=Sb[d][:, h, :],
                                                                 start=True, stop=True), [r_zt, r_Sb[d]], [r_pWS])
                      yield
                      with s.group("pe"):
                          for h in range(4):
                              s.op("pe", lambda e, h=h: e.matmul(pP1[pa, h * 128:(h + 1) * 128], lhsT=kqT[:, h, n, 64:128], rhs=Sb[d][:, h, :],
                                                                 start=True, stop=True), [r_kqT, r_Sb[d]], [r_pP1])
                      yield
                      v3 = lambda bank: bank[pa, :].rearrange("p (h e) -> p h e", h=4)
                      s.op("dve", lambda e: e.tensor_tensor(out=vnew[d][pa], in0=U[pa, :, :], in1=v3(pWS), op=ALU.subtract),
                           [r_zt, r_pWS, r_vnew[d]], [r_vnew[d]])
                      c.relbank(pWS)
                      yield
                      s.op("act", lambda e: e.copy(p1sb[d][pa], v3(pP1)), [r_pP1, r_p1sb[d]], [r_p1sb[d]])
                      c.relbank(pP1)
                      yield
                      pSU, r_pSU = yield from c.getbank()
                      pP2, r_pP2 = yield from c.getbank()
                      with s.group("pe"):
                          for h in range(4):
                              s.op("pe", lambda e, h=h: e.matmul(pSU[:, h * 128:(h + 1) * 128], lhsT=KD[pa, h, :], rhs=vnew[d][pa, h, :],
                                                                 start=True, stop=True), [r_zt, r_vnew[d]], [r_pSU])
                      yield
                      with s.group("pe"):
                          for h in range(4):
                              s.op("pe", lambda e, h=h: e.matmul(pP2[pa, h * 128:(h + 1) * 128], lhsT=DG[pa, h, :], rhs=p1sb[d][pa, h, :],
                                                                 start=True, stop=False), [r_zt, r_p1sb[d]], [r_pP2])
                              s.op("pe", lambda e, h=h: e.matmul(pP2[pa, h * 128:(h + 1) * 128], lhsT=IT[pa, h, :], rhs=vnew[d][pa, h, :],
                                                                 start=False, stop=True), [r_zt, r_vnew[d]], [r_pP2])
                      yield
                      for h in range(4):
                          s.op("dve", lambda e, h=h: e.scalar_tensor_tensor(out=Sb[d][:, h, :], in0=S[d][:, h, :], scalar=eGt[:, h:h + 1], in1=pSU[:, h * 128:(h + 1) * 128],
                                                                            op0=ALU.mult, op1=ALU.add), [r_pSU, r_S[d], r_sc, r_Sb[d]], [r_Sb[d]])
                      yield
                      s.op("pool", lambda e: e.tensor_tensor(out=S[d][:], in0=S[d][:], in1=eGt.unsqueeze(2).to_broadcast([128, 4, 128]), op=ALU.mult),
                           [r_S[d], r_sc], [r_S[d]])
                      yield
                      s.op("dve", lambda e: e.tensor_tensor(out=S[d][:], in0=pSU[:].rearrange("p (h e) -> p h e", h=4), in1=S[d][:], op=ALU.add),
                           [r_pSU, r_S[d]], [r_S[d]])
                      c.relbank(pSU)
                      yield
                      s.op("act", lambda e: e.copy(ot[d][pa], v3(pP2)), [r_pP2, r_ot[d]], [r_ot[d]])
                      c.relbank(pP2)
                      yield
                      s.dma("sp", c.ost[sq_i, d, rows, :], ot[d][pa].rearrange("p h e -> p (h e)"), reads=[r_ot[d]], writes=[r_ost_t[(d, n)]])
                      done_o[d].add(n)

                  def c2_dir(d):
                      for t in range(NCH):
                          yield from c2_step(t, d)

                  order = []
                  for k in range(NT // 2):
                      order += [k, NT - 1 - k]

                  def c1_worker(w):
                      for i in order[w::NSLOT]:
                          yield from c1_tile(i, slots[w])
                          done_c1.add(i)
                  NS3 = 2
                  T3 = []
                  for k in range(NS3):
                      T = {}
                      for nm, shp, dt in (("of", [128, 4, 128], F32), ("ob", [128, 4, 128], F32), ("gt", [128, 4, 128], BF16), ("gd", [128, 4, 128], F32),
                                          ("jk", [128, 128], BF16), ("rs", [128, 8], F32), ("fo", [128, 512], BF16)):
                          T[nm] = sb1(f"{nm}3_{k}", shp, dt)
                          T["r_" + nm] = Res()
                      T3.append(T)
                  ss3 = sb1("ss3", [128, NT, 4], F32); r_ss3 = Res()
                  s.op("pool", lambda e: e.memset(ss3[:], 0.0), [], [r_ss3])

                  def c3_tile(i, T):
                      rows = slice(i * 128, (i + 1) * 128)
                      of, ob, gt, gd, jk, rs, fo = T["of"], T["ob"], T["gt"], T["gd"], T["jk"], T["rs"], T["fo"]
                      s.dma("sp", of[:].rearrange("p h e -> p (h e)"), c.ost[sq_i, 0, rows, :], reads=[r_ost_t[(0, 2 * i)], r_ost_t[(0, 2 * i + 1)]], writes=[T["r_of"]])
                      s.dma("act", ob[:].rearrange("p h e -> p (h e)"), c.ost[sq_i, 1, rows, :], reads=[r_ost_t[(1, 2 * i)], r_ost_t[(1, 2 * i + 1)]], writes=[T["r_ob"]])
                      s.dma("sp", gt[:].rearrange("p h e -> p (h e)"), c.mix[sq_i, rows, 512:1024], reads=[c.r_mix[sq_i]], writes=[T["r_gt"]])
                      yield
                      s.op("dve", lambda e: e.tensor_tensor(out=of[:], in0=of[:], in1=ob[:], op=ALU.add), [T["r_of"], T["r_ob"]], [T["r_of"]])
                      yield
                      s.op("pool", lambda e: e.tensor_tensor(out=gd[:], in0=gt[:], in1=dnw[:].unsqueeze(1).to_broadcast([128, 4, 128]), op=ALU.mult),
                           [T["r_gt"], r_dnw, T["r_gd"]], [T["r_gd"]])
                      yield
                      for h in range(4):
                          s.op("act", lambda e, h=h: e.activation(out=jk[:], in_=of[:, h, :], func=AF.Square, accum_out=ss3[:, i, h:h + 1]),
                               [T["r_of"], T["r_jk"], r_ss3], [T["r_jk"], r_ss3])
                          yield
                      s.op("act", lambda e: e.activation(out=rs[:, 0:4], in_=ss3[:, i, :], func=AF.Ln, bias=EPS, scale=1.0 / 128), [r_ss3, T["r_rs"]], [T["r_rs"]])
                      yield
                      s.op("act", lambda e: e.activation(out=rs[:, 4:8], in_=rs[:, 0:4], func=AF.Exp, scale=-0.5), [T["r_rs"]], [T["r_rs"]])
                      yield
                      s.op("dve", lambda e: e.tensor_tensor(out=of[:], in0=of[:], in1=rs[:, 4:8].unsqueeze(2).to_broadcast([128, 4, 128]), op=ALU.mult),
                           [T["r_of"], T["r_rs"]], [T["r_of"]])
                      yield
                      s.op("pool", lambda e: e.tensor_tensor(out=fo[:].rearrange("p (h e) -> p h e", h=4), in0=of[:], in1=gd[:], op=ALU.mult),
                           [T["r_of"], T["r_gd"], T["r_fo"]], [T["r_fo"]])
                      yield
                      s.dma("sp", c.mix[sq_i, rows, 512:1024], fo[:], reads=[T["r_fo"]], writes=[c.r_mix[sq_i]])


                  c3_order = sorted(range(NT), key=lambda i: (max(2 * i + 1, 63 - 2 * i), i))

                  def c3_worker(w):
                      for i in c3_order[w::NS3]:
                          while not all((2 * i + a) in done_o[d] for d in range(2) for a in range(2)):
                              yield
                          yield from c3_tile(i, T3[w])
                  c.free_banks = list(range(8))
                  _interleave([c1_worker(w) for w in range(NSLOT)] + [c2_dir(0), c2_dir(1)] + [c3_worker(w) for w in range(NS3)])
                  assert len(c.free_banks) == 8
                  s.barrier()


def _interleave(gens):
    gens = list(gens)
    while gens:
        for g in list(gens):
            try:
                next(g)
            except StopIteration:
                gens.remove(g)


def _host_constants():
    ident = np.eye(128, dtype=np.float32)
    masks = np.zeros((128, 16, 128), np.float32)
    p = np.arange(128)[:, None]
    q = np.arange(128)[None, :]
    same = (p // 64 == q // 64)
    masks[:, 0, :] = same
    masks[:, 1, :] = same & (p > q)
    masks[:, 2, :] = same & (p <= q)
    masks[:, 3, :] = same & (p < q)
    masks[:, 4, :] = same & (p >= q)
    masks[:, 5, :] = (p < 64) & (q >= 0)
    masks[:, 6, :] = (p >= 64) & (q >= 0)
    pl = p % 64
    ql = q % 64
    first = q < 64
    masks[:, 7, :] = np.where(first, pl <= ql, pl >= ql)
    masks[:, 8, :] = np.where(first, pl > ql, pl < ql)
    masks[:, 9, :] = np.where(first, pl < ql, pl > ql)
    masks[:, 10, :] = (pl == ql)
    return ident, masks


def _rpb_gather(rpb):
    a = np.arange(2)[:, None, None, None]
    kc = np.arange(64)[None, :, None, None]
    t = np.arange(16)[None, None, :, None]
    qc = np.arange(64)[None, None, None, :]
    i = np.clip(a + 14 - t, 0, 14) + 0 * kc + 0 * qc
    jj = np.clip(kc - qc + 15, 0, 30) + 0 * a + 0 * t
    g = rpb[:, :, i, jj]
    return np.ascontiguousarray(g.reshape(2, 8, 128, 1024))


def _na_valid():
    v = np.zeros((2, 64, 16, 64), np.float32)
    for a in range(2):
        for t in range(16):
            i = a + 14 - t
            if not (0 <= i <= 14):
                continue
            for qc in range(64):
                cs = int(np.clip(qc - 8, 0, 48))
                v[a, cs:cs + 16, t, qc] = 1.0
    return v.reshape(128, 1024)


def make_in_maps(inputs, n_cores=NCORES, n_seq=SEQ_PER_CORE):
    f = lambda a: np.ascontiguousarray(np.asarray(a, dtype=np.float32))
    ident, masks = _host_constants()
    shared = {
        "norm_w": f(inputs["norm_w"]), "w_in": f(inputs["w_in"]),
        "qk_gain_q": f(inputs["qk_gain_q"]), "qk_gain_k": f(inputs["qk_gain_k"]),
        "rpb_g": _rpb_gather(f(inputs["rpb"])),
        "conv_wT": np.ascontiguousarray(f(inputs["conv_w"]).transpose(0, 2, 1)),
        "a_log": f(inputs["a_log"]).reshape(2, 8), "dt_bias": f(inputs["dt_bias"]).reshape(2, 8),
        "dn_norm_w": f(inputs["dn_norm_w"]), "w_out": f(inputs["w_out"]),
        "c_ident": ident, "c_masks": masks, "c_navalid": _na_valid(),
    }
    x = f(inputs["x"])
    maps = []
    for i in range(n_cores):
        m = dict(shared)
        m["x"] = np.ascontiguousarray(x[i * n_seq:(i + 1) * n_seq])
        maps.append(m)
    return maps


def kernel(**inputs):
    nc, c = build_program()
    in_maps = make_in_maps(inputs)
    res = run_bass_kernel_spmd(nc, in_maps, core_ids=list(range(NCORES)))
    out = np.concatenate([np.asarray(r["out"]) for r in res.results], axis=0)
    return out.astype(np.float32)
```

```python
import numpy as np
from contextlib import ExitStack
import concourse.bass as bass
import concourse.mybir as mybir
from concourse.bass_utils import run_bass_kernel_spmd

F32 = mybir.dt.float32
BF16 = mybir.dt.bfloat16
ALU = mybir.AluOpType
AF = mybir.ActivationFunctionType
AX = mybir.AxisListType

D = 1024
L = 4096
DIN = 4112
NT = L // 128
EPS = 1e-6
NCORES = 8
SEQ_PER_CORE = 2


class Res:
    __slots__ = ("name", "w", "r")

    def __init__(self, name=""):
        self.name = name
        self.w = None
        self.r = []


class Sched:
    NRING = 8

    def __init__(self, nc):
        self.nc = nc
        self.E = {"pe": nc.tensor, "act": nc.scalar, "dve": nc.vector,
                  "pool": nc.gpsimd, "sp": nc.sync}
        self.sems = {}
        self.cnt = {}
        for e in self.E:
            self.sems[e] = nc.alloc_semaphore(name=f"s_{e}")
            self.cnt[e] = 0
        self.ring = {}
        self.ring_i = {}
        for q in ("sp", "pool", "act"):
            self.ring[q] = []
            for i in range(self.NRING):
                k = f"d_{q}{i}"
                self.sems[k] = nc.alloc_semaphore(name=k)
                self.cnt[k] = 0
                self.ring[q].append(k)
            self.ring_i[q] = 0
        self.seen = {e: {} for e in self.E}
        self.grp = None
        self.n_ins = 0
        self.n_wait = 0

    def _need(self, eng, reads, writes):
        need = {}

        def add(tok):
            if tok is None:
                return
            k, v = tok
            if k == eng and eng == "pe":
                return
            if need.get(k, 0) < v:
                need[k] = v
        for r in reads:
            add(r.w)
        for w in writes:
            add(w.w)
            for t in w.r:
                if t[0] == eng:
                    continue
                add(t)
        return need

    def _emit_waits(self, eng, need):
        seen = self.seen[eng]
        for k, v in need.items():
            if seen.get(k, 0) < v:
                if self.grp is not None and self.grp[0] == eng and k == eng and v >= self.grp[1][1]:
                    raise RuntimeError("dependency inside open group on same engine")
                self.E[eng].wait_ge(self.sems[k], v)
                seen[k] = v
                self.n_wait += 1

    def _mark(self, tok, reads, writes):
        for r in reads:
            r.r.append(tok)
        for w in writes:
            w.w = tok
            w.r = []

    def op(self, eng, fn, reads=(), writes=()):
        need = self._need(eng, reads, writes)
        self._emit_waits(eng, need)
        ins = fn(self.E[eng])
        self.n_ins += 1
        if self.grp is not None and self.grp[0] == eng:
            tok = self.grp[1]
            self.grp[2] = ins
        else:
            self.cnt[eng] += 1
            tok = (eng, self.cnt[eng])
            ins.then_inc(self.sems[eng], 1)
        self._mark(tok, reads, writes)
        return ins

    def group(self, eng):
        s = self

        class G:
            def __enter__(self_):
                assert s.grp is None
                s.cnt[eng] += 1
                s.grp = [eng, (eng, s.cnt[eng]), None]

            def __exit__(self_, *a):
                g = s.grp
                s.grp = None
                if a[0] is not None:
                    return False
                assert g[2] is not None
                g[2].then_inc(s.sems[eng], 1)
                return False
        return G()

    def dma(self, q, out, in_, reads=(), writes=(), **kw):
        ring = self.ring[q]
        i = self.ring_i[q]
        self.ring_i[q] = i + 1
        k = ring[i % self.NRING]
        need = self._need(q, reads, writes)
        if self.cnt[k] > 0 and need.get(k, 0) < self.cnt[k]:
            need[k] = self.cnt[k]
        self._emit_waits(q, need)
        ins = self.E[q].dma_start(out=out, in_=in_, **kw)
        self.cnt[k] += 16
        ins.then_inc(self.sems[k], 16)
        tok = (k, self.cnt[k])
        self._mark(tok, reads, writes)
        self.n_ins += 1
        return ins

    def barrier(self):
        for e in self.E:
            need = {k: v for k, v in self.cnt.items() if v > 0 and not (k == e and e == "pe")}
            self._emit_waits(e, need)

    def finish(self):
        need = {k: v for k, v in self.cnt.items() if v > 0}
        self._emit_waits("sp", need)


class Ctx:
    pass


_uid = [0]
CPARTS = "012"


def _u(name):
    _uid[0] += 1
    return f"{name}_{_uid[0]}"


def build_program(n_layers=2, n_seq=SEQ_PER_CORE, phases="ABCD", debug=False):
    nc = bass.Bass("TRN2", target_bir_lowering=False)
    s = Sched(nc)
    c = Ctx()
    c.nc, c.s = nc, s
    c.n_seq = n_seq

    def din(name, shape, dt=F32):
        return nc.dram_tensor(name, list(shape), dt, kind="ExternalInput").ap()

    def dscr(name, shape, dt=F32):
        return nc.dram_tensor(name, list(shape), dt, kind="ExternalOutput" if debug else "Internal").ap()

    c.x = din("x", [n_seq, L, D])
    c.norm_w = din("norm_w", [2, D])
    c.w_in = din("w_in", [2, D, DIN])
    c.gq = din("qk_gain_q", [2, 64])
    c.gk = din("qk_gain_k", [2, 64])
    c.rpbg = din("rpb_g", [2, 8, 128, 1024])
    c.conv_w = din("conv_wT", [2, 1536, 5])
    c.a_log = din("a_log", [2, 8])
    c.dt_bias = din("dt_bias", [2, 8])
    c.dn_w = din("dn_norm_w", [2, 128])
    c.w_out = din("w_out", [2, D, D])
    c.c_ident = din("c_ident", [128, 128])
    c.c_masks = din("c_masks", [128, 16, 128])
    c.c_navalid = din("c_navalid", [128, 1024])
    c.out = nc.dram_tensor("out", [n_seq, L, D], F32, kind="ExternalOutput").ap()

    c.xmid = dscr("xmid", [n_seq, L, D])
    c.cin = dscr("cin", [n_seq, 1536, L + 4], BF16)
    c.qT = dscr("qT", [n_seq, 512, L], BF16)
    c.kT = dscr("kT", [n_seq, 512, L], BF16)
    c.v1 = dscr("v1", [n_seq, L, 520], BF16)
    c.mix = dscr("mix", [n_seq, L, D], BF16)
    c.gates = dscr("gates", [n_seq, L, 24])
    c.zst = dscr("zst", [n_seq, NT, 2, 128, 2048], BF16)
    c.ktd = dscr("ktd", [n_seq, L, 512], BF16)
    c.vtd = dscr("vtd", [n_seq, L, 512], BF16)
    c.ost = dscr("ost", [n_seq, 2, L, 512])
    c.r_xmid = [Res() for _ in range(n_seq)]
    c.r_cin = [Res() for _ in range(n_seq)]
    c.r_qT = [Res() for _ in range(n_seq)]
    c.r_kT = [Res() for _ in range(n_seq)]
    c.r_v1 = [Res() for _ in range(n_seq)]
    c.r_mix = [Res() for _ in range(n_seq)]
    c.r_gates = [Res() for _ in range(n_seq)]
    c.r_zst = [Res() for _ in range(n_seq)]
    c.r_ktd = [Res() for _ in range(n_seq)]
    c.r_vtd = [Res() for _ in range(n_seq)]
    c.r_ost = [Res() for _ in range(n_seq)]
    c.dbg = {}
    if debug:
        c.dbg_out = {}

    with ExitStack() as es0:
        c.pb = [es0.enter_context(nc.psum_tensor(f"pb{i}", [128, 512], F32)) for i in range(8)]
        c.pr = [Res(f"pb{i}") for i in range(8)]
        c.pi = 0
        c.ring_banks = list(range(8))

        def nextbank():
            i = c.ring_banks[c.pi % len(c.ring_banks)]
            c.pi += 1
            return c.pb[i], c.pr[i]
        c.nextbank = nextbank
        c.free_banks = list(range(8))

        def getbank():
            while not c.free_banks:
                yield
            i = c.free_banks.pop(0)
            return c.pb[i], c.pr[i]

        def relbank(pp):
            i = [k for k in range(8) if c.pb[k] is pp][0]
            assert i not in c.free_banks
            c.free_banks.append(i)
        c.getbank, c.relbank = getbank, relbank

        c.ident_f = es0.enter_context(nc.sbuf_tensor("ident_f", [128, 128], F32)); c.r_identf = Res()
        c.ident = es0.enter_context(nc.sbuf_tensor("ident", [128, 128], BF16)); c.r_ident = Res()
        s.dma("sp", c.ident_f[:], c.c_ident, writes=[c.r_identf])
        s.op("dve", lambda e: e.tensor_copy(c.ident[:], c.ident_f[:]), [c.r_identf], [c.r_ident])

        for l in range(n_layers):
            src = c.x if l == 0 else c.xmid
            dst = c.out if l == n_layers - 1 else c.xmid
            if "A" in phases:
                phase_A(c, l, src)
                s.barrier()
            if "B" in phases:
                phase_B(c, l)
                s.barrier()
            if "C" in phases:
                phase_C(c, l)
                s.barrier()
            if "D" in phases:
                phase_D(c, l, src, dst)
                s.barrier()
        s.finish()
    c.stats = (s.n_ins, s.n_wait)
    return nc, c


def phase_A(c, l, src):
    nc, s = c.nc, c.s
    with ExitStack() as es:
        def sb(name, shape, dt):
            return es.enter_context(nc.sbuf_tensor(_u(name), shape, dt))
        Wb = sb("Wb", [128, 8, DIN], BF16); r_Wb = Res()
        nw = sb("nw", [128, 8], F32); r_nw = Res()
        stage = [sb(f"wstage{i}", [128, 8, 512], F32) for i in range(2)]
        r_stage = [Res() for _ in range(2)]
        s.dma("sp", nw[:], c.norm_w[l].rearrange("(t p) -> p t", p=128), writes=[r_nw], allow_slow_non_contiguous=True)
        wv = c.w_in[l].rearrange("(t p) e -> p t e", p=128)
        chunks = [(i * 512, 512) for i in range(8)] + [(4096, 16)]
        for ci, (c0, cw) in enumerate(chunks):
            st, rs = stage[ci % 2], r_stage[ci % 2]
            s.dma("sp" if ci % 2 == 0 else "act", st[:, :, 0:cw], wv[:, :, c0:c0 + cw], writes=[rs])
            eng = "dve" if ci % 2 == 0 else "pool"
            s.op(eng, lambda e, st=st, c0=c0, cw=cw: e.tensor_tensor(
                out=Wb[:, :, c0:c0 + cw], in0=st[:, :, 0:cw], in1=nw[:].unsqueeze(2).to_broadcast([128, 8, cw]), op=ALU.mult),
                [rs, r_nw], [r_Wb])
        gqk = sb("gqk", [128, 2], F32); r_gqk = Res()
        s.dma("sp", gqk[0:64, 0:1], c.gq[l].rearrange("(p o) -> p o", o=1), writes=[r_gqk])
        s.dma("sp", gqk[64:128, 0:1], c.gq[l].rearrange("(p o) -> p o", o=1), writes=[r_gqk])
        s.dma("sp", gqk[0:64, 1:2], c.gk[l].rearrange("(p o) -> p o", o=1), writes=[r_gqk])
        s.dma("sp", gqk[64:128, 1:2], c.gk[l].rearrange("(p o) -> p o", o=1), writes=[r_gqk])
        gprod = sb("gprod", [128, 1], F32); r_gprod = Res()
        s.op("dve", lambda e: e.tensor_tensor(out=gprod[:], in0=gqk[:, 0:1], in1=gqk[:, 1:2], op=ALU.mult), [r_gqk], [r_gprod])
        bones = sb("bones", [128, 128], BF16); r_bones = Res()
        mk = sb("mk_a", [128, 128], F32); r_mk = Res()
        s.dma("sp", mk[:], c.c_masks[:, 0, :], writes=[r_mk])
        s.op("dve", lambda e: e.tensor_copy(bones[:], mk[:]), [r_mk], [r_bones])
        dtb = sb("dtb", [128, 8], F32); r_dtb = Res()
        nega = sb("nega", [128, 8], F32); r_nega = Res()
        s.dma("sp", dtb[:], c.dt_bias[l].partition_broadcast(128), writes=[r_dtb])
        s.dma("sp", nega[:], c.a_log[l].partition_broadcast(128), writes=[r_nega])
        s.op("act", lambda e: e.activation(out=nega[:], in_=nega[:], func=AF.Exp), [r_nega], [r_nega])
        s.op("dve", lambda e: e.tensor_scalar(out=nega[:], in0=nega[:], scalar1=-1.0, scalar2=None, op0=ALU.mult), [r_nega], [r_nega])

        xt = [sb(f"xt{i}", [128, D], F32) for i in range(4)]; r_xt = [Res() for _ in range(4)]
        junk = sb("junkA", [128, D], BF16); r_junk = Res()
        ss = [sb(f"ssA{k}", [128, NT], F32) for k in range(2)]; r_ss = [Res() for _ in range(2)]
        lnv = sb("lnvA", [128, 2 * NT, 2], F32); r_lnv = [Res() for _ in range(2 * NT)]
        hb = [sb(f"hb{i}", [128, D], BF16) for i in range(4)]; r_hb = [Res() for _ in range(4)]
        hT4 = [sb(f"hT4_{i}", [128, 8, 512], BF16) for i in range(2)]; r_hT4 = [Res() for _ in range(2)]
        vst = [sb(f"vst{i}", [128, 8, 65], BF16) for i in range(2)]; r_vst = [Res() for _ in range(2)]
        gz = [sb(f"gz{i}", [128, D], BF16) for i in range(2)]; r_gz = [Res() for _ in range(2)]
        gstash = [sb(f"gstash{k}", [128, NT, 16], F32) for k in range(2)]; r_gst = [Res() for _ in range(2)]
        gproc = sb("gproc", [128, NT, 24], F32); r_gproc = Res()
        fm = [sb(f"fm{i}", [128, 512], BF16) for i in range(4)]; r_fm = [Res() for _ in range(4)]
        sq = [sb(f"sq{i}", [128, 512], BF16) for i in range(2)]; r_sq = [Res() for _ in range(2)]
        rn = [sb(f"rn{i}", [128, 512], F32) for i in range(2)]; r_rn = [Res() for _ in range(2)]
        for i in range(2):
            s.op("pool", lambda e, i=i: e.memset(vst[i][:], 1.0), [], [r_vst[i]])
        cnt = {"fm": 0, "sq": 0, "ev": 0}

        def prepA(sq_i, st):
            sk = sq_i % 2
            h4, r_h4 = hT4[st % 2], r_hT4[st % 2]
            if st == 0:
                s.op("dve", lambda e: e.memset(ss[sk][:], 0.0), [r_ss[sk]], [r_ss[sk]])
            for ti in range(4):
                i = st * 4 + ti
                b = i % 4
                lv = lnv[:, sk * NT + i, :]
                r_lv = r_lnv[sk * NT + i]
                s.dma("sp", xt[b][:], src[sq_i, i * 128:(i + 1) * 128, :], reads=[c.r_xmid[sq_i]], writes=[r_xt[b]])
                yield
                s.op("act", lambda e: e.activation(out=junk[:], in_=xt[b][:], func=AF.Square, accum_out=ss[sk][:, i:i + 1]),
                     [r_xt[b], r_ss[sk]], [r_junk, r_ss[sk]])
                yield
                s.op("act", lambda e: e.activation(out=lv[:, 0:1], in_=ss[sk][:, i:i + 1], func=AF.Ln, bias=EPS, scale=1.0 / D),
                     [r_ss[sk], r_lv], [r_lv])
                yield
                s.op("act", lambda e: e.activation(out=lv[:, 1:2], in_=lv[:, 0:1], func=AF.Exp, scale=-0.5), [r_lv], [r_lv])
                yield
                s.op("dve", lambda e: e.tensor_scalar(out=hb[b][:], in0=xt[b][:], scalar1=lv[:, 1:2], scalar2=None, op0=ALU.mult),
                     [r_xt[b], r_lv], [r_hb[b]])
                yield
            for ti in range(4):
                i = st * 4 + ti
                b = i % 4
                pt, r_pt = c.nextbank()
                ptb = pt[:].bitcast(BF16)
                with s.group("pe"):
                    for dt in range(8):
                        s.op("pe", lambda e, dt=dt: e.transpose(ptb[:, dt * 128:(dt + 1) * 128], hb[b][:, dt * 128:(dt + 1) * 128], c.ident[:]),
                             [r_hb[b], c.r_ident], [r_pt])
                yield
                s.op("dve", lambda e: e.tensor_copy(h4[:, :, ti * 128:(ti + 1) * 128], ptb.rearrange("p (a b) -> p a b", a=8)), [r_pt], [r_h4])
                yield

        def mainA(sq_i, st):
            sk = sq_i % 2
            h4, r_h4 = hT4[st % 2], r_hT4[st % 2]
            for ti in range(4):
                i = st * 4 + ti
                vb = i % 2
                for (c0, cw, kind) in ((1024, 512, "v"), (1536, 512, "az"), (3584, 512, "dz"), (4096, 16, "g")):
                    pp, r_pp = c.nextbank()
                    with s.group("pe"):
                        for dt in range(8):
                            s.op("pe", lambda e, dt=dt: e.matmul(
                                pp[:, 0:cw], lhsT=h4[:, dt, ti * 128:(ti + 1) * 128], rhs=Wb[:, dt, c0:c0 + cw],
                                start=(dt == 0), stop=(dt == 7)), [r_h4, r_Wb], [r_pp])
                    yield
                    if kind == "v":
                        s.op("act", lambda e: e.copy(vst[vb][:, :, 0:64], pp[:].rearrange("p (h d) -> p h d", h=8)), [r_pp], [r_vst[vb]])
                        s.dma("pool", c.v1[sq_i, i * 128:(i + 1) * 128, :], vst[vb][:].rearrange("p h d -> p (h d)"),
                              reads=[r_vst[vb]], writes=[c.r_v1[sq_i]])
                    elif kind == "az":
                        s.op("act", lambda e: e.activation(out=gz[vb][:, 0:512], in_=pp[:], func=AF.Silu), [r_pp], [r_gz[vb]])
                    elif kind == "dz":
                        s.op("act", lambda e: e.activation(out=gz[vb][:, 512:1024], in_=pp[:], func=AF.Silu), [r_pp], [r_gz[vb]])
                        s.dma("pool", c.mix[sq_i, i * 128:(i + 1) * 128, :], gz[vb][:], reads=[r_gz[vb]], writes=[c.r_mix[sq_i]])
                    else:
                        s.op("dve", lambda e: e.tensor_copy(gstash[sk][:, i, :], pp[:, 0:16]), [r_pp], [r_gst[sk]])
                    yield
            for et in range(20):
                if et < 8:
                    c0 = et * 128
                else:
                    c0 = 2048 + (et - 8) * 128
                pp, r_pp = c.nextbank()
                with s.group("pe"):
                    for dt in range(8):
                        s.op("pe", lambda e, dt=dt: e.matmul(
                            pp[:], lhsT=Wb[:, dt, c0:c0 + 128], rhs=h4[:, dt, :], start=(dt == 0), stop=(dt == 7)),
                            [r_h4, r_Wb], [r_pp])
                yield
                fb = cnt["fm"] % 4
                cnt["fm"] += 1
                if et >= 8:
                    eng = "act" if cnt["ev"] % 2 == 0 else "dve"
                    cnt["ev"] += 1
                    if eng == "act":
                        s.op("act", lambda e: e.copy(fm[fb][:], pp[:]), [r_pp], [r_fm[fb]])
                    else:
                        s.op("dve", lambda e: e.tensor_copy(fm[fb][:], pp[:]), [r_pp], [r_fm[fb]])
                    ch0 = (et - 8) * 128
                    s.dma("pool", c.cin[sq_i, ch0:ch0 + 128, 2 + st * 512:2 + (st + 1) * 512], fm[fb][:],
                          reads=[r_fm[fb]], writes=[c.r_cin[sq_i]])
                    yield
                else:
                    qb = cnt["sq"] % 2
                    cnt["sq"] += 1
                    s.op("act", lambda e: e.activation(out=sq[qb][:], in_=pp[:], func=AF.Square), [r_pp], [r_sq[qb]])
                    yield
                    p2, r_p2 = c.nextbank()
                    s.op("pe", lambda e: e.matmul(p2[:], lhsT=bones[:], rhs=sq[qb][:], start=True, stop=True), [r_bones, r_sq[qb]], [r_p2])
                    yield
                    s.op("act", lambda e: e.activation(out=rn[qb][:], in_=p2[:], func=AF.Ln, bias=EPS, scale=1.0 / 64), [r_p2], [r_rn[qb]])
                    yield
                    s.op("act", lambda e: e.activation(out=rn[qb][:], in_=rn[qb][:], func=AF.Exp, scale=-0.5), [r_rn[qb]], [r_rn[qb]])
                    yield
                    if et < 4:
                        s.op("dve", lambda e: e.scalar_tensor_tensor(out=fm[fb][:], in0=pp[:], scalar=0.125, in1=rn[qb][:], op0=ALU.mult, op1=ALU.mult),
                             [r_pp, r_rn[qb]], [r_fm[fb]])
                        s.dma("pool", c.qT[sq_i, et * 128:(et + 1) * 128, st * 512:(st + 1) * 512], fm[fb][:], reads=[r_fm[fb]], writes=[c.r_qT[sq_i]])
                    else:
                        s.op("dve", lambda e: e.scalar_tensor_tensor(out=fm[fb][:], in0=pp[:], scalar=gprod[:, 0:1], in1=rn[qb][:], op0=ALU.mult, op1=ALU.mult),
                             [r_pp, r_rn[qb], r_gprod], [r_fm[fb]])
                        s.dma("pool", c.kT[sq_i, (et - 4) * 128:(et - 3) * 128, st * 512:(st + 1) * 512], fm[fb][:], reads=[r_fm[fb]], writes=[c.r_kT[sq_i]])
                    yield
            if st == 7:
                gs = gstash[sk]
                gv = gs[:].rearrange("p t (d k h) -> p t d k h", d=2, k=2)
                bview = gv[:, :, :, 0, :]
                aview = gv[:, :, :, 1, :]
                beta_o = gproc[:, :, 0:8].rearrange("p t (d h) -> p t d h", d=2)
                g_o = gproc[:, :, 8:16].rearrange("p t (d h) -> p t d h", d=2)
                lnb_o = gproc[:, :, 16:24].rearrange("p t (d h) -> p t d h", d=2)
                s.op("act", lambda e: e.activation(out=beta_o, in_=bview, func=AF.Sigmoid), [r_gst[sk], r_gproc], [r_gproc])
                yield
                s.op("act", lambda e: e.activation(out=lnb_o, in_=beta_o, func=AF.Ln), [r_gproc], [r_gproc])
                yield
                s.op("dve", lambda e: e.tensor_tensor(out=g_o, in0=aview, in1=dtb[:].rearrange("p (d h) -> p d h", d=2).unsqueeze(1).to_broadcast([128, NT, 2, 4]),
                                                      op=ALU.add), [r_gst[sk], r_dtb, r_gproc], [r_gproc])
                yield
                s.op("dve", lambda e: e.tensor_scalar(out=gproc[:, :, 8:16], in0=gproc[:, :, 8:16], scalar1=80.0, scalar2=None, op0=ALU.min), [r_gproc], [r_gproc])
                yield
                s.op("act", lambda e: e.activation(out=gproc[:, :, 8:16], in_=gproc[:, :, 8:16], func=AF.Exp), [r_gproc], [r_gproc])
                yield
                s.op("act", lambda e: e.activation(out=gproc[:, :, 8:16], in_=gproc[:, :, 8:16], func=AF.Ln, bias=1.0), [r_gproc], [r_gproc])
                yield
                s.op("dve", lambda e: e.tensor_tensor(out=gproc[:, :, 8:16], in0=gproc[:, :, 8:16], in1=nega[:].unsqueeze(1).to_broadcast([128, NT, 8]),
                                                      op=ALU.mult), [r_gproc, r_nega], [r_gproc])
                yield
                gdst = c.gates[sq_i].rearrange("(t p) c -> p t c", p=128)
                for qd in range(4):
                    s.dma("sp", gdst[:, qd * 8:(qd + 1) * 8, :], gproc[:, qd * 8:(qd + 1) * 8, :], reads=[r_gproc], writes=[c.r_gates[sq_i]])

        seqs = [(sq_i, st) for sq_i in range(c.n_seq) for st in range(8)]
        _interleave([prepA(*seqs[0])])
        for k, (sq_i, st) in enumerate(seqs):
            gens = [mainA(sq_i, st)]
            if k + 1 < len(seqs):
                gens.append(prepA(*seqs[k + 1]))
            _interleave(gens)


def phase_D(c, l, src, dst):
    nc, s = c.nc, c.s
    with ExitStack() as es:
        def sb(name, shape, dt):
            return es.enter_context(nc.sbuf_tensor(_u(name), shape, dt))
        Wo = sb("Wo", [128, 8, D], BF16); r_Wo = Res()
        stage = [sb(f"wostage{i}", [128, 8, 512], F32) for i in range(2)]
        r_stage = [Res() for _ in range(2)]
        wv = c.w_out[l].rearrange("(t p) e -> p t e", p=128)
        for ci in range(2):
            s.dma("sp", stage[ci][:], wv[:, :, ci * 512:(ci + 1) * 512], writes=[r_stage[ci]])
            s.op("dve" if ci == 0 else "pool", lambda e, ci=ci: e.tensor_copy(Wo[:, :, ci * 512:(ci + 1) * 512], stage[ci][:]), [r_stage[ci]], [r_Wo])
        xt = [sb(f"xtD{i}", [128, D], F32) for i in range(2)]; r_xt = [Res() for _ in range(2)]
        mt = [sb(f"mtD{i}", [128, D], BF16) for i in range(2)]; r_mt = [Res() for _ in range(2)]
        mT = [sb(f"mTD{i}", [128, 8, 128], BF16) for i in range(2)]; r_mT = [Res() for _ in range(2)]
        yo = [sb(f"yoD{i}", [128, D], F32) for i in range(2)]; r_yo = [Res() for _ in range(2)]
        n = 0
        for sq_i in range(c.n_seq):
            for i in range(NT):
                b = n % 2
                n += 1
                rows = slice(i * 128, (i + 1) * 128)
                s.dma("sp", xt[b][:], src[sq_i, rows, :], reads=[c.r_xmid[sq_i]], writes=[r_xt[b]])
                s.dma("act", mt[b][:], c.mix[sq_i, rows, :], reads=[c.r_mix[sq_i]], writes=[r_mt[b]])
                pt, r_pt = c.nextbank()
                ptb = pt[:].bitcast(BF16)
                with s.group("pe"):
                    for et in range(8):
                        s.op("pe", lambda e, et=et, b=b: e.transpose(ptb[:, et * 128:(et + 1) * 128], mt[b][:, et * 128:(et + 1) * 128], c.ident[:]),
                             [r_mt[b], c.r_ident], [r_pt])
                s.op("act", lambda e, b=b: e.copy(mT[b][:], ptb.rearrange("p (a b) -> p a b", a=8)), [r_pt], [r_mT[b]])
                for ch in range(2):
                    pp, r_pp = c.nextbank()
                    with s.group("pe"):
                        for et in range(8):
                            s.op("pe", lambda e, et=et, b=b, ch=ch, pp=pp: e.matmul(
                                pp[:], lhsT=mT[b][:, et, :], rhs=Wo[:, et, ch * 512:(ch + 1) * 512], start=(et == 0), stop=(et == 7)),
                                [r_mT[b], r_Wo], [r_pp])
                    s.op("dve", lambda e, b=b, ch=ch, pp=pp: e.tensor_tensor(
                        out=yo[b][:, ch * 512:(ch + 1) * 512], in0=pp[:], in1=xt[b][:, ch * 512:(ch + 1) * 512], op=ALU.add),
                        [r_pp, r_xt[b]], [r_yo[b]])
                wres = [c.r_xmid[sq_i]] if dst is c.xmid else []
                s.dma("pool", dst[sq_i, rows, :], yo[b][:], reads=[r_yo[b]], writes=wres)


def _rs(r):
    return min(max(r - 4, 0), 56)


def phase_B(c, l):
    nc, s = c.nc, c.s
    Rj = {}
    for j in range(32):
        rows = [r for r in range(64) if _rs(r) <= 2 * j + 1 and _rs(r) + 7 >= 2 * j]
        Rj[j] = (rows[0], rows[-1])
    c.ring_banks = [0, 1, 2, 3, 4, 5]
    with ExitStack() as es:
        def sb(name, shape, dt):
            return es.enter_context(nc.sbuf_tensor(_u(name), shape, dt))
        Mtab = sb("Mtab", [128, 8, 1024], BF16); r_Mtab = Res()
        with ExitStack() as es2:
            nav = es2.enter_context(nc.sbuf_tensor(_u("nav"), [128, 1024], F32)); r_nav = Res()
            stg = [es2.enter_context(nc.sbuf_tensor(_u(f"rstg{i}"), [128, 1024], F32)) for i in range(2)]
            r_stg = [Res() for _ in range(2)]
            s.dma("sp", nav[:], c.c_navalid, writes=[r_nav])
            for h in range(8):
                b = h % 2
                s.dma("sp", stg[b][:], c.rpbg[l, h], writes=[r_stg[b]])
                s.op("act", lambda e, b=b: e.activation(out=stg[b][:], in_=stg[b][:], func=AF.Exp), [r_stg[b]], [r_stg[b]])
                s.op("dve", lambda e, b=b, h=h: e.tensor_tensor(out=Mtab[:, h, :], in0=stg[b][:], in1=nav[:], op=ALU.mult),
                     [r_stg[b], r_nav], [r_Mtab])
            s.barrier()
        V1 = sb("V1all", [128, NT, 520], BF16); r_V1 = Res()
        mixA = sb("mixA", [128, NT, 512], BF16); r_mixA = Res()
        QT2 = [sb(f"QT2_{i}", [128, L], BF16) for i in range(2)]; r_QT2 = [Res() for _ in range(2)]
        KT2 = [sb(f"KT2_{i}", [128, L], BF16) for i in range(2)]; r_KT2 = [Res() for _ in range(2)]
        NSL = 7
        Es = [[sb(f"E{hh}_{k}", [128, 768], BF16) for k in range(NSL)] for hh in range(2)]
        r_Es = [[Res() for k in range(NSL)] for hh in range(2)]
        rdl = [sb(f"rdB{k}", [128, 2], F32) for k in range(2)]; r_rdl = [Res() for _ in range(2)]
        attl = [sb(f"attB{k}", [128, 2, 64], F32) for k in range(2)]; r_attl = [Res() for _ in range(2)]
        nq = 0
        for sq_i in range(c.n_seq):
            v1v = c.v1[sq_i].rearrange("(j p) c -> p j c", p=128)
            mxv = c.mix[sq_i].rearrange("(j p) c -> p j c", p=128)
            for qd in range(4):
                s.dma("sp", V1[:, qd * 8:(qd + 1) * 8, :], v1v[:, qd * 8:(qd + 1) * 8, :], reads=[c.r_v1[sq_i]], writes=[r_V1])
                s.dma("act", mixA[:, qd * 8:(qd + 1) * 8, :], mxv[:, qd * 8:(qd + 1) * 8, 0:512], reads=[c.r_mix[sq_i]], writes=[r_mixA])
            qbs = {}
            for hp in range(4):
                qbs[hp] = nq % 2
                nq += 1

            def qk_gen(hp, j):
                qb = qbs[hp]
                if j == 0:
                    s.dma("sp", QT2[qb][:], c.qT[sq_i, hp * 128:(hp + 1) * 128, :], reads=[c.r_qT[sq_i]], writes=[r_QT2[qb]])
                    s.dma("sp", KT2[qb][:], c.kT[sq_i, hp * 128:(hp + 1) * 128, :], reads=[c.r_kT[sq_i]], writes=[r_KT2[qb]])
                lo, hi = Rj[j]
                ncols = (hi - lo + 1) * 64
                t0 = lo - 2 * j + 7
                sl = (hp * 32 + j) % NSL
                for hh in range(2):
                    head = 2 * hp + hh
                    ph = slice(hh * 64, hh * 64 + 64)
                    c0 = 0
                    while c0 < ncols:
                        cw = min(512, ncols - c0)
                        pp, r_pp = c.nextbank()
                        s.op("pe", lambda e: e.matmul(
                            pp[:, 0:cw], lhsT=KT2[qb][ph, 128 * j:128 * j + 128], rhs=QT2[qb][ph, 64 * lo + c0:64 * lo + c0 + cw],
                            start=True, stop=True), [r_KT2[qb], r_QT2[qb]], [r_pp])
                        yield
                        s.op("act", lambda e: e.activation(out=Es[hh][sl][:, c0:c0 + cw], in_=pp[:, 0:cw], func=AF.Exp), [r_pp], [r_Es[hh][sl]])
                        yield
                        eng = "dve" if (hh == 0 or (j % 3 == 0)) else "pool"
                        s.op(eng, lambda e: e.tensor_tensor(
                            out=Es[hh][sl][:, c0:c0 + cw], in0=Es[hh][sl][:, c0:c0 + cw],
                            in1=Mtab[:, head, t0 * 64 + c0:t0 * 64 + c0 + cw], op=ALU.mult), [r_Es[hh][sl], r_Mtab], [r_Es[hh][sl]])
                        yield
                        c0 += cw
                    for r in range(lo, hi + 1):
                        for a in range(2):
                            kr = 2 * j + a
                            if not (_rs(r) <= kr <= _rs(r) + 7):
                                s.op("pool", lambda e: e.memset(Es[hh][sl][a * 64:(a + 1) * 64, (r - lo) * 64:(r - lo + 1) * 64], 0.0),
                                     [r_Es[hh][sl]], [r_Es[hh][sl]])
                                yield

            def pv_gen(hp, j):
                for i in range(32):
                    r0, r1 = 2 * i, 2 * i + 1
                    T0 = list(range(_rs(r0) // 2, (_rs(r0) + 7) // 2 + 1))
                    T1 = list(range(_rs(r1) // 2, (_rs(r1) + 7) // 2 + 1))
                    if max(T0[-1], T1[-1]) != j:
                        continue
                    ob, r_ob = c.pb[6 + (i % 2)], c.pr[6 + (i % 2)]
                    common = [jj for jj in T0 if jj in T1]
                    ex0 = [jj for jj in T0 if jj not in T1]
                    ex1 = [jj for jj in T1 if jj not in T0]
                    plan = [(jj, 0, 128) for jj in common] + [(jj, 0, 64) for jj in ex0] + [(jj, 64, 64) for jj in ex1]
                    for hh in range(2):
                        head = 2 * hp + hh
                        with s.group("pe"):
                            for pi, (jj, p0, m) in enumerate(plan):
                                r = r0 if p0 == 0 else r1
                                blk = (r - Rj[jj][0]) * 64
                                sl = (hp * 32 + jj) % NSL
                                s.op("pe", lambda e: e.matmul(
                                    ob[p0:p0 + m, hh * 66:hh * 66 + 65], lhsT=Es[hh][sl][:, blk:blk + m],
                                    rhs=V1[:, jj, head * 65:head * 65 + 65], start=(pi == 0), stop=(pi == len(plan) - 1)),
                                    [r_Es[hh][sl], r_V1], [r_ob])
                        yield
                    rd, r_rd, att, r_att = rdl[i % 2], r_rdl[i % 2], attl[i % 2], r_attl[i % 2]
                    ov = ob[:, 0:132].rearrange("p (h d) -> p h d", h=2)
                    s.op("dve", lambda e: e.reciprocal(out=rd[:], in_=ov[:, :, 64]), [r_ob, r_rd], [r_rd])
                    yield
                    s.op("dve", lambda e: e.tensor_tensor(out=att[:], in0=ov[:, :, 0:64], in1=rd[:].unsqueeze(2).to_broadcast([128, 2, 64]), op=ALU.mult),
                         [r_ob, r_rd, r_att], [r_att])
                    yield
                    s.op("pool", lambda e: e.tensor_tensor(
                        out=mixA[:, i, hp * 128:(hp + 1) * 128], in0=att[:].rearrange("p h d -> p (h d)"),
                        in1=mixA[:, i, hp * 128:(hp + 1) * 128], op=ALU.mult), [r_att, r_mixA], [r_mixA])
                    yield

            units = [(hp, j) for hp in range(4) for j in range(32)]
            LAG = 2
            for k in range(len(units) + LAG):
                gens = []
                if k - LAG >= 0:
                    gens.append(pv_gen(*units[k - LAG]))
                if k < len(units):
                    gens.append(qk_gen(*units[k]))
                _interleave(gens)
            for qd in range(4):
                s.dma("pool", mxv[:, qd * 8:(qd + 1) * 8, 0:512], mixA[:, qd * 8:(qd + 1) * 8, :], reads=[r_mixA], writes=[c.r_mix[sq_i]])
    c.ring_banks = list(range(8))


def phase_C(c, l):
    nc, s = c.nc, c.s
    NCH = 64
    with ExitStack() as es:
        def sb(name, shape, dt):
            return es.enter_context(nc.sbuf_tensor(_u(name), shape, dt))
        mskf = sb("mskf", [128, 7, 128], F32); r_mskf = Res()
        s.dma("sp", mskf[:], c.c_masks[:, 0:7, :], writes=[r_mskf])
        BD_ONES, BD_GT, BD_LE, BD_LT, BD_GE, CH_A, CH_B = [mskf[:, k, :] for k in range(7)]
        mske = sb("mske", [128, 4, 128], F32); r_mske = Res()
        s.dma("sp", mske[:], c.c_masks[:, 7:11, :], writes=[r_mske])
        Ma = mske[:, 0, :].rearrange("p (d c) -> p d c", d=2)
        Mb = mske[:, 1, :].rearrange("p (d c) -> p d c", d=2)
        Mc = mske[:, 2, :].rearrange("p (d c) -> p d c", d=2)
        EQ = mske[:, 3, 0:64]
        ones_b = sb("ones_b", [128, 128], BF16); r_onesb = Res()
        s.op("pool", lambda e: e.memset(ones_b[:], 1.0), [], [r_onesb])
        dnw = sb("dnw_bc", [128, 128], F32); r_dnw = Res()
        s.dma("sp", dnw[:], c.dn_w[l].partition_broadcast(128), writes=[r_dnw])
        cw = sb("cwC", [128, 12, 5], F32); r_cw = Res()
        s.dma("sp", cw[:], c.conv_w[l].rearrange("(t p) j -> p t j", p=128), writes=[r_cw])

        kqT = sb("kqT", [128, 4, NCH, 128], BF16); r_kqT = Res()
        GT = sb("GTall", [128, NT, 24], F32); r_GT = Res()
        sc = sb("scC", [128, NT, 32], F32); r_scl = [Res() for _ in range(NT)]

        for sq_i in range(c.n_seq):
            gsrc = c.gates[sq_i].rearrange("(t p) c -> p t c", p=128)
            for qd in range(4):
                s.dma("sp", GT[:, qd * 8:(qd + 1) * 8, :], gsrc[:, qd * 8:(qd + 1) * 8, :], reads=[c.r_gates[sq_i]], writes=[r_GT])
            with ExitStack() as e0:
              if "0" in CPARTS:
                  def sb0(name, shape, dt):
                      return e0.enter_context(nc.sbuf_tensor(_u(name), shape, dt))
                  dg = sb0("dgC", [128, 60, 128], BF16); r_dg = Res()
                  for t in range(12):
                      for jx in range(5):
                          eng = "dve" if (t * 5 + jx) % 2 == 0 else "pool"
                          s.op(eng, lambda e, t=t, jx=jx: e.tensor_scalar(out=dg[:, t * 5 + jx, :], in0=c.ident_f[:], scalar1=cw[:, t, jx:jx + 1],
                                                                          scalar2=None, op0=ALU.mult), [c.r_identf, r_cw], [r_dg])
                  NS0 = 6
                  T0 = []
                  for k in range(NS0):
                      T = {}
                      for nm, shp, dt in (("xin", [128, 516], BF16), ("yb", [128, 512], F32), ("sqb", [128, 512], BF16), ("rnb", [128, 512], F32),
                                          ("knb", [128, 512], BF16), ("tst", [128, 4, 128], BF16)):
                          T[nm] = sb0(f"{nm}0_{k}", shp, dt)
                          T["r_" + nm] = Res()
                      T0.append(T)

                  def c0_unit(tb, h, qkv, T):
                      xin, yb, sqb, rnb, knb = T["xin"], T["yb"], T["sqb"], T["rnb"], T["knb"]
                      r_xin, r_yb, r_sqb, r_rnb, r_knb = T["r_xin"], T["r_yb"], T["r_sqb"], T["r_rnb"], T["r_knb"]
                      t = qkv * 4 + h
                      ch0 = t * 128
                      lo_c, hi_c = tb * 512, tb * 512 + 516
                      d0, d1 = 0, 516
                      if tb == 0:
                          lo_c, d0 = 2, 2
                          s.op("pool", lambda e: e.memset(xin[:, 0:2], 0.0), [r_xin], [r_xin])
                      if tb == 7:
                          hi_c, d1 = L + 2, 514
                          s.op("pool", lambda e: e.memset(xin[:, 514:516], 0.0), [r_xin], [r_xin])
                      s.dma("sp", xin[:, d0:d1], c.cin[sq_i, ch0:ch0 + 128, lo_c:hi_c], reads=[c.r_cin[sq_i]], writes=[r_xin])
                      yield
                      pp, r_pp = c.nextbank()
                      with s.group("pe"):
                          for jx in range(5):
                              s.op("pe", lambda e, jx=jx: e.matmul(pp[:], lhsT=dg[:, t * 5 + jx, :], rhs=xin[:, jx:jx + 512], start=(jx == 0), stop=(jx == 4)),
                                   [r_dg, r_xin], [r_pp])
                      yield
                      if qkv == 2:
                          s.op("act", lambda e: e.activation(out=knb[:], in_=pp[:], func=AF.Silu), [r_pp, r_knb], [r_knb])
                          yield
                          yield
                          yield
                          yield
                          yield
                          yield
                          pt, r_pt = c.nextbank()
                          ptb = pt[:].bitcast(BF16)
                          with s.group("pe"):
                              for k in range(4):
                                  s.op("pe", lambda e, k=k: e.transpose(ptb[:, k * 128:(k + 1) * 128], knb[:, k * 128:(k + 1) * 128], c.ident[:]),
                                       [r_knb, c.r_ident], [r_pt])
                          yield
                          s.op("dve", lambda e: e.tensor_copy(T["tst"][:], ptb[:, 0:512].rearrange("p (a b) -> p a b", a=4)), [r_pt, T["r_tst"]], [T["r_tst"]])
                          yield
                          s.dma("pool", c.vtd[sq_i].rearrange("(t p) (h d) -> p t h d", p=128, h=4)[:, tb * 4:(tb + 1) * 4, h, :], T["tst"][:],
                                reads=[T["r_tst"]], writes=[c.r_vtd[sq_i]])
                      else:
                          s.op("act", lambda e: e.activation(out=yb[:], in_=pp[:], func=AF.Silu), [r_pp, r_yb], [r_yb])
                          yield
                          s.op("pool", lambda e: e.tensor_tensor(out=sqb[:], in0=yb[:], in1=yb[:], op=ALU.mult), [r_yb, r_sqb], [r_sqb])
                          yield
                          p2, r_p2 = c.nextbank()
                          s.op("pe", lambda e: e.matmul(p2[:], lhsT=ones_b[:], rhs=sqb[:], start=True, stop=True), [r_onesb, r_sqb], [r_p2])
                          yield
                          s.op("act", lambda e: e.activation(out=rnb[:], in_=p2[:], func=AF.Ln, bias=EPS), [r_p2, r_rnb], [r_rnb])
                          yield
                          s.op("act", lambda e: e.activation(out=rnb[:], in_=rnb[:], func=AF.Exp, scale=-0.5), [r_rnb], [r_rnb])
                          yield
                          if qkv == 0:
                              s.op("dve", lambda e: e.scalar_tensor_tensor(
                                  out=kqT[:, h, tb * 8:(tb + 1) * 8, 64:128], in0=yb[:].rearrange("p (n c) -> p n c", n=8), scalar=128 ** -0.5,
                                  in1=rnb[:].rearrange("p (n c) -> p n c", n=8), op0=ALU.mult, op1=ALU.mult), [r_yb, r_rnb], [r_kqT])
                          else:
                              s.op("dve", lambda e: e.tensor_tensor(out=knb[:], in0=yb[:], in1=rnb[:], op=ALU.mult), [r_yb, r_rnb, r_knb], [r_knb])
                              yield
                              s.op("pool", lambda e: e.tensor_copy(kqT[:, h, tb * 8:(tb + 1) * 8, 0:64], knb[:].rearrange("p (n c) -> p n c", n=8)),
                                   [r_knb], [r_kqT])
                              pt, r_pt = c.nextbank()
                              ptb = pt[:].bitcast(BF16)
                              with s.group("pe"):
                                  for k in range(4):
                                      s.op("pe", lambda e, k=k: e.transpose(ptb[:, k * 128:(k + 1) * 128], knb[:, k * 128:(k + 1) * 128], c.ident[:]),
                                           [r_knb, c.r_ident], [r_pt])
                              yield
                              s.op("act", lambda e: e.copy(T["tst"][:], ptb[:, 0:512].rearrange("p (a b) -> p a b", a=4)), [r_pt, T["r_tst"]], [T["r_tst"]])
                              yield
                              s.dma("pool", c.ktd[sq_i].rearrange("(t p) (h d) -> p t h d", p=128, h=4)[:, tb * 4:(tb + 1) * 4, h, :], T["tst"][:],
                                    reads=[T["r_tst"]], writes=[c.r_ktd[sq_i]])

                  for tb in range(8):
                      for hp in range(2):
                          units = [(tb, 2 * hp + hh, qkv) for hh in range(2) for qkv in range(3)]
                          _interleave([c0_unit(*u, T0[k]) for k, u in enumerate(units)])
                  s.barrier()
            with ExitStack() as e1:
              if "1" in CPARTS or "2" in CPARTS:
                  def sb1(name, shape, dt):
                      return e1.enter_context(nc.sbuf_tensor(_u(name), shape, dt))
                  NSLOT = 2
                  slots = []
                  for k in range(NSLOT):
                      T = {}
                      for nm, shp, dt in (("RG0", [128, 2, 4, 64], F32), ("RG1", [128, 2, 4, 64], F32), ("RB", [128, 2, 4, 64], F32),
                                          ("zc", [128, 32], F32), ("beg", [128, 8], F32),
                                          ("Nbd", [128, 8, 128], BF16), ("NTbd", [128, 8, 128], BF16), ("Pb", [128, 8, 128], BF16),
                                          ("Qb", [128, 8, 128], BF16), ("Zb0", [128, 8, 128], BF16), ("Zb1", [128, 8, 128], BF16),
                                          ("bv", [128, 8, 128], BF16), ("kgb", [128, 8, 128], BF16), ("rec", [128, 4096], BF16),
                                          ("kt", [128, 4, 128], BF16), ("vt", [128, 4, 128], BF16)):
                          T[nm] = sb1(f"{nm}_{k}", shp, dt)
                          T["r_" + nm] = Res()
                      s.op("pool", lambda e, T=T: e.memset(T["Nbd"][:], 0.0), [], [T["r_Nbd"]])
                      s.op("pool", lambda e, T=T: e.memset(T["NTbd"][:], 0.0), [], [T["r_NTbd"]])
                      slots.append(T)
                  bc = lambda ap: ap.unsqueeze(3).to_broadcast([128, 2, 4, 64])
                  bm = lambda m: m.unsqueeze(2).to_broadcast([128, 2, 4, 64])
                  LTm = (BD_GT, BD_LT)
                  LT2m = (BD_LE, BD_GE)
                  done_c1 = set()
                  done_o = [set(), set()]
                  r_zst_t = [Res() for _ in range(NT)]
                  r_ost_t = {(d, n): Res() for d in range(2) for n in range(NCH)}

                  def c1_tile(i, T):
                      RG0, RG1, RB, zc, beg = T["RG0"], T["RG1"], T["RB"], T["zc"], T["beg"]
                      r_RG0, r_RG1, r_RB, r_zc, r_beg = T["r_RG0"], T["r_RG1"], T["r_RB"], T["r_zc"], T["r_beg"]
                      Nbd, NTbd, r_Nbd, r_NTbd = T["Nbd"], T["NTbd"], T["r_Nbd"], T["r_NTbd"]
                      rec, r_rec = T["rec"], T["r_rec"]
                      r_sc = r_scl[i]
                      recv = rec[:].rearrange("p (d x) -> p d x", d=2)
                      kt, vt, r_kt, r_vt = T["kt"], T["vt"], T["r_kt"], T["r_vt"]
                      s.dma("sp", kt[:].rearrange("p h d -> p (h d)"), c.ktd[sq_i, i * 128:(i + 1) * 128, :], reads=[c.r_ktd[sq_i]], writes=[r_kt])
                      s.dma("sp", vt[:].rearrange("p h d -> p (h d)"), c.vtd[sq_i, i * 128:(i + 1) * 128, :], reads=[c.r_vtd[sq_i]], writes=[r_vt])
                      g8 = GT[:, i, 8:16]
                      gv = g8.rearrange("p (d h) -> p d h", d=2)
                      beta8 = GT[:, i, 0:8]
                      beta = beta8.rearrange("p (d h) -> p d h", d=2)
                      s.op("dve", lambda e: e.tensor_tensor(out=RG0[:], in0=bc(gv), in1=bm(Ma), op=ALU.mult), [r_GT, r_mske, r_RG0], [r_RG0])
                      s.op("pool", lambda e: e.tensor_tensor(out=RG1[:], in0=bc(gv), in1=bm(Mb), op=ALU.mult), [r_GT, r_mske, r_RG1], [r_RG1])
                      s.op("pool", lambda e: e.tensor_tensor(out=RB[:], in0=bc(beta), in1=EQ.unsqueeze(1).unsqueeze(1).to_broadcast([128, 2, 4, 64]), op=ALU.mult),
                           [r_GT, r_mske, r_RB], [r_RB])
                      yield
                      D1, r_D1, D2, r_D2, tI, r_tI = RG0, r_RG0, RG1, r_RG1, RB, r_RB
                      pX, r_pX = yield from c.getbank()
                      for d in range(2):
                          s.op("pe", lambda e, d=d: e.matmul(pX[:, d * 256:(d + 1) * 256], lhsT=LTm[d], rhs=RG0[:, d].rearrange("p h c -> p (h c)"),
                                                            start=True, stop=True), [r_mskf, r_RG0], [r_pX])
                      yield
                      s.op("act", lambda e: e.activation(out=D1[:].rearrange("p d h c -> p (d h c)"), in_=pX[:], func=AF.Exp), [r_pX, r_D1], [r_D1])
                      c.relbank(pX)
                      yield
                      pY, r_pY = yield from c.getbank()
                      for d in range(2):
                          s.op("pe", lambda e, d=d: e.matmul(pY[:, d * 256:(d + 1) * 256], lhsT=LT2m[d], rhs=RG1[:, d].rearrange("p h c -> p (h c)"),
                                                            start=True, stop=True), [r_mskf, r_RG1], [r_pY])
                      yield
                      s.op("act", lambda e: e.activation(out=D2[:].rearrange("p d h c -> p (d h c)"), in_=pY[:], func=AF.Exp), [r_pY, r_D2], [r_D2])
                      c.relbank(pY)
                      yield
                      pZ, r_pZ = yield from c.getbank()
                      with s.group("pe"):
                          s.op("pe", lambda e: e.matmul(pZ[:, 0:4], lhsT=BD_LE, rhs=g8[:, 0:4], start=True, stop=True), [r_mskf, r_GT], [r_pZ])
                          s.op("pe", lambda e: e.matmul(pZ[:, 4:8], lhsT=BD_GE, rhs=g8[:, 4:8], start=True, stop=True), [r_mskf, r_GT], [r_pZ])
                          s.op("pe", lambda e: e.matmul(pZ[:, 8:16], lhsT=BD_ONES, rhs=g8, start=True, stop=True), [r_mskf, r_GT], [r_pZ])
                          s.op("pe", lambda e: e.matmul(pZ[:, 16:24], lhsT=CH_A, rhs=g8, start=True, stop=True), [r_mskf, r_GT], [r_pZ])
                          s.op("pe", lambda e: e.matmul(pZ[:, 24:32], lhsT=CH_B, rhs=g8, start=True, stop=True), [r_mskf, r_GT], [r_pZ])
                      yield
                      s.op("act", lambda e: e.copy(zc[:], pZ[:, 0:32]), [r_pZ, r_zc], [r_zc])
                      c.relbank(pZ)
                      yield
                      s.op("dve", lambda e: e.tensor_tensor(out=zc[:, 8:16], in0=zc[:, 8:16], in1=zc[:, 0:8], op=ALU.subtract), [r_zc], [r_zc])
                      yield
                      s.op("act", lambda e: e.activation(out=sc[:, i, :], in_=zc[:], func=AF.Exp), [r_zc, r_sc], [r_sc])
                      yield
                      s.op("dve", lambda e: e.tensor_tensor(out=beg[:], in0=beta8, in1=sc[:, i, 0:8], op=ALU.mult), [r_GT, r_sc, r_beg], [r_beg])
                      s.op("pool", lambda e: e.tensor_tensor(out=recv[:, :, 1280:1536].rearrange("p d (h c) -> p d h c", h=4),
                                                             in0=bc(sc[:, i, 0:8].rearrange("p (d h) -> p d h", d=2)),
                                                             in1=EQ.unsqueeze(1).unsqueeze(1).to_broadcast([128, 2, 4, 64]), op=ALU.mult),
                           [r_sc, r_mske, r_rec], [r_rec])
                      yield
                      pB, r_pB = yield from c.getbank()
                      s.op("pe", lambda e: e.matmul(pB[:], lhsT=BD_ONES, rhs=RB[:].rearrange("p d h c -> p (d h c)"), start=True, stop=True), [r_mskf, r_RB], [r_pB])
                      yield
                      pW, r_pW = yield from c.getbank()
                      with s.group("pe"):
                          for h in range(4):
                              for a in range(2):
                                  n = 2 * i + a
                                  s.op("pe", lambda e, h=h, a=a, n=n: e.matmul(pW[a * 64:(a + 1) * 64, h * 128:(h + 1) * 128], lhsT=kqT[:, h, n, 0:64],
                                                                             rhs=kqT[:, h, n, :], start=True, stop=True), [r_kqT], [r_pW])
                      yield
                      Wv = pW[:].rearrange("p (h x) -> p h x", h=4)
                      KK = Wv[:, :, 0:64].unsqueeze(1).to_broadcast([128, 2, 4, 64])
                      KQ = Wv[:, :, 64:128].unsqueeze(1).to_broadcast([128, 2, 4, 64])
                      s.op("dve", lambda e: e.tensor_tensor(out=tI[:], in0=D1[:], in1=KQ, op=ALU.mult), [r_D1, r_pW, r_tI], [r_tI])
                      yield
                      s.op("dve", lambda e: e.tensor_tensor(out=D1[:], in0=D1[:], in1=KK, op=ALU.mult), [r_D1, r_pW], [r_D1])
                      yield
                      s.op("dve", lambda e: e.tensor_tensor(out=D2[:], in0=D2[:], in1=KK, op=ALU.mult), [r_D2, r_pW], [r_D2])
                      c.relbank(pW)
                      yield
                      s.op("dve", lambda e: e.tensor_tensor(out=D1[:], in0=D1[:], in1=pB[:].rearrange("p (d h c) -> p d h c", d=2, h=4), op=ALU.mult), [r_D1, r_pB], [r_D1])
                      c.relbank(pB)
                      yield
                      s.op("pool", lambda e: e.tensor_tensor(out=D2[:], in0=D2[:], in1=bc(beta), op=ALU.mult), [r_D2, r_GT], [r_D2])
                      yield
                      s.op("pool", lambda e: e.tensor_tensor(out=recv[:, :, 1024:1280].rearrange("p d (h c) -> p d h c", h=4), in0=tI[:], in1=bm(Ma), op=ALU.mult),
                           [r_tI, r_mske, r_rec], [r_rec])
                      for a in range(2):
                          pa = slice(a * 64, (a + 1) * 64)
                          ca = slice(a * 64, (a + 1) * 64)
                          s.op("dve" if a == 0 else "pool", lambda e, pa=pa, ca=ca: e.tensor_tensor(
                              out=Nbd[pa, :, ca].rearrange("p (d h) c -> p d h c", d=2), in0=D1[pa], in1=bm(Mc)[pa], op=ALU.mult),
                              [r_D1, r_mske, r_Nbd], [r_Nbd])
                          s.op("pool" if a == 0 else "dve", lambda e, pa=pa, ca=ca: e.tensor_tensor(
                              out=NTbd[pa, :, ca].rearrange("p (d h) c -> p d h c", d=2), in0=D2[pa], in1=bm(Mb)[pa], op=ALU.mult),
                              [r_D2, r_mske, r_NTbd], [r_NTbd])
                          yield
                      Zb = [T["Zb0"], T["Zb1"]]
                      r_Zb = [T["r_Zb0"], T["r_Zb1"]]
                      s.op("pool", lambda e: e.tensor_tensor(out=Zb[0][:], in0=c.ident[:].unsqueeze(1).to_broadcast([128, 8, 128]), in1=Nbd[:], op=ALU.subtract),
                           [c.r_ident, r_Nbd, r_Zb[0]], [r_Zb[0]])
                      bv, r_bv, kgb, r_kgb = T["bv"], T["r_bv"], T["kgb"], T["r_kgb"]
                      bc128 = lambda ap: ap.rearrange("p (d h) -> p d h", d=2).unsqueeze(3).to_broadcast([128, 2, 4, 128])
                      s.op("pool", lambda e: e.tensor_tensor(out=bv[:].rearrange("p (d h) x -> p d h x", d=2), in0=vt[:].unsqueeze(1).to_broadcast([128, 2, 4, 128]),
                                                             in1=bc128(beta8), op=ALU.mult), [r_vt, r_GT, r_bv], [r_bv])
                      yield
                      s.op("pool", lambda e: e.tensor_tensor(out=kgb[:].rearrange("p (d h) x -> p d h x", d=2), in0=kt[:].unsqueeze(1).to_broadcast([128, 2, 4, 128]),
                                                             in1=bc128(beg[:]), op=ALU.mult), [r_kt, r_beg, r_kgb], [r_kgb])
                      yield
                      s.op("pool", lambda e: e.tensor_tensor(out=recv[:, :, 1536:2048].rearrange("p d (h x) -> p d h x", h=4), in0=kt[:].unsqueeze(1).to_broadcast([128, 2, 4, 128]),
                                                             in1=bc128(sc[:, i, 8:16]), op=ALU.mult), [r_kt, r_sc, r_rec], [r_rec])
                      yield
                      Pseq = [(NTbd, r_NTbd), (T["Pb"], T["r_Pb"])]
                      Qseq = [(Nbd, r_Nbd), (T["Qb"], T["r_Qb"])]
                      zi = 0
                      for lev in range(1, 6):
                          Pc, r_Pc = Pseq[(lev - 1) % 2]
                          Qc, r_Qc = Qseq[(lev - 1) % 2]
                          Pn, r_Pn = Pseq[lev % 2]
                          Qn, r_Qn = Qseq[lev % 2]
                          def mmP(half, pp, r_pp):
                              with s.group("pe"):
                                  for k in range(4):
                                      p = half * 4 + k
                                      s.op("pe", lambda e, p=p, k=k: e.matmul(pp[:, k * 128:(k + 1) * 128], lhsT=Qc[:, p, :], rhs=Pc[:, p, :],
                                                                            start=True, stop=True), [r_Pc, r_Qc], [r_pp])

                          def mmQ(half, pp, r_pp):
                              with s.group("pe"):
                                  for k in range(4):
                                      p = half * 4 + k
                                      s.op("pe", lambda e, p=p, k=k: e.matmul(pp[:, k * 128:(k + 1) * 128], lhsT=Pc[:, p, :], rhs=Qc[:, p, :],
                                                                            start=True, stop=True), [r_Pc, r_Qc], [r_pp])

                          def evP(half, pp, r_pp):
                              s.op("act", lambda e: e.copy(Pn[:, half * 4:(half + 1) * 4, :], pp[:].rearrange("p (k x) -> p k x", k=4)), [r_pp, r_Pn], [r_Pn])
                              c.relbank(pp)

                          def evQ(half, pp, r_pp):
                              s.op("act", lambda e: e.copy(Qn[:, half * 4:(half + 1) * 4, :], pp[:].rearrange("p (k x) -> p k x", k=4)), [r_pp, r_Qn], [r_Qn])
                              c.relbank(pp)
                          pa0, r_pa0 = yield from c.getbank()
                          mmP(0, pa0, r_pa0)
                          yield
                          pa1, r_pa1 = yield from c.getbank()
                          mmP(1, pa1, r_pa1)
                          yield
                          evP(0, pa0, r_pa0)
                          yield
                          if lev < 5:
                              qa0, r_qa0 = yield from c.getbank()
                              mmQ(0, qa0, r_qa0)
                              yield
                          evP(1, pa1, r_pa1)
                          yield
                          if lev < 5:
                              qa1, r_qa1 = yield from c.getbank()
                              mmQ(1, qa1, r_qa1)
                              yield
                              evQ(0, qa0, r_qa0)
                              yield
                              evQ(1, qa1, r_qa1)
                              yield
                          Zc, r_Zc = Zb[zi], r_Zb[zi]
                          Zn, r_Zn = Zb[1 - zi], r_Zb[1 - zi]
                          for half in range(2):
                              pp, r_pp = yield from c.getbank()
                              with s.group("pe"):
                                  for k in range(4):
                                      p = half * 4 + k
                                      s.op("pe", lambda e, p=p, k=k, pp=pp, Zc=Zc, Pn=Pn: e.matmul(pp[:, k * 128:(k + 1) * 128], lhsT=Pn[:, p, :], rhs=Zc[:, p, :],
                                                                                            start=True, stop=True), [r_Pn, r_Zc], [r_pp])
                              yield
                              ppv = pp[:].rearrange("p (k x) -> p k x", k=4)
                              s.op("dve", lambda e, ppv=ppv, half=half, Zn=Zn, Zc=Zc: e.tensor_tensor(out=Zn[:, half * 4:(half + 1) * 4, :], in0=ppv,
                                                                                                     in1=Zc[:, half * 4:(half + 1) * 4, :], op=ALU.add),
                                   [r_pp, r_Zc, r_Zn], [r_Zn])
                              c.relbank(pp)
                              yield
                          zi = 1 - zi
                      Z2, r_Z2 = Zb[zi], r_Zb[zi]
                      for half in range(2):
                          pp, r_pp = yield from c.getbank()
                          with s.group("pe"):
                              for k in range(4):
                                  p = half * 4 + k
                                  s.op("pe", lambda e, p=p, k=k, pp=pp: e.matmul(pp[:, k * 128:(k + 1) * 128], lhsT=Z2[:, p, :], rhs=bv[:, p, :], start=True, stop=True),
                                       [r_Z2, r_bv], [r_pp])
                          yield
                          s.op("act", lambda e, pp=pp, half=half: e.copy(recv[:, half, 0:512], pp[:]), [r_pp, r_rec], [r_rec])
                          c.relbank(pp)
                          yield
                      for half in range(2):
                          pp, r_pp = yield from c.getbank()
                          with s.group("pe"):
                              for k in range(4):
                                  p = half * 4 + k
                                  s.op("pe", lambda e, p=p, k=k, pp=pp: e.matmul(pp[:, k * 128:(k + 1) * 128], lhsT=kgb[:, p, :], rhs=Z2[:, p, :], start=True, stop=True),
                                       [r_kgb, r_Z2], [r_pp])
                          yield
                          s.op("dve", lambda e, pp=pp, half=half: e.tensor_copy(recv[:, half, 512:1024], pp[:]), [r_pp, r_rec], [r_rec])
                          c.relbank(pp)
                          yield
                      s.dma("pool", c.zst[sq_i, i].rearrange("d p x -> p d x"), recv, reads=[r_rec], writes=[r_zst_t[i]])

                  sb2 = sb1
                  zb = [[sb2(f"zbuf{d}_{k}", [128, 2048], BF16) for k in range(2)] for d in range(2)]
                  r_zb = [[Res() for k in range(2)] for d in range(2)]
                  S = [sb2(f"S{d}", [128, 4, 128], F32) for d in range(2)]; r_S = [Res() for _ in range(2)]
                  Sb = [sb2(f"Sb{d}", [128, 4, 128], BF16) for d in range(2)]; r_Sb = [Res() for _ in range(2)]
                  vnew = [sb2(f"vnew_{d}", [128, 4, 128], BF16) for d in range(2)]; r_vnew = [Res() for _ in range(2)]
                  p1sb = [sb2(f"p1sb_{d}", [128, 4, 128], BF16) for d in range(2)]; r_p1sb = [Res() for _ in range(2)]
                  ot = [sb2(f"ot_{d}", [128, 4, 128], F32) for d in range(2)]; r_ot = [Res() for _ in range(2)]
                  for d in range(2):
                      s.op("pool", lambda e, d=d: e.memset(S[d][:], 0.0), [r_S[d]], [r_S[d]])
                      s.op("pool", lambda e, d=d: e.memset(Sb[d][:], 0.0), [r_Sb[d]], [r_Sb[d]])
                  bc4 = lambda ap: ap.unsqueeze(2).to_broadcast([64, 4, 128])

                  def c2_step(t, d):
                      n = t if d == 0 else NCH - 1 - t
                      i, a = n // 2, n % 2
                      pa = slice(a * 64, (a + 1) * 64)
                      rows = slice(n * 64, (n + 1) * 64)
                      zt, r_zt = zb[d][i % 2], r_zb[d][i % 2]
                      r_sc = r_scl[i]
                      first_of_tile = (a == 0) if d == 0 else (a == 1)
                      if first_of_tile:
                          if t == 0:
                              while i not in done_c1:
                                  yield
                              s.dma("sp", zt[:], c.zst[sq_i, i, d], reads=[r_zst_t[i]], writes=[r_zt])
                          inx = i + 1 if d == 0 else i - 1
                          if 0 <= inx < NT:
                              while inx not in done_c1:
                                  yield
                              s.dma("sp", zb[d][inx % 2][:], c.zst[sq_i, inx, d], reads=[r_zst_t[inx]], writes=[r_zb[d][inx % 2]])
                      U = zt[:, 0:512].rearrange("p (q x) -> p q x", q=4)
                      Wt = zt[:, 512:1024].rearrange("p (q x) -> p q x", q=4)
                      IT = zt[:, 1024:1280].rearrange("p (q x) -> p q x", q=4)
                      DG = zt[:, 1280:1536].rearrange("p (q x) -> p q x", q=4)
                      KD = zt[:, 1536:2048].rearrange("p (q x) -> p q x", q=4)
                      eGt = sc[:, i, 16 + a * 8 + d * 4:16 + a * 8 + (d + 1) * 4]
                      yield
                      pWS, r_pWS = yield from c.getbank()
                      pP1, r_pP1 = yield from c.getbank()
                      with s.group("pe"):
                          for h in range(4):
                              s.op("pe", lambda e, h=h: e.matmul(pWS[pa, h * 128:(h + 1) * 128], lhsT=Wt[:, h, a * 64:(a + 1) * 64], rhs=Sb[d][:, h, :],
                                                                 start=True, stop=True), [r_zt, r_Sb[d]], [r_pWS])
                      yield
                      with s.group("pe"):
                          for h in range(4):
                              s.op("pe", lambda e, h=h: e.matmul(pP1[pa, h * 128:(h + 1) * 128], lhsT=kqT[:, h, n, 64:128], rhs=Sb[d][:, h, :],
                                                                 start=True, stop=True), [r_kqT, r_Sb[d]], [r_pP1])
                      yield
                      v3 = lambda bank: bank[pa, :].rearrange("p (h e) -> p h e", h=4)
                      s.op("dve", lambda e: e.tensor_tensor(out=vnew[d][pa], in0=U[pa, :, :], in1=v3(pWS), op=ALU.subtract),
                           [r_zt, r_pWS, r_vnew[d]], [r_vnew[d]])
                      c.relbank(pWS)
                      yield
                      s.op("act", lambda e: e.copy(p1sb[d][pa], v3(pP1)), [r_pP1, r_p1sb[d]], [r_p1sb[d]])
                      c.relbank(pP1)
                      yield
                      pSU, r_pSU = yield from c.getbank()
                      pP2, r_pP2 = yield from c.getbank()
                      with s.group("pe"):
                          for h in range(4):
                              s.op("pe", lambda e, h=h: e.matmul(pSU[:, h * 128:(h + 1) * 128], lhsT=KD[pa, h, :], rhs=vnew[d][pa, h, :],
                                                                 start=True, stop=True), [r_zt, r_vnew[d]], [r_pSU])
                      yield
                      with s.group("pe"):
                          for h in range(4):
                              s.op("pe", lambda e, h=h: e.matmul(pP2[pa, h * 128:(h + 1) * 128], lhsT=DG[pa, h, :], rhs=p1sb[d][pa, h, :],
                                                                 start=True, stop=False), [r_zt, r_p1sb[d]], [r_pP2])
                              s.op("pe", lambda e, h=h: e.matmul(pP2[pa, h * 128:(h + 1) * 128], lhsT=IT[pa, h, :], rhs=vnew[d][pa, h, :],
                                                                 start=False, stop=True), [r_zt, r_vnew[d]], [r_pP2])
                      yield
                      for h in range(4):
                          s.op("dve", lambda e, h=h: e.scalar_tensor_tensor(out=Sb[d][:, h, :], in0=S[d][:, h, :], scalar=eGt[:, h:h + 1], in1=pSU[:, h * 128:(h + 1) * 128],
                                                                            op0=ALU.mult, op1=ALU.add), [r_pSU, r_S[d], r_sc, r_Sb[d]], [r_Sb[d]])
                      yield
                      s.op("pool", lambda e: e.tensor_tensor(out=S[d][:], in0=S[d][:], in1=eGt.unsqueeze(2).to_broadcast([128, 4, 128]), op=ALU.mult),
                           [r_S[d], r_sc], [r_S[d]])
                      yield
                      s.op("dve", lambda e: e.tensor_tensor(out=S[d][:], in0=pSU[:].rearrange("p (h e) -> p h e", h=4), in1=S[d][:], op=ALU.add),
                           [r_pSU, r_S[d]], [r_S[d]])
                      c.relbank(pSU)
                      yield
                      s.op("act", lambda e: e.copy(ot[d][pa], v3(pP2)), [r_pP2, r_ot[d]], [r_ot[d]])
                      c.relbank(pP2)
                      yield
                      s.dma("sp", c.ost[sq_i, d, rows, :], ot[d][pa].rearrange("p h e -> p (h e)"), reads=[r_ot[d]], writes=[r_ost_t[(d, n)]])
                      done_o[d].add(n)

                  def c2_dir(d):
                      for t in range(NCH):
                          yield from c2_step(t, d)

                  order = []
                  for k in range(NT // 2):
                      order += [k, NT - 1 - k]

                  def c1_worker(w):
                      for i in order[w::NSLOT]:
                          yield from c1_tile(i, slots[w])
                          done_c1.add(i)
                  NS3 = 2
                  T3 = []
                  for k in range(NS3):
                      T = {}
                      for nm, shp, dt in (("of", [128, 4, 128], F32), ("ob", [128, 4, 128], F32), ("gt", [128, 4, 128], BF16), ("gd", [128, 4, 128], F32),
                                          ("jk", [128, 128], BF16), ("rs", [128, 8], F32), ("fo", [128, 512], BF16)):
                          T[nm] = sb1(f"{nm}3_{k}", shp, dt)
                          T["r_" + nm] = Res()
                      T3.append(T)
                  ss3 = sb1("ss3", [128, NT, 4], F32); r_ss3 = Res()
                  s.op("pool", lambda e: e.memset(ss3[:], 0.0), [], [r_ss3])

                  def c3_tile(i, T):
                      rows = slice(i * 128, (i + 1) * 128)
                      of, ob, gt, gd, jk, rs, fo = T["of"], T["ob"], T["gt"], T["gd"], T["jk"], T["rs"], T["fo"]
                      s.dma("sp", of[:].rearrange("p h e -> p (h e)"), c.ost[sq_i, 0, rows, :], reads=[r_ost_t[(0, 2 * i)], r_ost_t[(0, 2 * i + 1)]], writes=[T["r_of"]])
                      s.dma("act", ob[:].rearrange("p h e -> p (h e)"), c.ost[sq_i, 1, rows, :], reads=[r_ost_t[(1, 2 * i)], r_ost_t[(1, 2 * i + 1)]], writes=[T["r_ob"]])
                      s.dma("sp", gt[:].rearrange("p h e -> p (h e)"), c.mix[sq_i, rows, 512:1024], reads=[c.r_mix[sq_i]], writes=[T["r_gt"]])
                      yield
                      s.op("dve", lambda e: e.tensor_tensor(out=of[:], in0=of[:], in1=ob[:], op=ALU.add), [T["r_of"], T["r_ob"]], [T["r_of"]])
                      yield
                      s.op("pool", lambda e: e.tensor_tensor(out=gd[:], in0=gt[:], in1=dnw[:].unsqueeze(1).to_broadcast([128, 4, 128]), op=ALU.mult),
                           [T["r_gt"], r_dnw, T["r_gd"]], [T["r_gd"]])
                      yield
                      for h in range(4):
                          s.op("act", lambda e, h=h: e.activation(out=jk[:], in_=of[:, h, :], func=AF.Square, accum_out=ss3[:, i, h:h + 1]),
                               [T["r_of"], T["r_jk"], r_ss3], [T["r_jk"], r_ss3])
                          yield
                      s.op("act", lambda e: e.activation(out=rs[:, 0:4], in_=ss3[:, i, :], func=AF.Ln, bias=EPS, scale=1.0 / 128), [r_ss3, T["r_rs"]], [T["r_rs"]])
                      yield
                      s.op("act", lambda e: e.activation(out=rs[:, 4:8], in_=rs[:, 0:4], func=AF.Exp, scale=-0.5), [T["r_rs"]], [T["r_rs"]])
                      yield
                      s.op("dve", lambda e: e.tensor_tensor(out=of[:], in0=of[:], in1=rs[:, 4:8].unsqueeze(2).to_broadcast([128, 4, 128]), op=ALU.mult),
                           [T["r_of"], T["r_rs"]], [T["r_of"]])
                      yield
                      s.op("pool", lambda e: e.tensor_tensor(out=fo[:].rearrange("p (h e) -> p h e", h=4), in0=of[:], in1=gd[:], op=ALU.mult),
                           [T["r_of"], T["r_gd"], T["r_fo"]], [T["r_fo"]])
                      yield
                      s.dma("sp", c.mix[sq_i, rows, 512:1024], fo[:], reads=[T["r_fo"]], writes=[c.r_mix[sq_i]])


                  c3_order = sorted(range(NT), key=lambda i: (max(2 * i + 1, 63 - 2 * i), i))

                  def c3_worker(w):
                      for i in c3_order[w::NS3]:
                          while not all((2 * i + a) in done_o[d] for d in range(2) for a in range(2)):
                              yield
                          yield from c3_tile(i, T3[w])
                  c.free_banks = list(range(8))
                  _interleave([c1_worker(w) for w in range(NSLOT)] + [c2_dir(0), c2_dir(1)] + [c3_worker(w) for w in range(NS3)])
                  assert len(c.free_banks) == 8
                  s.barrier()


def _interleave(gens):
    gens = list(gens)
    while gens:
        for g in list(gens):
            try:
                next(g)
            except StopIteration:
                gens.remove(g)


def _host_constants():
    ident = np.eye(128, dtype=np.float32)
    masks = np.zeros((128, 16, 128), np.float32)
    p = np.arange(128)[:, None]
    q = np.arange(128)[None, :]
    same = (p // 64 == q // 64)
    masks[:, 0, :] = same
    masks[:, 1, :] = same & (p > q)
    masks[:, 2, :] = same & (p <= q)
    masks[:, 3, :] = same & (p < q)
    masks[:, 4, :] = same & (p >= q)
    masks[:, 5, :] = (p < 64) & (q >= 0)
    masks[:, 6, :] = (p >= 64) & (q >= 0)
    pl = p % 64
    ql = q % 64
    first = q < 64
    masks[:, 7, :] = np.where(first, pl <= ql, pl >= ql)
    masks[:, 8, :] = np.where(first, pl > ql, pl < ql)
    masks[:, 9, :] = np.where(first, pl < ql, pl > ql)
    masks[:, 10, :] = (pl == ql)
    return ident, masks


def _rpb_gather(rpb):
    a = np.arange(2)[:, None, None, None]
    kc = np.arange(64)[None, :, None, None]
    t = np.arange(16)[None, None, :, None]
    qc = np.arange(64)[None, None, None, :]
    i = np.clip(a + 14 - t, 0, 14) + 0 * kc + 0 * qc
    jj = np.clip(kc - qc + 15, 0, 30) + 0 * a + 0 * t
    g = rpb[:, :, i, jj]
    return np.ascontiguousarray(g.reshape(2, 8, 128, 1024))


def _na_valid():
    v = np.zeros((2, 64, 16, 64), np.float32)
    for a in range(2):
        for t in range(16):
            i = a + 14 - t
            if not (0 <= i <= 14):
                continue
            for qc in range(64):
                cs = int(np.clip(qc - 8, 0, 48))
                v[a, cs:cs + 16, t, qc] = 1.0
    return v.reshape(128, 1024)


def make_in_maps(inputs, n_cores=NCORES, n_seq=SEQ_PER_CORE):
    f = lambda a: np.ascontiguousarray(np.asarray(a, dtype=np.float32))
    ident, masks = _host_constants()
    shared = {
        "norm_w": f(inputs["norm_w"]), "w_in": f(inputs["w_in"]),
        "qk_gain_q": f(inputs["qk_gain_q"]), "qk_gain_k": f(inputs["qk_gain_k"]),
        "rpb_g": _rpb_gather(f(inputs["rpb"])),
        "conv_wT": np.ascontiguousarray(f(inputs["conv_w"]).transpose(0, 2, 1)),
        "a_log": f(inputs["a_log"]).reshape(2, 8), "dt_bias": f(inputs["dt_bias"]).reshape(2, 8),
        "dn_norm_w": f(inputs["dn_norm_w"]), "w_out": f(inputs["w_out"]),
        "c_ident": ident, "c_masks": masks, "c_navalid": _na_valid(),
    }
    x = f(inputs["x"])
    maps = []
    for i in range(n_cores):
        m = dict(shared)
        m["x"] = np.ascontiguousarray(x[i * n_seq:(i + 1) * n_seq])
        maps.append(m)
    return maps


def kernel(**inputs):
    nc, c = build_program()
    in_maps = make_in_maps(inputs)
    res = run_bass_kernel_spmd(nc, in_maps, core_ids=list(range(NCORES)))
    out = np.concatenate([np.asarray(r["out"]) for r in res.results], axis=0)
    return out.astype(np.float32)
```

```python
import numpy as np
from contextlib import ExitStack
import concourse.bass as bass
import concourse.mybir as mybir
from concourse.bass_utils import run_bass_kernel_spmd

F32 = mybir.dt.float32
BF16 = mybir.dt.bfloat16
ALU = mybir.AluOpType
AF = mybir.ActivationFunctionType
AX = mybir.AxisListType

D = 1024
L = 4096
DIN = 4112
NT = L // 128
EPS = 1e-6
NCORES = 8
SEQ_PER_CORE = 2


class Res:
    __slots__ = ("name", "w", "r")

    def __init__(self, name=""):
        self.name = name
        self.w = None
        self.r = []


class Sched:
    NRING = 8

    def __init__(self, nc):
        self.nc = nc
        self.E = {"pe": nc.tensor, "act": nc.scalar, "dve": nc.vector,
                  "pool": nc.gpsimd, "sp": nc.sync}
        self.sems = {}
        self.cnt = {}
        for e in self.E:
            self.sems[e] = nc.alloc_semaphore(name=f"s_{e}")
            self.cnt[e] = 0
        self.ring = {}
        self.ring_i = {}
        for q in ("sp", "pool", "act"):
            self.ring[q] = []
            for i in range(self.NRING):
                k = f"d_{q}{i}"
                self.sems[k] = nc.alloc_semaphore(name=k)
                self.cnt[k] = 0
                self.ring[q].append(k)
            self.ring_i[q] = 0
        self.seen = {e: {} for e in self.E}
        self.grp = None
        self.n_ins = 0
        self.n_wait = 0

    def _need(self, eng, reads, writes):
        need = {}

        def add(tok):
            if tok is None:
                return
            k, v = tok
            if k == eng and eng == "pe":
                return
            if need.get(k, 0) < v:
                need[k] = v
        for r in reads:
            add(r.w)
        for w in writes:
            add(w.w)
            for t in w.r:
                if t[0] == eng:
                    continue
                add(t)
        return need

    def _emit_waits(self, eng, need):
        seen = self.seen[eng]
        for k, v in need.items():
            if seen.get(k, 0) < v:
                if self.grp is not None and self.grp[0] == eng and k == eng and v >= self.grp[1][1]:
                    raise RuntimeError("dependency inside open group on same engine")
                self.E[eng].wait_ge(self.sems[k], v)
                seen[k] = v
                self.n_wait += 1

    def _mark(self, tok, reads, writes):
        for r in reads:
            r.r.append(tok)
        for w in writes:
            w.w = tok
            w.r = []

    def op(self, eng, fn, reads=(), writes=()):
        need = self._need(eng, reads, writes)
        self._emit_waits(eng, need)
        ins = fn(self.E[eng])
        self.n_ins += 1
        if self.grp is not None and self.grp[0] == eng:
            tok = self.grp[1]
            self.grp[2] = ins
        else:
            self.cnt[eng] += 1
            tok = (eng, self.cnt[eng])
            ins.then_inc(self.sems[eng], 1)
        self._mark(tok, reads, writes)
        return ins

    def group(self, eng):
        s = self

        class G:
            def __enter__(self_):
                assert s.grp is None
                s.cnt[eng] += 1
                s.grp = [eng, (eng, s.cnt[eng]), None]

            def __exit__(self_, *a):
                g = s.grp
                s.grp = None
                if a[0] is not None:
                    return False
                assert g[2] is not None
                g[2].then_inc(s.sems[eng], 1)
                return False
        return G()

    def dma(self, q, out, in_, reads=(), writes=(), **kw):
        ring = self.ring[q]
        i = self.ring_i[q]
        self.ring_i[q] = i + 1
        k = ring[i % self.NRING]
        need = self._need(q, reads, writes)
        if self.cnt[k] > 0 and need.get(k, 0) < self.cnt[k]:
            need[k] = self.cnt[k]
        self._emit_waits(q, need)
        ins = self.E[q].dma_start(out=out, in_=in_, **kw)
        self.cnt[k] += 16
        ins.then_inc(self.sems[k], 16)
        tok = (k, self.cnt[k])
        self._mark(tok, reads, writes)
        self.n_ins += 1
        return ins

    def barrier(self):
        for e in self.E:
            need = {k: v for k, v in self.cnt.items() if v > 0 and not (k == e and e == "pe")}
            self._emit_waits(e, need)

    def finish(self):
        need = {k: v for k, v in self.cnt.items() if v > 0}
        self._emit_waits("sp", need)


class Ctx:
    pass


_uid = [0]
CPARTS = "012"


def _u(name):
    _uid[0] += 1
    return f"{name}_{_uid[0]}"


def build_program(n_layers=2, n_seq=SEQ_PER_CORE, phases="ABCD", debug=False):
    nc = bass.Bass("TRN2", target_bir_lowering=False)
    s = Sched(nc)
    c = Ctx()
    c.nc, c.s = nc, s
    c.n_seq = n_seq

    def din(name, shape, dt=F32):
        return nc.dram_tensor(name, list(shape), dt, kind="ExternalInput").ap()

    def dscr(name, shape, dt=F32):
        return nc.dram_tensor(name, list(shape), dt, kind="ExternalOutput" if debug else "Internal").ap()

    c.x = din("x", [n_seq, L, D])
    c.norm_w = din("norm_w", [2, D])
    c.w_in = din("w_in", [2, D, DIN])
    c.gq = din("qk_gain_q", [2, 64])
    c.gk = din("qk_gain_k", [2, 64])
    c.rpbg = din("rpb_g", [2, 8, 128, 1024])
    c.conv_w = din("conv_wT", [2, 1536, 5])
    c.a_log = din("a_log", [2, 8])
    c.dt_bias = din("dt_bias", [2, 8])
    c.dn_w = din("dn_norm_w", [2, 128])
    c.w_out = din("w_out", [2, D, D])
    c.c_ident = din("c_ident", [128, 128])
    c.c_masks = din("c_masks", [128, 16, 128])
    c.c_navalid = din("c_navalid", [128, 1024])
    c.out = nc.dram_tensor("out", [n_seq, L, D], F32, kind="ExternalOutput").ap()

    c.xmid = dscr("xmid", [n_seq, L, D])
    c.cin = dscr("cin", [n_seq, 1536, L + 4], BF16)
    c.qT = dscr("qT", [n_seq, 512, L], BF16)
    c.kT = dscr("kT", [n_seq, 512, L], BF16)
    c.v1 = dscr("v1", [n_seq, L, 520], BF16)
    c.mix = dscr("mix", [n_seq, L, D], BF16)
    c.gates = dscr("gates", [n_seq, L, 24])
    c.zst = dscr("zst", [n_seq, NT, 2, 128, 2048], BF16)
    c.ktd = dscr("ktd", [n_seq, L, 512], BF16)
    c.vtd = dscr("vtd", [n_seq, L, 512], BF16)
    c.ost = dscr("ost", [n_seq, 2, L, 512])
    c.r_xmid = [Res() for _ in range(n_seq)]
    c.r_cin = [Res() for _ in range(n_seq)]
    c.r_qT = [Res() for _ in range(n_seq)]
    c.r_kT = [Res() for _ in range(n_seq)]
    c.r_v1 = [Res() for _ in range(n_seq)]
    c.r_mix = [Res() for _ in range(n_seq)]
    c.r_gates = [Res() for _ in range(n_seq)]
    c.r_zst = [Res() for _ in range(n_seq)]
    c.r_ktd = [Res() for _ in range(n_seq)]
    c.r_vtd = [Res() for _ in range(n_seq)]
    c.r_ost = [Res() for _ in range(n_seq)]
    c.dbg = {}
    if debug:
        c.dbg_out = {}

    with ExitStack() as es0:
        c.pb = [es0.enter_context(nc.psum_tensor(f"pb{i}", [128, 512], F32)) for i in range(8)]
        c.pr = [Res(f"pb{i}") for i in range(8)]
        c.pi = 0
        c.ring_banks = list(range(8))

        def nextbank():
            i = c.ring_banks[c.pi % len(c.ring_banks)]
            c.pi += 1
            return c.pb[i], c.pr[i]
        c.nextbank = nextbank
        c.free_banks = list(range(8))

        def getbank():
            while not c.free_banks:
                yield
            i = c.free_banks.pop(0)
            return c.pb[i], c.pr[i]

        def relbank(pp):
            i = [k for k in range(8) if c.pb[k] is pp][0]
            assert i not in c.free_banks
            c.free_banks.append(i)
        c.getbank, c.relbank = getbank, relbank

        c.ident_f = es0.enter_context(nc.sbuf_tensor("ident_f", [128, 128], F32)); c.r_identf = Res()
        c.ident = es0.enter_context(nc.sbuf_tensor("ident", [128, 128], BF16)); c.r_ident = Res()
        s.dma("sp", c.ident_f[:], c.c_ident, writes=[c.r_identf])
        s.op("dve", lambda e: e.tensor_copy(c.ident[:], c.ident_f[:]), [c.r_identf], [c.r_ident])

        for l in range(n_layers):
            src = c.x if l == 0 else c.xmid
            dst = c.out if l == n_layers - 1 else c.xmid
            if "A" in phases:
                phase_A(c, l, src)
                s.barrier()
            if "B" in phases:
                phase_B(c, l)
                s.barrier()
            if "C" in phases:
                phase_C(c, l)
                s.barrier()
            if "D" in phases:
                phase_D(c, l, src, dst)
                s.barrier()
        s.finish()
    c.stats = (s.n_ins, s.n_wait)
    return nc, c


def phase_A(c, l, src):
    nc, s = c.nc, c.s
    with ExitStack() as es:
        def sb(name, shape, dt):
            return es.enter_context(nc.sbuf_tensor(_u(name), shape, dt))
        Wb = sb("Wb", [128, 8, DIN], BF16); r_Wb = Res()
        nw = sb("nw", [128, 8], F32); r_nw = Res()
        stage = [sb(f"wstage{i}", [128, 8, 512], F32) for i in range(2)]
        r_stage = [Res() for _ in range(2)]
        s.dma("sp", nw[:], c.norm_w[l].rearrange("(t p) -> p t", p=128), writes=[r_nw], allow_slow_non_contiguous=True)
        wv = c.w_in[l].rearrange("(t p) e -> p t e", p=128)
        chunks = [(i * 512, 512) for i in range(8)] + [(4096, 16)]
        for ci, (c0, cw) in enumerate(chunks):
            st, rs = stage[ci % 2], r_stage[ci % 2]
            s.dma("sp" if ci % 2 == 0 else "act", st[:, :, 0:cw], wv[:, :, c0:c0 + cw], writes=[rs])
            eng = "dve" if ci % 2 == 0 else "pool"
            s.op(eng, lambda e, st=st, c0=c0, cw=cw: e.tensor_tensor(
                out=Wb[:, :, c0:c0 + cw], in0=st[:, :, 0:cw], in1=nw[:].unsqueeze(2).to_broadcast([128, 8, cw]), op=ALU.mult),
                [rs, r_nw], [r_Wb])
        gqk = sb("gqk", [128, 2], F32); r_gqk = Res()
        s.dma("sp", gqk[0:64, 0:1], c.gq[l].rearrange("(p o) -> p o", o=1), writes=[r_gqk])
        s.dma("sp", gqk[64:128, 0:1], c.gq[l].rearrange("(p o) -> p o", o=1), writes=[r_gqk])
        s.dma("sp", gqk[0:64, 1:2], c.gk[l].rearrange("(p o) -> p o", o=1), writes=[r_gqk])
        s.dma("sp", gqk[64:128, 1:2], c.gk[l].rearrange("(p o) -> p o", o=1), writes=[r_gqk])
        gprod = sb("gprod", [128, 1], F32); r_gprod = Res()
        s.op("dve", lambda e: e.tensor_tensor(out=gprod[:], in0=gqk[:, 0:1], in1=gqk[:, 1:2], op=ALU.mult), [r_gqk], [r_gprod])
        bones = sb("bones", [128, 128], BF16); r_bones = Res()
        mk = sb("mk_a", [128, 128], F32); r_mk = Res()
        s.dma("sp", mk[:], c.c_masks[:, 0, :], writes=[r_mk])
        s.op("dve", lambda e: e.tensor_copy(bones[:], mk[:]), [r_mk], [r_bones])
        dtb = sb("dtb", [128, 8], F32); r_dtb = Res()
        nega = sb("nega", [128, 8], F32); r_nega = Res()
        s.dma("sp", dtb[:], c.dt_bias[l].partition_broadcast(128), writes=[r_dtb])
        s.dma("sp", nega[:], c.a_log[l].partition_broadcast(128), writes=[r_nega])
        s.op("act", lambda e: e.activation(out=nega[:], in_=nega[:], func=AF.Exp), [r_nega], [r_nega])
        s.op("dve", lambda e: e.tensor_scalar(out=nega[:], in0=nega[:], scalar1=-1.0, scalar2=None, op0=ALU.mult), [r_nega], [r_nega])

        xt = [sb(f"xt{i}", [128, D], F32) for i in range(4)]; r_xt = [Res() for _ in range(4)]
        junk = sb("junkA", [128, D], BF16); r_junk = Res()
        ss = [sb(f"ssA{k}", [128, NT], F32) for k in range(2)]; r_ss = [Res() for _ in range(2)]
        lnv = sb("lnvA", [128, 2 * NT, 2], F32); r_lnv = [Res() for _ in range(2 * NT)]
        hb = [sb(f"hb{i}", [128, D], BF16) for i in range(4)]; r_hb = [Res() for _ in range(4)]
        hT4 = [sb(f"hT4_{i}", [128, 8, 512], BF16) for i in range(2)]; r_hT4 = [Res() for _ in range(2)]
        vst = [sb(f"vst{i}", [128, 8, 65], BF16) for i in range(2)]; r_vst = [Res() for _ in range(2)]
        gz = [sb(f"gz{i}", [128, D], BF16) for i in range(2)]; r_gz = [Res() for _ in range(2)]
        gstash = [sb(f"gstash{k}", [128, NT, 16], F32) for k in range(2)]; r_gst = [Res() for _ in range(2)]
        gproc = sb("gproc", [128, NT, 24], F32); r_gproc = Res()
        fm = [sb(f"fm{i}", [128, 512], BF16) for i in range(4)]; r_fm = [Res() for _ in range(4)]
        sq = [sb(f"sq{i}", [128, 512], BF16) for i in range(2)]; r_sq = [Res() for _ in range(2)]
        rn = [sb(f"rn{i}", [128, 512], F32) for i in range(2)]; r_rn = [Res() for _ in range(2)]
        for i in range(2):
            s.op("pool", lambda e, i=i: e.memset(vst[i][:], 1.0), [], [r_vst[i]])
        cnt = {"fm": 0, "sq": 0, "ev": 0}

        def prepA(sq_i, st):
            sk = sq_i % 2
            h4, r_h4 = hT4[st % 2], r_hT4[st % 2]
            if st == 0:
                s.op("dve", lambda e: e.memset(ss[sk][:], 0.0), [r_ss[sk]], [r_ss[sk]])
            for ti in range(4):
                i = st * 4 + ti
                b = i % 4
                lv = lnv[:, sk * NT + i, :]
                r_lv = r_lnv[sk * NT + i]
                s.dma("sp", xt[b][:], src[sq_i, i * 128:(i + 1) * 128, :], reads=[c.r_xmid[sq_i]], writes=[r_xt[b]])
                yield
                s.op("act", lambda e: e.activation(out=junk[:], in_=xt[b][:], func=AF.Square, accum_out=ss[sk][:, i:i + 1]),
                     [r_xt[b], r_ss[sk]], [r_junk, r_ss[sk]])
                yield
                s.op("act", lambda e: e.activation(out=lv[:, 0:1], in_=ss[sk][:, i:i + 1], func=AF.Ln, bias=EPS, scale=1.0 / D),
                     [r_ss[sk], r_lv], [r_lv])
                yield
                s.op("act", lambda e: e.activation(out=lv[:, 1:2], in_=lv[:, 0:1], func=AF.Exp, scale=-0.5), [r_lv], [r_lv])
                yield
                s.op("dve", lambda e: e.tensor_scalar(out=hb[b][:], in0=xt[b][:], scalar1=lv[:, 1:2], scalar2=None, op0=ALU.mult),
                     [r_xt[b], r_lv], [r_hb[b]])
                yield
            for ti in range(4):
                i = st * 4 + ti
                b = i % 4
                pt, r_pt = c.nextbank()
                ptb = pt[:].bitcast(BF16)
                with s.group("pe"):
                    for dt in range(8):
                        s.op("pe", lambda e, dt=dt: e.transpose(ptb[:, dt * 128:(dt + 1) * 128], hb[b][:, dt * 128:(dt + 1) * 128], c.ident[:]),
                             [r_hb[b], c.r_ident], [r_pt])
                yield
                s.op("dve", lambda e: e.tensor_copy(h4[:, :, ti * 128:(ti + 1) * 128], ptb.rearrange("p (a b) -> p a b", a=8)), [r_pt], [r_h4])
                yield

        def mainA(sq_i, st):
            sk = sq_i % 2
            h4, r_h4 = hT4[st % 2], r_hT4[st % 2]
            for ti in range(4):
                i = st * 4 + ti
                vb = i % 2
                for (c0, cw, kind) in ((1024, 512, "v"), (1536, 512, "az"), (3584, 512, "dz"), (4096, 16, "g")):
                    pp, r_pp = c.nextbank()
                    with s.group("pe"):
                        for dt in range(8):
                            s.op("pe", lambda e, dt=dt: e.matmul(
                                pp[:, 0:cw], lhsT=h4[:, dt, ti * 128:(ti + 1) * 128], rhs=Wb[:, dt, c0:c0 + cw],
                                start=(dt == 0), stop=(dt == 7)), [r_h4, r_Wb], [r_pp])
                    yield
                    if kind == "v":
                        s.op("act", lambda e: e.copy(vst[vb][:, :, 0:64], pp[:].rearrange("p (h d) -> p h d", h=8)), [r_pp], [r_vst[vb]])
                        s.dma("pool", c.v1[sq_i, i * 128:(i + 1) * 128, :], vst[vb][:].rearrange("p h d -> p (h d)"),
                              reads=[r_vst[vb]], writes=[c.r_v1[sq_i]])
                    elif kind == "az":
                        s.op("act", lambda e: e.activation(out=gz[vb][:, 0:512], in_=pp[:], func=AF.Silu), [r_pp], [r_gz[vb]])
                    elif kind == "dz":
                        s.op("act", lambda e: e.activation(out=gz[vb][:, 512:1024], in_=pp[:], func=AF.Silu), [r_pp], [r_gz[vb]])
                        s.dma("pool", c.mix[sq_i, i * 128:(i + 1) * 128, :], gz[vb][:], reads=[r_gz[vb]], writes=[c.r_mix[sq_i]])
                    else:
                        s.op("dve", lambda e: e.tensor_copy(gstash[sk][:, i, :], pp[:, 0:16]), [r_pp], [r_gst[sk]])
                    yield
            for et in range(20):
                if et < 8:
                    c0 = et * 128
                else:
                    c0 = 2048 + (et - 8) * 128
                pp, r_pp = c.nextbank()
                with s.group("pe"):
                    for dt in range(8):
                        s.op("pe", lambda e, dt=dt: e.matmul(
                            pp[:], lhsT=Wb[:, dt, c0:c0 + 128], rhs=h4[:, dt, :], start=(dt == 0), stop=(dt == 7)),
                            [r_h4, r_Wb], [r_pp])
                yield
                fb = cnt["fm"] % 4
                cnt["fm"] += 1
                if et >= 8:
                    eng = "act" if cnt["ev"] % 2 == 0 else "dve"
                    cnt["ev"] += 1
                    if eng == "act":
                        s.op("act", lambda e: e.copy(fm[fb][:], pp[:]), [r_pp], [r_fm[fb]])
                    else:
                        s.op("dve", lambda e: e.tensor_copy(fm[fb][:], pp[:]), [r_pp], [r_fm[fb]])
                    ch0 = (et - 8) * 128
                    s.dma("pool", c.cin[sq_i, ch0:ch0 + 128, 2 + st * 512:2 + (st + 1) * 512], fm[fb][:],
                          reads=[r_fm[fb]], writes=[c.r_cin[sq_i]])
                    yield
                else:
                    qb = cnt["sq"] % 2
                    cnt["sq"] += 1
                    s.op("act", lambda e: e.activation(out=sq[qb][:], in_=pp[:], func=AF.Square), [r_pp], [r_sq[qb]])
                    yield
                    p2, r_p2 = c.nextbank()
                    s.op("pe", lambda e: e.matmul(p2[:], lhsT=bones[:], rhs=sq[qb][:], start=True, stop=True), [r_bones, r_sq[qb]], [r_p2])
                    yield
                    s.op("act", lambda e: e.activation(out=rn[qb][:], in_=p2[:], func=AF.Ln, bias=EPS, scale=1.0 / 64), [r_p2], [r_rn[qb]])
                    yield
                    s.op("act", lambda e: e.activation(out=rn[qb][:], in_=rn[qb][:], func=AF.Exp, scale=-0.5), [r_rn[qb]], [r_rn[qb]])
                    yield
                    if et < 4:
                        s.op("dve", lambda e: e.scalar_tensor_tensor(out=fm[fb][:], in0=pp[:], scalar=0.125, in1=rn[qb][:], op0=ALU.mult, op1=ALU.mult),
                             [r_pp, r_rn[qb]], [r_fm[fb]])
                        s.dma("pool", c.qT[sq_i, et * 128:(et + 1) * 128, st * 512:(st + 1) * 512], fm[fb][:], reads=[r_fm[fb]], writes=[c.r_qT[sq_i]])
                    else:
                        s.op("dve", lambda e: e.scalar_tensor_tensor(out=fm[fb][:], in0=pp[:], scalar=gprod[:, 0:1], in1=rn[qb][:], op0=ALU.mult, op1=ALU.mult),
                             [r_pp, r_rn[qb], r_gprod], [r_fm[fb]])
                        s.dma("pool", c.kT[sq_i, (et - 4) * 128:(et - 3) * 128, st * 512:(st + 1) * 512], fm[fb][:], reads=[r_fm[fb]], writes=[c.r_kT[sq_i]])
                    yield
            if st == 7:
                gs = gstash[sk]
                gv = gs[:].rearrange("p t (d k h) -> p t d k h", d=2, k=2)
                bview = gv[:, :, :, 0, :]
                aview = gv[:, :, :, 1, :]
                beta_o = gproc[:, :, 0:8].rearrange("p t (d h) -> p t d h", d=2)
                g_o = gproc[:, :, 8:16].rearrange("p t (d h) -> p t d h", d=2)
                lnb_o = gproc[:, :, 16:24].rearrange("p t (d h) -> p t d h", d=2)
                s.op("act", lambda e: e.activation(out=beta_o, in_=bview, func=AF.Sigmoid), [r_gst[sk], r_gproc], [r_gproc])
                yield
                s.op("act", lambda e: e.activation(out=lnb_o, in_=beta_o, func=AF.Ln), [r_gproc], [r_gproc])
                yield
                s.op("dve", lambda e: e.tensor_tensor(out=g_o, in0=aview, in1=dtb[:].rearrange("p (d h) -> p d h", d=2).unsqueeze(1).to_broadcast([128, NT, 2, 4]),
                                                      op=ALU.add), [r_gst[sk], r_dtb, r_gproc], [r_gproc])
                yield
                s.op("dve", lambda e: e.tensor_scalar(out=gproc[:, :, 8:16], in0=gproc[:, :, 8:16], scalar1=80.0, scalar2=None, op0=ALU.min), [r_gproc], [r_gproc])
                yield
                s.op("act", lambda e: e.activation(out=gproc[:, :, 8:16], in_=gproc[:, :, 8:16], func=AF.Exp), [r_gproc], [r_gproc])
                yield
                s.op("act", lambda e: e.activation(out=gproc[:, :, 8:16], in_=gproc[:, :, 8:16], func=AF.Ln, bias=1.0), [r_gproc], [r_gproc])
                yield
                s.op("dve", lambda e: e.tensor_tensor(out=gproc[:, :, 8:16], in0=gproc[:, :, 8:16], in1=nega[:].unsqueeze(1).to_broadcast([128, NT, 8]),
                                                      op=ALU.mult), [r_gproc, r_nega], [r_gproc])
                yield
                gdst = c.gates[sq_i].rearrange("(t p) c -> p t c", p=128)
                for qd in range(4):
                    s.dma("sp", gdst[:, qd * 8:(qd + 1) * 8, :], gproc[:, qd * 8:(qd + 1) * 8, :], reads=[r_gproc], writes=[c.r_gates[sq_i]])

        seqs = [(sq_i, st) for sq_i in range(c.n_seq) for st in range(8)]
        _interleave([prepA(*seqs[0])])
        for k, (sq_i, st) in enumerate(seqs):
            gens = [mainA(sq_i, st)]
            if k + 1 < len(seqs):
                gens.append(prepA(*seqs[k + 1]))
            _interleave(gens)


def phase_D(c, l, src, dst):
    nc, s = c.nc, c.s
    with ExitStack() as es:
        def sb(name, shape, dt):
            return es.enter_context(nc.sbuf_tensor(_u(name), shape, dt))
        Wo = sb("Wo", [128, 8, D], BF16); r_Wo = Res()
        stage = [sb(f"wostage{i}", [128, 8, 512], F32) for i in range(2)]
        r_stage = [Res() for _ in range(2)]
        wv = c.w_out[l].rearrange("(t p) e -> p t e", p=128)
        for ci in range(2):
            s.dma("sp", stage[ci][:], wv[:, :, ci * 512:(ci + 1) * 512], writes=[r_stage[ci]])
            s.op("dve" if ci == 0 else "pool", lambda e, ci=ci: e.tensor_copy(Wo[:, :, ci * 512:(ci + 1) * 512], stage[ci][:]), [r_stage[ci]], [r_Wo])
        ND = 3
        xt = [sb(f"xtD{i}", [128, D], F32) for i in range(ND)]; r_xt = [Res() for _ in range(ND)]
        mt = [sb(f"mtD{i}", [128, D], BF16) for i in range(ND)]; r_mt = [Res() for _ in range(ND)]
        mT = [sb(f"mTD{i}", [128, 8, 128], BF16) for i in range(ND)]; r_mT = [Res() for _ in range(ND)]
        yo = [sb(f"yoD{i}", [128, D], F32) for i in range(ND)]; r_yo = [Res() for _ in range(ND)]

        def d_tile(u, b):
            sq_i, i = u
            rows = slice(i * 128, (i + 1) * 128)
            s.dma("sp", xt[b][:], src[sq_i, rows, :], reads=[c.r_xmid[sq_i]], writes=[r_xt[b]])
            s.dma("act", mt[b][:], c.mix[sq_i, rows, :], reads=[c.r_mix[sq_i]], writes=[r_mt[b]])
            yield
            pt, r_pt = c.nextbank()
            ptb = pt[:].bitcast(BF16)
            with s.group("pe"):
                for et in range(8):
                    s.op("pe", lambda e, et=et: e.transpose(ptb[:, et * 128:(et + 1) * 128], mt[b][:, et * 128:(et + 1) * 128], c.ident[:]),
                         [r_mt[b], c.r_ident], [r_pt])
            yield
            s.op("act", lambda e: e.copy(mT[b][:], ptb.rearrange("p (a b) -> p a b", a=8)), [r_pt], [r_mT[b]])
            yield
            for ch in range(2):
                pp, r_pp = c.nextbank()
                with s.group("pe"):
                    for et in range(8):
                        s.op("pe", lambda e, et=et: e.matmul(
                            pp[:], lhsT=mT[b][:, et, :], rhs=Wo[:, et, ch * 512:(ch + 1) * 512], start=(et == 0), stop=(et == 7)),
                            [r_mT[b], r_Wo], [r_pp])
                yield
                s.op("dve", lambda e: e.tensor_tensor(
                    out=yo[b][:, ch * 512:(ch + 1) * 512], in0=pp[:], in1=xt[b][:, ch * 512:(ch + 1) * 512], op=ALU.add),
                    [r_pp, r_xt[b]], [r_yo[b]])
                yield
            wres = [c.r_xmid[sq_i]] if dst is c.xmid else []
            s.dma("pool", dst[sq_i, rows, :], yo[b][:], reads=[r_yo[b]], writes=wres)

        _rolling([(sq_i, i) for sq_i in range(c.n_seq) for i in range(NT)], d_tile, ND)


def _rs(r):
    return min(max(r - 4, 0), 56)


def phase_B(c, l):
    nc, s = c.nc, c.s
    Rj = {}
    for j in range(32):
        rows = [r for r in range(64) if _rs(r) <= 2 * j + 1 and _rs(r) + 7 >= 2 * j]
        Rj[j] = (rows[0], rows[-1])
    c.ring_banks = [0, 1, 2, 3, 4, 5]
    with ExitStack() as es:
        def sb(name, shape, dt):
            return es.enter_context(nc.sbuf_tensor(_u(name), shape, dt))
        Mtab = sb("Mtab", [128, 8, 1024], BF16); r_Mtab = Res()
        with ExitStack() as es2:
            nav = es2.enter_context(nc.sbuf_tensor(_u("nav"), [128, 1024], F32)); r_nav = Res()
            stg = [es2.enter_context(nc.sbuf_tensor(_u(f"rstg{i}"), [128, 1024], F32)) for i in range(2)]
            r_stg = [Res() for _ in range(2)]
            s.dma("sp", nav[:], c.c_navalid, writes=[r_nav])
            for h in range(8):
                b = h % 2
                s.dma("sp", stg[b][:], c.rpbg[l, h], writes=[r_stg[b]])
                s.op("act", lambda e, b=b: e.activation(out=stg[b][:], in_=stg[b][:], func=AF.Exp), [r_stg[b]], [r_stg[b]])
                s.op("dve", lambda e, b=b, h=h: e.tensor_tensor(out=Mtab[:, h, :], in0=stg[b][:], in1=nav[:], op=ALU.mult),
                     [r_stg[b], r_nav], [r_Mtab])
            s.barrier()
        V1 = sb("V1all", [128, NT, 520], BF16); r_V1 = Res()
        mixA = sb("mixA", [128, NT, 512], BF16); r_mixA = Res()
        QT2 = [sb(f"QT2_{i}", [128, L], BF16) for i in range(2)]; r_QT2 = [Res() for _ in range(2)]
        KT2 = [sb(f"KT2_{i}", [128, L], BF16) for i in range(2)]; r_KT2 = [Res() for _ in range(2)]
        NSL = 7
        Es = [[sb(f"E{hh}_{k}", [128, 768], BF16) for k in range(NSL)] for hh in range(2)]
        r_Es = [[Res() for k in range(NSL)] for hh in range(2)]
        rdl = [sb(f"rdB{k}", [128, 2], F32) for k in range(2)]; r_rdl = [Res() for _ in range(2)]
        attl = [sb(f"attB{k}", [128, 2, 64], F32) for k in range(2)]; r_attl = [Res() for _ in range(2)]
        nq = 0
        for sq_i in range(c.n_seq):
            v1v = c.v1[sq_i].rearrange("(j p) c -> p j c", p=128)
            mxv = c.mix[sq_i].rearrange("(j p) c -> p j c", p=128)
            for qd in range(4):
                s.dma("sp", V1[:, qd * 8:(qd + 1) * 8, :], v1v[:, qd * 8:(qd + 1) * 8, :], reads=[c.r_v1[sq_i]], writes=[r_V1])
                s.dma("act", mixA[:, qd * 8:(qd + 1) * 8, :], mxv[:, qd * 8:(qd + 1) * 8, 0:512], reads=[c.r_mix[sq_i]], writes=[r_mixA])
            qbs = {}
            for hp in range(4):
                qbs[hp] = nq % 2
                nq += 1

            def qk_gen(hp, j):
                qb = qbs[hp]
                if j == 0:
                    s.dma("sp", QT2[qb][:], c.qT[sq_i, hp * 128:(hp + 1) * 128, :], reads=[c.r_qT[sq_i]], writes=[r_QT2[qb]])
                    s.dma("sp", KT2[qb][:], c.kT[sq_i, hp * 128:(hp + 1) * 128, :], reads=[c.r_kT[sq_i]], writes=[r_KT2[qb]])
                lo, hi = Rj[j]
                ncols = (hi - lo + 1) * 64
                t0 = lo - 2 * j + 7
                sl = (hp * 32 + j) % NSL
                for hh in range(2):
                    head = 2 * hp + hh
                    ph = slice(hh * 64, hh * 64 + 64)
                    c0 = 0
                    while c0 < ncols:
                        cw = min(512, ncols - c0)
                        pp, r_pp = c.nextbank()
                        s.op("pe", lambda e: e.matmul(
                            pp[:, 0:cw], lhsT=KT2[qb][ph, 128 * j:128 * j + 128], rhs=QT2[qb][ph, 64 * lo + c0:64 * lo + c0 + cw],
                            start=True, stop=True), [r_KT2[qb], r_QT2[qb]], [r_pp])
                        yield
                        s.op("act", lambda e: e.activation(out=Es[hh][sl][:, c0:c0 + cw], in_=pp[:, 0:cw], func=AF.Exp), [r_pp], [r_Es[hh][sl]])
                        yield
                        eng = "dve" if (hh == 0 or (j % 3 == 0)) else "pool"
                        s.op(eng, lambda e: e.tensor_tensor(
                            out=Es[hh][sl][:, c0:c0 + cw], in0=Es[hh][sl][:, c0:c0 + cw],
                            in1=Mtab[:, head, t0 * 64 + c0:t0 * 64 + c0 + cw], op=ALU.mult), [r_Es[hh][sl], r_Mtab], [r_Es[hh][sl]])
                        yield
                        c0 += cw
                    for r in range(lo, hi + 1):
                        for a in range(2):
                            kr = 2 * j + a
                            if not (_rs(r) <= kr <= _rs(r) + 7):
                                s.op("pool", lambda e: e.memset(Es[hh][sl][a * 64:(a + 1) * 64, (r - lo) * 64:(r - lo + 1) * 64], 0.0),
                                     [r_Es[hh][sl]], [r_Es[hh][sl]])
                                yield

            def pv_gen(hp, j):
                for i in range(32):
                    r0, r1 = 2 * i, 2 * i + 1
                    T0 = list(range(_rs(r0) // 2, (_rs(r0) + 7) // 2 + 1))
                    T1 = list(range(_rs(r1) // 2, (_rs(r1) + 7) // 2 + 1))
                    if max(T0[-1], T1[-1]) != j:
                        continue
                    ob, r_ob = c.pb[6 + (i % 2)], c.pr[6 + (i % 2)]
                    common = [jj for jj in T0 if jj in T1]
                    ex0 = [jj for jj in T0 if jj not in T1]
                    ex1 = [jj for jj in T1 if jj not in T0]
                    plan = [(jj, 0, 128) for jj in common] + [(jj, 0, 64) for jj in ex0] + [(jj, 64, 64) for jj in ex1]
                    for hh in range(2):
                        head = 2 * hp + hh
                        with s.group("pe"):
                            for pi, (jj, p0, m) in enumerate(plan):
                                r = r0 if p0 == 0 else r1
                                blk = (r - Rj[jj][0]) * 64
                                sl = (hp * 32 + jj) % NSL
                                s.op("pe", lambda e: e.matmul(
                                    ob[p0:p0 + m, hh * 66:hh * 66 + 65], lhsT=Es[hh][sl][:, blk:blk + m],
                                    rhs=V1[:, jj, head * 65:head * 65 + 65], start=(pi == 0), stop=(pi == len(plan) - 1)),
                                    [r_Es[hh][sl], r_V1], [r_ob])
                        yield
                    rd, r_rd, att, r_att = rdl[i % 2], r_rdl[i % 2], attl[i % 2], r_attl[i % 2]
                    ov = ob[:, 0:132].rearrange("p (h d) -> p h d", h=2)
                    s.op("dve", lambda e: e.reciprocal(out=rd[:], in_=ov[:, :, 64]), [r_ob, r_rd], [r_rd])
                    yield
                    s.op("dve", lambda e: e.tensor_tensor(out=att[:], in0=ov[:, :, 0:64], in1=rd[:].unsqueeze(2).to_broadcast([128, 2, 64]), op=ALU.mult),
                         [r_ob, r_rd, r_att], [r_att])
                    yield
                    s.op("pool", lambda e: e.tensor_tensor(
                        out=mixA[:, i, hp * 128:(hp + 1) * 128], in0=att[:].rearrange("p h d -> p (h d)"),
                        in1=mixA[:, i, hp * 128:(hp + 1) * 128], op=ALU.mult), [r_att, r_mixA], [r_mixA])
                    yield

            units = [(hp, j) for hp in range(4) for j in range(32)]
            LAG = 2
            for k in range(len(units) + LAG):
                gens = []
                if k - LAG >= 0:
                    gens.append(pv_gen(*units[k - LAG]))
                if k < len(units):
                    gens.append(qk_gen(*units[k]))
                _interleave(gens)
            for qd in range(4):
                s.dma("pool", mxv[:, qd * 8:(qd + 1) * 8, 0:512], mixA[:, qd * 8:(qd + 1) * 8, :], reads=[r_mixA], writes=[c.r_mix[sq_i]])
    c.ring_banks = list(range(8))


def phase_C(c, l):
    nc, s = c.nc, c.s
    NCH = 64
    with ExitStack() as es:
        def sb(name, shape, dt):
            return es.enter_context(nc.sbuf_tensor(_u(name), shape, dt))
        mskf = sb("mskf", [128, 7, 128], F32); r_mskf = Res()
        s.dma("sp", mskf[:], c.c_masks[:, 0:7, :], writes=[r_mskf])
        BD_ONES, BD_GT, BD_LE, BD_LT, BD_GE, CH_A, CH_B = [mskf[:, k, :] for k in range(7)]
        mske = sb("mske", [128, 4, 128], F32); r_mske = Res()
        s.dma("sp", mske[:], c.c_masks[:, 7:11, :], writes=[r_mske])
        Ma = mske[:, 0, :].rearrange("p (d c) -> p d c", d=2)
        Mb = mske[:, 1, :].rearrange("p (d c) -> p d c", d=2)
        Mc = mske[:, 2, :].rearrange("p (d c) -> p d c", d=2)
        EQ = mske[:, 3, 0:64]
        ones_b = sb("ones_b", [128, 128], BF16); r_onesb = Res()
        s.op("pool", lambda e: e.memset(ones_b[:], 1.0), [], [r_onesb])
        dnw = sb("dnw_bc", [128, 128], F32); r_dnw = Res()
        s.dma("sp", dnw[:], c.dn_w[l].partition_broadcast(128), writes=[r_dnw])
        cw = sb("cwC", [128, 12, 5], F32); r_cw = Res()
        s.dma("sp", cw[:], c.conv_w[l].rearrange("(t p) j -> p t j", p=128), writes=[r_cw])

        kqT = sb("kqT", [128, 4, NCH, 128], BF16); r_kqT = Res()
        GT = sb("GTall", [128, NT, 24], F32); r_GT = Res()
        sc = sb("scC", [128, NT, 32], F32); r_scl = [Res() for _ in range(NT)]

        for sq_i in range(c.n_seq):
            gsrc = c.gates[sq_i].rearrange("(t p) c -> p t c", p=128)
            for qd in range(4):
                s.dma("sp", GT[:, qd * 8:(qd + 1) * 8, :], gsrc[:, qd * 8:(qd + 1) * 8, :], reads=[c.r_gates[sq_i]], writes=[r_GT])
            with ExitStack() as e0:
              if "0" in CPARTS:
                  def sb0(name, shape, dt):
                      return e0.enter_context(nc.sbuf_tensor(_u(name), shape, dt))
                  dg = sb0("dgC", [128, 60, 128], BF16); r_dg = Res()
                  for t in range(12):
                      for jx in range(5):
                          if (t * 5 + jx) % 2 == 0:
                              s.op("dve", lambda e, t=t, jx=jx: e.tensor_scalar(out=dg[:, t * 5 + jx, :], in0=c.ident_f[:], scalar1=cw[:, t, jx:jx + 1],
                                                                                scalar2=None, op0=ALU.mult), [c.r_identf, r_cw], [r_dg])
                          else:
                              s.op("act", lambda e, t=t, jx=jx: e.mul(dg[:, t * 5 + jx, :], c.ident_f[:], cw[:, t, jx:jx + 1]), [c.r_identf, r_cw], [r_dg])
                  NS0 = 6
                  T0 = []
                  for k in range(NS0):
                      T = {}
                      for nm, shp, dt in (("xin", [128, 516], BF16), ("yb", [128, 512], F32), ("sqb", [128, 512], BF16), ("rnb", [128, 512], F32),
                                          ("knb", [128, 512], BF16), ("tst", [128, 4, 128], BF16)):
                          T[nm] = sb0(f"{nm}0_{k}", shp, dt)
                          T["r_" + nm] = Res()
                      T0.append(T)

                  def c0_unit(tb, h, qkv, T):
                      xin, yb, sqb, rnb, knb = T["xin"], T["yb"], T["sqb"], T["rnb"], T["knb"]
                      r_xin, r_yb, r_sqb, r_rnb, r_knb = T["r_xin"], T["r_yb"], T["r_sqb"], T["r_rnb"], T["r_knb"]
                      t = qkv * 4 + h
                      ch0 = t * 128
                      lo_c, hi_c = tb * 512, tb * 512 + 516
                      d0, d1 = 0, 516
                      if tb == 0:
                          lo_c, d0 = 2, 2
                          s.op("pool", lambda e: e.memset(xin[:, 0:2], 0.0), [r_xin], [r_xin])
                      if tb == 7:
                          hi_c, d1 = L + 2, 514
                          s.op("pool", lambda e: e.memset(xin[:, 514:516], 0.0), [r_xin], [r_xin])
                      s.dma("sp", xin[:, d0:d1], c.cin[sq_i, ch0:ch0 + 128, lo_c:hi_c], reads=[c.r_cin[sq_i]], writes=[r_xin])
                      yield
                      pp, r_pp = c.nextbank()
                      with s.group("pe"):
                          for jx in range(5):
                              s.op("pe", lambda e, jx=jx: e.matmul(pp[:], lhsT=dg[:, t * 5 + jx, :], rhs=xin[:, jx:jx + 512], start=(jx == 0), stop=(jx == 4)),
                                   [r_dg, r_xin], [r_pp])
                      yield
                      if qkv == 2:
                          s.op("act", lambda e: e.activation(out=knb[:], in_=pp[:], func=AF.Silu), [r_pp, r_knb], [r_knb])
                          yield
                          yield
                          yield
                          yield
                          yield
                          yield
                          pt, r_pt = c.nextbank()
                          ptb = pt[:].bitcast(BF16)
                          with s.group("pe"):
                              for k in range(4):
                                  s.op("pe", lambda e, k=k: e.transpose(ptb[:, k * 128:(k + 1) * 128], knb[:, k * 128:(k + 1) * 128], c.ident[:]),
                                       [r_knb, c.r_ident], [r_pt])
                          yield
                          s.op("dve", lambda e: e.tensor_copy(T["tst"][:], ptb[:, 0:512].rearrange("p (a b) -> p a b", a=4)), [r_pt, T["r_tst"]], [T["r_tst"]])
                          yield
                          s.dma("sp", c.vtd[sq_i].rearrange("(t p) (h d) -> p t h d", p=128, h=4)[:, tb * 4:(tb + 1) * 4, h, :], T["tst"][:],
                                reads=[T["r_tst"]], writes=[c.r_vtd[sq_i]])
                      else:
                          s.op("act", lambda e: e.activation(out=yb[:], in_=pp[:], func=AF.Silu), [r_pp, r_yb], [r_yb])
                          yield
                          s.op("dve", lambda e: e.tensor_tensor(out=sqb[:], in0=yb[:], in1=yb[:], op=ALU.mult), [r_yb, r_sqb], [r_sqb])
                          yield
                          p2, r_p2 = c.nextbank()
                          s.op("pe", lambda e: e.matmul(p2[:], lhsT=ones_b[:], rhs=sqb[:], start=True, stop=True), [r_onesb, r_sqb], [r_p2])
                          yield
                          s.op("act", lambda e: e.activation(out=rnb[:], in_=p2[:], func=AF.Ln, bias=EPS), [r_p2, r_rnb], [r_rnb])
                          yield
                          s.op("act", lambda e: e.activation(out=rnb[:], in_=rnb[:], func=AF.Exp, scale=-0.5), [r_rnb], [r_rnb])
                          yield
                          if qkv == 0:
                              s.op("dve", lambda e: e.scalar_tensor_tensor(
                                  out=kqT[:, h, tb * 8:(tb + 1) * 8, 64:128], in0=yb[:].rearrange("p (n c) -> p n c", n=8), scalar=128 ** -0.5,
                                  in1=rnb[:].rearrange("p (n c) -> p n c", n=8), op0=ALU.mult, op1=ALU.mult), [r_yb, r_rnb], [r_kqT])
                          else:
                              s.op("dve", lambda e: e.tensor_tensor(out=knb[:], in0=yb[:], in1=rnb[:], op=ALU.mult), [r_yb, r_rnb, r_knb], [r_knb])
                              yield
                              s.op("dve", lambda e: e.tensor_copy(kqT[:, h, tb * 8:(tb + 1) * 8, 0:64], knb[:].rearrange("p (n c) -> p n c", n=8)),
                                   [r_knb], [r_kqT])
                              pt, r_pt = c.nextbank()
                              ptb = pt[:].bitcast(BF16)
                              with s.group("pe"):
                                  for k in range(4):
                                      s.op("pe", lambda e, k=k: e.transpose(ptb[:, k * 128:(k + 1) * 128], knb[:, k * 128:(k + 1) * 128], c.ident[:]),
                                           [r_knb, c.r_ident], [r_pt])
                              yield
                              s.op("act", lambda e: e.copy(T["tst"][:], ptb[:, 0:512].rearrange("p (a b) -> p a b", a=4)), [r_pt, T["r_tst"]], [T["r_tst"]])
                              yield
                              s.dma("sp", c.ktd[sq_i].rearrange("(t p) (h d) -> p t h d", p=128, h=4)[:, tb * 4:(tb + 1) * 4, h, :], T["tst"][:],
                                    reads=[T["r_tst"]], writes=[c.r_ktd[sq_i]])

                  units = [(tb, h, qkv) for tb in range(8) for h in range(4) for qkv in range(3)]
                  _rolling(units, lambda u, k: c0_unit(*u, T0[k]), NS0)
                  s.barrier()
            with ExitStack() as e1:
              if "1" in CPARTS or "2" in CPARTS:
                  def sb1(name, shape, dt):
                      return e1.enter_context(nc.sbuf_tensor(_u(name), shape, dt))
                  NSLOT = 2
                  slots = []
                  for k in range(NSLOT):
                      T = {}
                      for nm, shp, dt in (("RG0", [128, 2, 4, 64], F32), ("RG1", [128, 2, 4, 64], F32), ("RB", [128, 2, 4, 64], F32),
                                          ("zc", [128, 32], F32), ("beg", [128, 8], F32),
                                          ("Nbd", [128, 8, 128], BF16), ("NTbd", [128, 8, 128], BF16), ("Pb", [128, 8, 128], BF16),
                                          ("Qb", [128, 8, 128], BF16), ("Zb0", [128, 8, 128], BF16), ("Zb1", [128, 8, 128], BF16),
                                          ("bv", [128, 8, 128], BF16), ("kgb", [128, 8, 128], BF16), ("rec", [128, 4096], BF16),
                                          ("kt", [128, 4, 128], BF16), ("vt", [128, 4, 128], BF16)):
                          T[nm] = sb1(f"{nm}_{k}", shp, dt)
                          T["r_" + nm] = Res()
                      s.op("pool", lambda e, T=T: e.memset(T["Nbd"][:], 0.0), [], [T["r_Nbd"]])
                      s.op("pool", lambda e, T=T: e.memset(T["NTbd"][:], 0.0), [], [T["r_NTbd"]])
                      slots.append(T)
                  bc = lambda ap: ap.unsqueeze(3).to_broadcast([128, 2, 4, 64])
                  bm = lambda m: m.unsqueeze(2).to_broadcast([128, 2, 4, 64])
                  LTm = (BD_GT, BD_LT)
                  LT2m = (BD_LE, BD_GE)
                  done_c1 = set()
                  done_o = [set(), set()]
                  r_zst_t = [Res() for _ in range(NT)]
                  r_ost_t = {(d, n): Res() for d in range(2) for n in range(NCH)}

                  def c1_tile(i, T):
                      RG0, RG1, RB, zc, beg = T["RG0"], T["RG1"], T["RB"], T["zc"], T["beg"]
                      r_RG0, r_RG1, r_RB, r_zc, r_beg = T["r_RG0"], T["r_RG1"], T["r_RB"], T["r_zc"], T["r_beg"]
                      Nbd, NTbd, r_Nbd, r_NTbd = T["Nbd"], T["NTbd"], T["r_Nbd"], T["r_NTbd"]
                      rec, r_rec = T["rec"], T["r_rec"]
                      r_sc = r_scl[i]
                      recv = rec[:].rearrange("p (d x) -> p d x", d=2)
                      kt, vt, r_kt, r_vt = T["kt"], T["vt"], T["r_kt"], T["r_vt"]
                      s.dma("sp", kt[:].rearrange("p h d -> p (h d)"), c.ktd[sq_i, i * 128:(i + 1) * 128, :], reads=[c.r_ktd[sq_i]], writes=[r_kt])
                      s.dma("sp", vt[:].rearrange("p h d -> p (h d)"), c.vtd[sq_i, i * 128:(i + 1) * 128, :], reads=[c.r_vtd[sq_i]], writes=[r_vt])
                      g8 = GT[:, i, 8:16]
                      gv = g8.rearrange("p (d h) -> p d h", d=2)
                      beta8 = GT[:, i, 0:8]
                      beta = beta8.rearrange("p (d h) -> p d h", d=2)
                      s.op("dve", lambda e: e.tensor_tensor(out=RG0[:], in0=bc(gv), in1=bm(Ma), op=ALU.mult), [r_GT, r_mske, r_RG0], [r_RG0])
                      s.op("pool", lambda e: e.tensor_tensor(out=RG1[:], in0=bc(gv), in1=bm(Mb), op=ALU.mult), [r_GT, r_mske, r_RG1], [r_RG1])
                      s.op("pool", lambda e: e.tensor_tensor(out=RB[:], in0=bc(beta), in1=EQ.unsqueeze(1).unsqueeze(1).to_broadcast([128, 2, 4, 64]), op=ALU.mult),
                           [r_GT, r_mske, r_RB], [r_RB])
                      yield
                      D1, r_D1, D2, r_D2, tI, r_tI = RG0, r_RG0, RG1, r_RG1, RB, r_RB
                      pX, r_pX = yield from c.getbank()
                      for d in range(2):
                          s.op("pe", lambda e, d=d: e.matmul(pX[:, d * 256:(d + 1) * 256], lhsT=LTm[d], rhs=RG0[:, d].rearrange("p h c -> p (h c)"),
                                                            start=True, stop=True), [r_mskf, r_RG0], [r_pX])
                      yield
                      s.op("act", lambda e: e.activation(out=D1[:].rearrange("p d h c -> p (d h c)"), in_=pX[:], func=AF.Exp), [r_pX, r_D1], [r_D1])
                      c.relbank(pX)
                      yield
                      pY, r_pY = yield from c.getbank()
                      for d in range(2):
                          s.op("pe", lambda e, d=d: e.matmul(pY[:, d * 256:(d + 1) * 256], lhsT=LT2m[d], rhs=RG1[:, d].rearrange("p h c -> p (h c)"),
                                                            start=True, stop=True), [r_mskf, r_RG1], [r_pY])
                      yield
                      s.op("act", lambda e: e.activation(out=D2[:].rearrange("p d h c -> p (d h c)"), in_=pY[:], func=AF.Exp), [r_pY, r_D2], [r_D2])
                      c.relbank(pY)
                      yield
                      pZ, r_pZ = yield from c.getbank()
                      with s.group("pe"):
                          s.op("pe", lambda e: e.matmul(pZ[:, 0:4], lhsT=BD_LE, rhs=g8[:, 0:4], start=True, stop=True), [r_mskf, r_GT], [r_pZ])
                          s.op("pe", lambda e: e.matmul(pZ[:, 4:8], lhsT=BD_GE, rhs=g8[:, 4:8], start=True, stop=True), [r_mskf, r_GT], [r_pZ])
                          s.op("pe", lambda e: e.matmul(pZ[:, 8:16], lhsT=BD_ONES, rhs=g8, start=True, stop=True), [r_mskf, r_GT], [r_pZ])
                          s.op("pe", lambda e: e.matmul(pZ[:, 16:24], lhsT=CH_A, rhs=g8, start=True, stop=True), [r_mskf, r_GT], [r_pZ])
                          s.op("pe", lambda e: e.matmul(pZ[:, 24:32], lhsT=CH_B, rhs=g8, start=True, stop=True), [r_mskf, r_GT], [r_pZ])
                      yield
                      s.op("act", lambda e: e.copy(zc[:], pZ[:, 0:32]), [r_pZ, r_zc], [r_zc])
                      c.relbank(pZ)
                      yield
                      s.op("dve", lambda e: e.tensor_tensor(out=zc[:, 8:16], in0=zc[:, 8:16], in1=zc[:, 0:8], op=ALU.subtract), [r_zc], [r_zc])
                      yield
                      s.op("act", lambda e: e.activation(out=sc[:, i, :], in_=zc[:], func=AF.Exp), [r_zc, r_sc], [r_sc])
                      yield
                      s.op("dve", lambda e: e.tensor_tensor(out=beg[:], in0=beta8, in1=sc[:, i, 0:8], op=ALU.mult), [r_GT, r_sc, r_beg], [r_beg])
                      s.op("pool", lambda e: e.tensor_tensor(out=recv[:, :, 1280:1536].rearrange("p d (h c) -> p d h c", h=4),
                                                             in0=bc(sc[:, i, 0:8].rearrange("p (d h) -> p d h", d=2)),
                                                             in1=EQ.unsqueeze(1).unsqueeze(1).to_broadcast([128, 2, 4, 64]), op=ALU.mult),
                           [r_sc, r_mske, r_rec], [r_rec])
                      yield
                      pB, r_pB = yield from c.getbank()
                      s.op("pe", lambda e: e.matmul(pB[:], lhsT=BD_ONES, rhs=RB[:].rearrange("p d h c -> p (d h c)"), start=True, stop=True), [r_mskf, r_RB], [r_pB])
                      yield
                      pW, r_pW = yield from c.getbank()
                      with s.group("pe"):
                          for h in range(4):
                              for a in range(2):
                                  n = 2 * i + a
                                  s.op("pe", lambda e, h=h, a=a, n=n: e.matmul(pW[a * 64:(a + 1) * 64, h * 128:(h + 1) * 128], lhsT=kqT[:, h, n, 0:64],
                                                                             rhs=kqT[:, h, n, :], start=True, stop=True), [r_kqT], [r_pW])
                      yield
                      Wv = pW[:].rearrange("p (h x) -> p h x", h=4)
                      KK = Wv[:, :, 0:64].unsqueeze(1).to_broadcast([128, 2, 4, 64])
                      KQ = Wv[:, :, 64:128].unsqueeze(1).to_broadcast([128, 2, 4, 64])
                      s.op("dve", lambda e: e.tensor_tensor(out=tI[:], in0=D1[:], in1=KQ, op=ALU.mult), [r_D1, r_pW, r_tI], [r_tI])
                      yield
                      s.op("dve", lambda e: e.tensor_tensor(out=D1[:], in0=D1[:], in1=KK, op=ALU.mult), [r_D1, r_pW], [r_D1])
                      yield
                      s.op("dve", lambda e: e.tensor_tensor(out=D2[:], in0=D2[:], in1=KK, op=ALU.mult), [r_D2, r_pW], [r_D2])
                      c.relbank(pW)
                      yield
                      s.op("dve", lambda e: e.tensor_tensor(out=D1[:], in0=D1[:], in1=pB[:].rearrange("p (d h c) -> p d h c", d=2, h=4), op=ALU.mult), [r_D1, r_pB], [r_D1])
                      c.relbank(pB)
                      yield
                      s.op("pool", lambda e: e.tensor_tensor(out=D2[:], in0=D2[:], in1=bc(beta), op=ALU.mult), [r_D2, r_GT], [r_D2])
                      yield
                      s.op("pool", lambda e: e.tensor_tensor(out=recv[:, :, 1024:1280].rearrange("p d (h c) -> p d h c", h=4), in0=tI[:], in1=bm(Ma), op=ALU.mult),
                           [r_tI, r_mske, r_rec], [r_rec])
                      for a in range(2):
                          pa = slice(a * 64, (a + 1) * 64)
                          ca = slice(a * 64, (a + 1) * 64)
                          s.op("dve" if a == 0 else "pool", lambda e, pa=pa, ca=ca: e.tensor_tensor(
                              out=Nbd[pa, :, ca].rearrange("p (d h) c -> p d h c", d=2), in0=D1[pa], in1=bm(Mc)[pa], op=ALU.mult),
                              [r_D1, r_mske, r_Nbd], [r_Nbd])
                          s.op("pool" if a == 0 else "dve", lambda e, pa=pa, ca=ca: e.tensor_tensor(
                              out=NTbd[pa, :, ca].rearrange("p (d h) c -> p d h c", d=2), in0=D2[pa], in1=bm(Mb)[pa], op=ALU.mult),
                              [r_D2, r_mske, r_NTbd], [r_NTbd])
                          yield
                      Zb = [T["Zb0"], T["Zb1"]]
                      r_Zb = [T["r_Zb0"], T["r_Zb1"]]
                      s.op("pool", lambda e: e.tensor_tensor(out=Zb[0][:], in0=c.ident[:].unsqueeze(1).to_broadcast([128, 8, 128]), in1=Nbd[:], op=ALU.subtract),
                           [c.r_ident, r_Nbd, r_Zb[0]], [r_Zb[0]])
                      bv, r_bv, kgb, r_kgb = T["bv"], T["r_bv"], T["kgb"], T["r_kgb"]
                      bc128 = lambda ap: ap.rearrange("p (d h) -> p d h", d=2).unsqueeze(3).to_broadcast([128, 2, 4, 128])
                      s.op("pool", lambda e: e.tensor_tensor(out=bv[:].rearrange("p (d h) x -> p d h x", d=2), in0=vt[:].unsqueeze(1).to_broadcast([128, 2, 4, 128]),
                                                             in1=bc128(beta8), op=ALU.mult), [r_vt, r_GT, r_bv], [r_bv])
                      yield
                      s.op("pool", lambda e: e.tensor_tensor(out=kgb[:].rearrange("p (d h) x -> p d h x", d=2), in0=kt[:].unsqueeze(1).to_broadcast([128, 2, 4, 128]),
                                                             in1=bc128(beg[:]), op=ALU.mult), [r_kt, r_beg, r_kgb], [r_kgb])
                      yield
                      s.op("pool", lambda e: e.tensor_tensor(out=recv[:, :, 1536:2048].rearrange("p d (h x) -> p d h x", h=4), in0=kt[:].unsqueeze(1).to_broadcast([128, 2, 4, 128]),
                                                             in1=bc128(sc[:, i, 8:16]), op=ALU.mult), [r_kt, r_sc, r_rec], [r_rec])
                      yield
                      Pseq = [(NTbd, r_NTbd), (T["Pb"], T["r_Pb"])]
                      Qseq = [(Nbd, r_Nbd), (T["Qb"], T["r_Qb"])]
                      zi = 0
                      for lev in range(1, 6):
                          Pc, r_Pc = Pseq[(lev - 1) % 2]
                          Qc, r_Qc = Qseq[(lev - 1) % 2]
                          Pn, r_Pn = Pseq[lev % 2]
                          Qn, r_Qn = Qseq[lev % 2]
                          def mmP(half, pp, r_pp):
                              with s.group("pe"):
                                  for k in range(4):
                                      p = half * 4 + k
                                      s.op("pe", lambda e, p=p, k=k: e.matmul(pp[:, k * 128:(k + 1) * 128], lhsT=Qc[:, p, :], rhs=Pc[:, p, :],
                                                                            start=True, stop=True), [r_Pc, r_Qc], [r_pp])

                          def mmQ(half, pp, r_pp):
                              with s.group("pe"):
                                  for k in range(4):
                                      p = half * 4 + k
                                      s.op("pe", lambda e, p=p, k=k: e.matmul(pp[:, k * 128:(k + 1) * 128], lhsT=Pc[:, p, :], rhs=Qc[:, p, :],
                                                                            start=True, stop=True), [r_Pc, r_Qc], [r_pp])

                          def evP(half, pp, r_pp):
                              s.op("act", lambda e: e.copy(Pn[:, half * 4:(half + 1) * 4, :], pp[:].rearrange("p (k x) -> p k x", k=4)), [r_pp, r_Pn], [r_Pn])
                              c.relbank(pp)

                          def evQ(half, pp, r_pp):
                              s.op("act", lambda e: e.copy(Qn[:, half * 4:(half + 1) * 4, :], pp[:].rearrange("p (k x) -> p k x", k=4)), [r_pp, r_Qn], [r_Qn])
                              c.relbank(pp)
                          pa0, r_pa0 = yield from c.getbank()
                          mmP(0, pa0, r_pa0)
                          yield
                          pa1, r_pa1 = yield from c.getbank()
                          mmP(1, pa1, r_pa1)
                          yield
                          evP(0, pa0, r_pa0)
                          yield
                          if lev < 5:
                              qa0, r_qa0 = yield from c.getbank()
                              mmQ(0, qa0, r_qa0)
                              yield
                          evP(1, pa1, r_pa1)
                          yield
                          if lev < 5:
                              qa1, r_qa1 = yield from c.getbank()
                              mmQ(1, qa1, r_qa1)
                              yield
                              evQ(0, qa0, r_qa0)
                              yield
                              evQ(1, qa1, r_qa1)
                              yield
                          Zc, r_Zc = Zb[zi], r_Zb[zi]
                          Zn, r_Zn = Zb[1 - zi], r_Zb[1 - zi]
                          for half in range(2):
                              pp, r_pp = yield from c.getbank()
                              with s.group("pe"):
                                  for k in range(4):
                                      p = half * 4 + k
                                      s.op("pe", lambda e, p=p, k=k, pp=pp, Zc=Zc, Pn=Pn: e.matmul(pp[:, k * 128:(k + 1) * 128], lhsT=Pn[:, p, :], rhs=Zc[:, p, :],
                                                                                            start=True, stop=True), [r_Pn, r_Zc], [r_pp])
                              yield
                              ppv = pp[:].rearrange("p (k x) -> p k x", k=4)
                              s.op("dve", lambda e, ppv=ppv, half=half, Zn=Zn, Zc=Zc: e.tensor_tensor(out=Zn[:, half * 4:(half + 1) * 4, :], in0=ppv,
                                                                                                     in1=Zc[:, half * 4:(half + 1) * 4, :], op=ALU.add),
                                   [r_pp, r_Zc, r_Zn], [r_Zn])
                              c.relbank(pp)
                              yield
                          zi = 1 - zi
                      Z2, r_Z2 = Zb[zi], r_Zb[zi]
                      for half in range(2):
                          pp, r_pp = yield from c.getbank()
                          with s.group("pe"):
                              for k in range(4):
                                  p = half * 4 + k
                                  s.op("pe", lambda e, p=p, k=k, pp=pp: e.matmul(pp[:, k * 128:(k + 1) * 128], lhsT=Z2[:, p, :], rhs=bv[:, p, :], start=True, stop=True),
                                       [r_Z2, r_bv], [r_pp])
                          yield
                          s.op("act", lambda e, pp=pp, half=half: e.copy(recv[:, half, 0:512], pp[:]), [r_pp, r_rec], [r_rec])
                          c.relbank(pp)
                          yield
                      for half in range(2):
                          pp, r_pp = yield from c.getbank()
                          with s.group("pe"):
                              for k in range(4):
                                  p = half * 4 + k
                                  s.op("pe", lambda e, p=p, k=k, pp=pp: e.matmul(pp[:, k * 128:(k + 1) * 128], lhsT=kgb[:, p, :], rhs=Z2[:, p, :], start=True, stop=True),
                                       [r_kgb, r_Z2], [r_pp])
                          yield
                          s.op("dve", lambda e, pp=pp, half=half: e.tensor_copy(recv[:, half, 512:1024], pp[:]), [r_pp, r_rec], [r_rec])
                          c.relbank(pp)
                          yield
                      s.dma("pool", c.zst[sq_i, i].rearrange("d p x -> p d x"), recv, reads=[r_rec], writes=[r_zst_t[i]])

                  sb2 = sb1
                  zb = [[sb2(f"zbuf{d}_{k}", [128, 2048], BF16) for k in range(2)] for d in range(2)]
                  r_zb = [[Res() for k in range(2)] for d in range(2)]
                  S = [sb2(f"S{d}", [128, 4, 128], F32) for d in range(2)]; r_S = [Res() for _ in range(2)]
                  Sb = [sb2(f"Sb{d}", [128, 4, 128], BF16) for d in range(2)]; r_Sb = [Res() for _ in range(2)]
                  vnew = [sb2(f"vnew_{d}", [128, 4, 128], BF16) for d in range(2)]; r_vnew = [Res() for _ in range(2)]
                  p1sb = [sb2(f"p1sb_{d}", [128, 4, 128], BF16) for d in range(2)]; r_p1sb = [Res() for _ in range(2)]
                  ot = [sb2(f"ot_{d}", [128, 4, 128], F32) for d in range(2)]; r_ot = [Res() for _ in range(2)]
                  for d in range(2):
                      s.op("pool", lambda e, d=d: e.memset(S[d][:], 0.0), [r_S[d]], [r_S[d]])
                      s.op("pool", lambda e, d=d: e.memset(Sb[d][:], 0.0), [r_Sb[d]], [r_Sb[d]])
                  bc4 = lambda ap: ap.unsqueeze(2).to_broadcast([64, 4, 128])

                  def c2_step(t, d):
                      n = t if d == 0 else NCH - 1 - t
                      i, a = n // 2, n % 2
                      pa = slice(a * 64, (a + 1) * 64)
                      rows = slice(n * 64, (n + 1) * 64)
                      zt, r_zt = zb[d][i % 2], r_zb[d][i % 2]
                      r_sc = r_scl[i]
                      first_of_tile = (a == 0) if d == 0 else (a == 1)
                      if first_of_tile:
                          if t == 0:
                              while i not in done_c1:
                                  yield
                              s.dma("sp", zt[:], c.zst[sq_i, i, d], reads=[r_zst_t[i]], writes=[r_zt])
                          inx = i + 1 if d == 0 else i - 1
                          if 0 <= inx < NT:
                              while inx not in done_c1:
                                  yield
                              s.dma("sp", zb[d][inx % 2][:], c.zst[sq_i, inx, d], reads=[r_zst_t[inx]], writes=[r_zb[d][inx % 2]])
                      U = zt[:, 0:512].rearrange("p (q x) -> p q x", q=4)
                      Wt = zt[:, 512:1024].rearrange("p (q x) -> p q x", q=4)
                      IT = zt[:, 1024:1280].rearrange("p (q x) -> p q x", q=4)
                      DG = zt[:, 1280:1536].rearrange("p (q x) -> p q x", q=4)
                      KD = zt[:, 1536:2048].rearrange("p (q x) -> p q x", q=4)
                      eGt = sc[:, i, 16 + a * 8 + d * 4:16 + a * 8 + (d + 1) * 4]
                      yield
                      pWS, r_pWS = yield from c.getbank()
                      pP1, r_pP1 = yield from c.getbank()
                      with s.group("pe"):
                          for h in range(4):
                              s.op("pe", lambda e, h=h: e.matmul(pWS[pa, h * 128:(h + 1) * 128], lhsT=Wt[:, h, a * 64:(a + 1) * 64], rhs=Sb[d][:, h, :],
                                                                 start=True, stop=True), [r_zt, r_Sb[d]], [r_pWS])
                      yield
                      with s.group("pe"):
                          for h in range(4):
                              s.op("pe", lambda e, h=h: e.matmul(pP1[pa, h * 128:(h + 1) * 128], lhsT=kqT[:, h, n, 64:128], rhs=Sb[d][:, h, :],
                                                                 start=True, stop=True), [r_kqT, r_Sb[d]], [r_pP1])
                      yield
                      v3 = lambda bank: bank[pa, :].rearrange("p (h e) -> p h e", h=4)
                      s.op("dve", lambda e: e.tensor_tensor(out=vnew[d][pa], in0=U[pa, :, :], in1=v3(pWS), op=ALU.subtract),
                           [r_zt, r_pWS, r_vnew[d]], [r_vnew[d]])
                      c.relbank(pWS)
                      yield
                      s.op("act", lambda e: e.copy(p1sb[d][pa], v3(pP1)), [r_pP1, r_p1sb[d]], [r_p1sb[d]])
                      c.relbank(pP1)
                      yield
                      pSU, r_pSU = yield from c.getbank()
                      pP2, r_pP2 = yield from c.getbank()
                      with s.group("pe"):
                          for h in range(4):
                              s.op("pe", lambda e, h=h: e.matmul(pSU[:, h * 128:(h + 1) * 128], lhsT=KD[pa, h, :], rhs=vnew[d][pa, h, :],
                                                                 start=True, stop=True), [r_zt, r_vnew[d]], [r_pSU])
                      yield
                      with s.group("pe"):
                          for h in range(4):
                              s.op("pe", lambda e, h=h: e.matmul(pP2[pa, h * 128:(h + 1) * 128], lhsT=DG[pa, h, :], rhs=p1sb[d][pa, h, :],
                                                                 start=True, stop=False), [r_zt, r_p1sb[d]], [r_pP2])
                              s.op("pe", lambda e, h=h: e.matmul(pP2[pa, h * 128:(h + 1) * 128], lhsT=IT[pa, h, :], rhs=vnew[d][pa, h, :],
                                                                 start=False, stop=True), [r_zt, r_vnew[d]], [r_pP2])
                      yield
                      for h in range(4):
                          s.op("dve", lambda e, h=h: e.scalar_tensor_tensor(out=Sb[d][:, h, :], in0=S[d][:, h, :], scalar=eGt[:, h:h + 1], in1=pSU[:, h * 128:(h + 1) * 128],
                                                                            op0=ALU.mult, op1=ALU.add), [r_pSU, r_S[d], r_sc, r_Sb[d]], [r_Sb[d]])
                      yield
                      s.op("pool", lambda e: e.tensor_tensor(out=S[d][:], in0=S[d][:], in1=eGt.unsqueeze(2).to_broadcast([128, 4, 128]), op=ALU.mult),
                           [r_S[d], r_sc], [r_S[d]])
                      yield
                      s.op("dve", lambda e: e.tensor_tensor(out=S[d][:], in0=pSU[:].rearrange("p (h e) -> p h e", h=4), in1=S[d][:], op=ALU.add),
                           [r_pSU, r_S[d]], [r_S[d]])
                      c.relbank(pSU)
                      yield
                      s.op("act", lambda e: e.copy(ot[d][pa], v3(pP2)), [r_pP2, r_ot[d]], [r_ot[d]])
                      c.relbank(pP2)
                      yield
                      s.dma("sp", c.ost[sq_i, d, rows, :], ot[d][pa].rearrange("p h e -> p (h e)"), reads=[r_ot[d]], writes=[r_ost_t[(d, n)]])
                      done_o[d].add(n)

                  def c2_dir(d):
                      for t in range(NCH):
                          yield from c2_step(t, d)

                  order = []
                  for k in range(NT // 2):
                      order += [k, NT - 1 - k]

                  def c1_worker(w):
                      for i in order[w::NSLOT]:
                          yield from c1_tile(i, slots[w])
                          done_c1.add(i)
                  NS3 = 2
                  T3 = []
                  for k in range(NS3):
                      T = {}
                      for nm, shp, dt in (("of", [128, 4, 128], F32), ("ob", [128, 4, 128], F32), ("gt", [128, 4, 128], BF16), ("gd", [128, 4, 128], F32),
                                          ("jk", [128, 128], BF16), ("rs", [128, 8], F32), ("fo", [128, 512], BF16)):
                          T[nm] = sb1(f"{nm}3_{k}", shp, dt)
                          T["r_" + nm] = Res()
                      T3.append(T)
                  ss3 = sb1("ss3", [128, NT, 4], F32); r_ss3 = Res()
                  s.op("pool", lambda e: e.memset(ss3[:], 0.0), [], [r_ss3])

                  def c3_tile(i, T):
                      rows = slice(i * 128, (i + 1) * 128)
                      of, ob, gt, gd, jk, rs, fo = T["of"], T["ob"], T["gt"], T["gd"], T["jk"], T["rs"], T["fo"]
                      s.dma("sp", of[:].rearrange("p h e -> p (h e)"), c.ost[sq_i, 0, rows, :], reads=[r_ost_t[(0, 2 * i)], r_ost_t[(0, 2 * i + 1)]], writes=[T["r_of"]])
                      s.dma("act", ob[:].rearrange("p h e -> p (h e)"), c.ost[sq_i, 1, rows, :], reads=[r_ost_t[(1, 2 * i)], r_ost_t[(1, 2 * i + 1)]], writes=[T["r_ob"]])
                      s.dma("sp", gt[:].rearrange("p h e -> p (h e)"), c.mix[sq_i, rows, 512:1024], reads=[c.r_mix[sq_i]], writes=[T["r_gt"]])
                      yield
                      s.op("dve", lambda e: e.tensor_tensor(out=of[:], in0=of[:], in1=ob[:], op=ALU.add), [T["r_of"], T["r_ob"]], [T["r_of"]])
                      yield
                      s.op("pool", lambda e: e.tensor_tensor(out=gd[:], in0=gt[:], in1=dnw[:].unsqueeze(1).to_broadcast([128, 4, 128]), op=ALU.mult),
                           [T["r_gt"], r_dnw, T["r_gd"]], [T["r_gd"]])
                      yield
                      for h in range(4):
                          s.op("act", lambda e, h=h: e.activation(out=jk[:], in_=of[:, h, :], func=AF.Square, accum_out=ss3[:, i, h:h + 1]),
                               [T["r_of"], T["r_jk"], r_ss3], [T["r_jk"], r_ss3])
                          yield
                      s.op("act", lambda e: e.activation(out=rs[:, 0:4], in_=ss3[:, i, :], func=AF.Ln, bias=EPS, scale=1.0 / 128), [r_ss3, T["r_rs"]], [T["r_rs"]])
                      yield
                      s.op("act", lambda e: e.activation(out=rs[:, 4:8], in_=rs[:, 0:4], func=AF.Exp, scale=-0.5), [T["r_rs"]], [T["r_rs"]])
                      yield
                      s.op("dve", lambda e: e.tensor_tensor(out=of[:], in0=of[:], in1=rs[:, 4:8].unsqueeze(2).to_broadcast([128, 4, 128]), op=ALU.mult),
                           [T["r_of"], T["r_rs"]], [T["r_of"]])
                      yield
                      s.op("pool", lambda e: e.tensor_tensor(out=fo[:].rearrange("p (h e) -> p h e", h=4), in0=of[:], in1=gd[:], op=ALU.mult),
                           [T["r_of"], T["r_gd"], T["r_fo"]], [T["r_fo"]])
                      yield
                      s.dma("sp", c.mix[sq_i, rows, 512:1024], fo[:], reads=[T["r_fo"]], writes=[c.r_mix[sq_i]])


                  c3_order = sorted(range(NT), key=lambda i: (max(2 * i + 1, 63 - 2 * i), i))

                  def c3_worker(w):
                      for i in c3_order[w::NS3]:
                          while not all((2 * i + a) in done_o[d] for d in range(2) for a in range(2)):
                              yield
                          yield from c3_tile(i, T3[w])
                  c.free_banks = list(range(8))
                  _interleave([c1_worker(w) for w in range(NSLOT)] + [c2_dir(0), c2_dir(1)] + [c3_worker(w) for w in range(NS3)])
                  assert len(c.free_banks) == 8
                  s.barrier()


def _rolling(units, make_gen, nslots):
    pending = list(units)
    active = []
    free = list(range(nslots))
    while pending or active:
        while pending and free:
            k = free.pop(0)
            active.append((make_gen(pending.pop(0), k), k))
        for item in list(active):
            try:
                next(item[0])
            except StopIteration:
                active.remove(item)
                free.append(item[1])


def _interleave(gens):
    gens = list(gens)
    while gens:
        for g in list(gens):
            try:
                next(g)
            except StopIteration:
                gens.remove(g)


def _host_constants():
    ident = np.eye(128, dtype=np.float32)
    masks = np.zeros((128, 16, 128), np.float32)
    p = np.arange(128)[:, None]
    q = np.arange(128)[None, :]
    same = (p // 64 == q // 64)
    masks[:, 0, :] = same
    masks[:, 1, :] = same & (p > q)
    masks[:, 2, :] = same & (p <= q)
    masks[:, 3, :] = same & (p < q)
    masks[:, 4, :] = same & (p >= q)
    masks[:, 5, :] = (p < 64) & (q >= 0)
    masks[:, 6, :] = (p >= 64) & (q >= 0)
    pl = p % 64
    ql = q % 64
    first = q < 64
    masks[:, 7, :] = np.where(first, pl <= ql, pl >= ql)
    masks[:, 8, :] = np.where(first, pl > ql, pl < ql)
    masks[:, 9, :] = np.where(first, pl < ql, pl > ql)
    masks[:, 10, :] = (pl == ql)
    return ident, masks


def _rpb_gather(rpb):
    a = np.arange(2)[:, None, None, None]
    kc = np.arange(64)[None, :, None, None]
    t = np.arange(16)[None, None, :, None]
    qc = np.arange(64)[None, None, None, :]
    i = np.clip(a + 14 - t, 0, 14) + 0 * kc + 0 * qc
    jj = np.clip(kc - qc + 15, 0, 30) + 0 * a + 0 * t
    g = rpb[:, :, i, jj]
    return np.ascontiguousarray(g.reshape(2, 8, 128, 1024))


def _na_valid():
    v = np.zeros((2, 64, 16, 64), np.float32)
    for a in range(2):
        for t in range(16):
            i = a + 14 - t
            if not (0 <= i <= 14):
                continue
            for qc in range(64):
                cs = int(np.clip(qc - 8, 0, 48))
                v[a, cs:cs + 16, t, qc] = 1.0
    return v.reshape(128, 1024)


def make_in_maps(inputs, n_cores=NCORES, n_seq=SEQ_PER_CORE):
    f = lambda a: np.ascontiguousarray(np.asarray(a, dtype=np.float32))
    ident, masks = _host_constants()
    shared = {
        "norm_w": f(inputs["norm_w"]), "w_in": f(inputs["w_in"]),
        "qk_gain_q": f(inputs["qk_gain_q"]), "qk_gain_k": f(inputs["qk_gain_k"]),
        "rpb_g": _rpb_gather(f(inputs["rpb"])),
        "conv_wT": np.ascontiguousarray(f(inputs["conv_w"]).transpose(0, 2, 1)),
        "a_log": f(inputs["a_log"]).reshape(2, 8), "dt_bias": f(inputs["dt_bias"]).reshape(2, 8),
        "dn_norm_w": f(inputs["dn_norm_w"]), "w_out": f(inputs["w_out"]),
        "c_ident": ident, "c_masks": masks, "c_navalid": _na_valid(),
    }
    x = f(inputs["x"])
    maps = []
    for i in range(n_cores):
        m = dict(shared)
        m["x"] = np.ascontiguousarray(x[i * n_seq:(i + 1) * n_seq])
        maps.append(m)
    return maps


def kernel(**inputs):
    nc, c = build_program()
    in_maps = make_in_maps(inputs)
    res = run_bass_kernel_spmd(nc, in_maps, core_ids=list(range(NCORES)))
    out = np.concatenate([np.asarray(r["out"]) for r in res.results], axis=0)
    return out.astype(np.float32)
```

```python
import numpy as np
from contextlib import ExitStack
import concourse.bass as bass
import concourse.mybir as mybir
from concourse.bass_utils import run_bass_kernel_spmd

F32 = mybir.dt.float32
BF16 = mybir.dt.bfloat16
ALU = mybir.AluOpType
AF = mybir.ActivationFunctionType
AX = mybir.AxisListType

D = 1024
L = 4096
DIN = 4112
NT = L // 128
EPS = 1e-6
NCORES = 8
SEQ_PER_CORE = 2


class Res:
    __slots__ = ("name", "w", "r")

    def __init__(self, name=""):
        self.name = name
        self.w = None
        self.r = []


class Sched:
    NRING = 8

    def __init__(self, nc):
        self.nc = nc
        self.E = {"pe": nc.tensor, "act": nc.scalar, "dve": nc.vector,
                  "pool": nc.gpsimd, "sp": nc.sync}
        self.sems = {}
        self.cnt = {}
        for e in self.E:
            self.sems[e] = nc.alloc_semaphore(name=f"s_{e}")
            self.cnt[e] = 0
        self.ring = {}
        self.ring_i = {}
        for q in ("sp", "pool", "act"):
            self.ring[q] = []
            for i in range(self.NRING):
                k = f"d_{q}{i}"
                self.sems[k] = nc.alloc_semaphore(name=k)
                self.cnt[k] = 0
                self.ring[q].append(k)
            self.ring_i[q] = 0
        self.seen = {e: {} for e in self.E}
        self.grp = None
        self.n_ins = 0
        self.n_wait = 0

    def _need(self, eng, reads, writes):
        need = {}

        def add(tok):
            if tok is None:
                return
            k, v = tok
            if k == eng and eng == "pe":
                return
            if need.get(k, 0) < v:
                need[k] = v
        for r in reads:
            add(r.w)
        for w in writes:
            add(w.w)
            for t in w.r:
                if t[0] == eng:
                    continue
                add(t)
        return need

    def _emit_waits(self, eng, need):
        seen = self.seen[eng]
        for k, v in need.items():
            if seen.get(k, 0) < v:
                if self.grp is not None and self.grp[0] == eng and k == eng and v >= self.grp[1][1]:
                    raise RuntimeError("dependency inside open group on same engine")
                self.E[eng].wait_ge(self.sems[k], v)
                seen[k] = v
                self.n_wait += 1

    def _mark(self, tok, reads, writes):
        for r in reads:
            r.r.append(tok)
        for w in writes:
            w.w = tok
            w.r = []

    def op(self, eng, fn, reads=(), writes=()):
        need = self._need(eng, reads, writes)
        self._emit_waits(eng, need)
        ins = fn(self.E[eng])
        self.n_ins += 1
        if self.grp is not None and self.grp[0] == eng:
            tok = self.grp[1]
            self.grp[2] = ins
        else:
            self.cnt[eng] += 1
            tok = (eng, self.cnt[eng])
            ins.then_inc(self.sems[eng], 1)
        self._mark(tok, reads, writes)
        return ins

    def group(self, eng):
        s = self

        class G:
            def __enter__(self_):
                assert s.grp is None
                s.cnt[eng] += 1
                s.grp = [eng, (eng, s.cnt[eng]), None]

            def __exit__(self_, *a):
                g = s.grp
                s.grp = None
                if a[0] is not None:
                    return False
                assert g[2] is not None
                g[2].then_inc(s.sems[eng], 1)
                return False
        return G()

    def dma(self, q, out, in_, reads=(), writes=(), **kw):
        ring = self.ring[q]
        i = self.ring_i[q]
        self.ring_i[q] = i + 1
        k = ring[i % self.NRING]
        need = self._need(q, reads, writes)
        if self.cnt[k] > 0 and need.get(k, 0) < self.cnt[k]:
            need[k] = self.cnt[k]
        self._emit_waits(q, need)
        ins = self.E[q].dma_start(out=out, in_=in_, **kw)
        self.cnt[k] += 16
        ins.then_inc(self.sems[k], 16)
        tok = (k, self.cnt[k])
        self._mark(tok, reads, writes)
        self.n_ins += 1
        return ins

    def barrier(self):
        for e in self.E:
            need = {k: v for k, v in self.cnt.items() if v > 0 and not (k == e and e == "pe")}
            self._emit_waits(e, need)

    def finish(self):
        need = {k: v for k, v in self.cnt.items() if v > 0}
        self._emit_waits("sp", need)


class Ctx:
    pass


_uid = [0]
CPARTS = "012"


def _u(name):
    _uid[0] += 1
    return f"{name}_{_uid[0]}"


def build_program(n_layers=2, n_seq=SEQ_PER_CORE, phases="ABCD", debug=False):
    nc = bass.Bass("TRN2", target_bir_lowering=False)
    s = Sched(nc)
    c = Ctx()
    c.nc, c.s = nc, s
    c.n_seq = n_seq

    def din(name, shape, dt=F32):
        return nc.dram_tensor(name, list(shape), dt, kind="ExternalInput").ap()

    def dscr(name, shape, dt=F32):
        return nc.dram_tensor(name, list(shape), dt, kind="ExternalOutput" if debug else "Internal").ap()

    c.x = din("x", [n_seq, L, D])
    c.norm_w = din("norm_w", [2, D])
    c.w_in = din("w_in", [2, D, DIN])
    c.gq = din("qk_gain_q", [2, 64])
    c.gk = din("qk_gain_k", [2, 64])
    c.rpbg = din("rpb_g", [2, 8, 128, 1024])
    c.conv_w = din("conv_wT", [2, 1536, 5])
    c.a_log = din("a_log", [2, 8])
    c.dt_bias = din("dt_bias", [2, 8])
    c.dn_w = din("dn_norm_w", [2, 128])
    c.w_out = din("w_out", [2, D, D])
    c.c_ident = din("c_ident", [128, 128])
    c.c_masks = din("c_masks", [128, 16, 128])
    c.c_navalid = din("c_navalid", [128, 1024])
    c.out = nc.dram_tensor("out", [n_seq, L, D], F32, kind="ExternalOutput").ap()

    c.xmid = dscr("xmid", [n_seq, L, D])
    c.cin = dscr("cin", [n_seq, 1536, L + 4], BF16)
    c.qT = dscr("qT", [n_seq, 512, L], BF16)
    c.kT = dscr("kT", [n_seq, 512, L], BF16)
    c.v1 = dscr("v1", [n_seq, L, 520], BF16)
    c.mix = dscr("mix", [n_seq, L, D], BF16)
    c.gates = dscr("gates", [n_seq, L, 24])
    c.zst = dscr("zst", [n_seq, NT, 2, 128, 2048], BF16)
    c.ktd = dscr("ktd", [n_seq, L, 512], BF16)
    c.vtd = dscr("vtd", [n_seq, L, 512], BF16)
    c.ost = dscr("ost", [n_seq, 2, L, 512])
    c.r_xmid = [Res() for _ in range(n_seq)]
    c.r_cin = [Res() for _ in range(n_seq)]
    c.r_qT = [Res() for _ in range(n_seq)]
    c.r_kT = [Res() for _ in range(n_seq)]
    c.r_v1 = [Res() for _ in range(n_seq)]
    c.r_mix = [Res() for _ in range(n_seq)]
    c.r_gates = [Res() for _ in range(n_seq)]
    c.r_zst = [Res() for _ in range(n_seq)]
    c.r_ktd = [Res() for _ in range(n_seq)]
    c.r_vtd = [Res() for _ in range(n_seq)]
    c.r_ost = [Res() for _ in range(n_seq)]
    c.dbg = {}
    if debug:
        c.dbg_out = {}

    with ExitStack() as es0:
        c.pb = [es0.enter_context(nc.psum_tensor(f"pb{i}", [128, 512], F32)) for i in range(8)]
        c.pr = [Res(f"pb{i}") for i in range(8)]
        c.pi = 0
        c.ring_banks = list(range(8))

        def nextbank():
            i = c.ring_banks[c.pi % len(c.ring_banks)]
            c.pi += 1
            return c.pb[i], c.pr[i]
        c.nextbank = nextbank
        c.free_banks = list(range(8))

        def getbank():
            while not c.free_banks:
                yield
            i = c.free_banks.pop(0)
            return c.pb[i], c.pr[i]

        def relbank(pp):
            i = [k for k in range(8) if c.pb[k] is pp][0]
            assert i not in c.free_banks
            c.free_banks.append(i)
        c.getbank, c.relbank = getbank, relbank

        c.ident_f = es0.enter_context(nc.sbuf_tensor("ident_f", [128, 128], F32)); c.r_identf = Res()
        c.ident = es0.enter_context(nc.sbuf_tensor("ident", [128, 128], BF16)); c.r_ident = Res()
        s.dma("sp", c.ident_f[:], c.c_ident, writes=[c.r_identf])
        s.op("dve", lambda e: e.tensor_copy(c.ident[:], c.ident_f[:]), [c.r_identf], [c.r_ident])

        for l in range(n_layers):
            src = c.x if l == 0 else c.xmid
            dst = c.out if l == n_layers - 1 else c.xmid
            if "A" in phases:
                phase_A(c, l, src)
                s.barrier()
            if "B" in phases:
                phase_B(c, l)
                s.barrier()
            if "C" in phases:
                phase_C(c, l)
                s.barrier()
            if "D" in phases:
                phase_D(c, l, src, dst)
                s.barrier()
        s.finish()
    c.stats = (s.n_ins, s.n_wait)
    return nc, c


def phase_A(c, l, src):
    nc, s = c.nc, c.s
    with ExitStack() as es:
        def sb(name, shape, dt):
            return es.enter_context(nc.sbuf_tensor(_u(name), shape, dt))
        Wb = sb("Wb", [128, 8, DIN], BF16); r_Wb = Res()
        nw = sb("nw", [128, 8], F32); r_nw = Res()
        stage = [sb(f"wstage{i}", [128, 8, 512], F32) for i in range(2)]
        r_stage = [Res() for _ in range(2)]
        s.dma("sp", nw[:], c.norm_w[l].rearrange("(t p) -> p t", p=128), writes=[r_nw], allow_slow_non_contiguous=True)
        wv = c.w_in[l].rearrange("(t p) e -> p t e", p=128)
        chunks = [(i * 512, 512) for i in range(8)] + [(4096, 16)]
        for ci, (c0, cw) in enumerate(chunks):
            st, rs = stage[ci % 2], r_stage[ci % 2]
            s.dma("sp" if ci % 2 == 0 else "act", st[:, :, 0:cw], wv[:, :, c0:c0 + cw], writes=[rs])
            eng = "dve" if ci % 2 == 0 else "pool"
            s.op(eng, lambda e, st=st, c0=c0, cw=cw: e.tensor_tensor(
                out=Wb[:, :, c0:c0 + cw], in0=st[:, :, 0:cw], in1=nw[:].unsqueeze(2).to_broadcast([128, 8, cw]), op=ALU.mult),
                [rs, r_nw], [r_Wb])
        gqk = sb("gqk", [128, 2], F32); r_gqk = Res()
        s.dma("sp", gqk[0:64, 0:1], c.gq[l].rearrange("(p o) -> p o", o=1), writes=[r_gqk])
        s.dma("sp", gqk[64:128, 0:1], c.gq[l].rearrange("(p o) -> p o", o=1), writes=[r_gqk])
        s.dma("sp", gqk[0:64, 1:2], c.gk[l].rearrange("(p o) -> p o", o=1), writes=[r_gqk])
        s.dma("sp", gqk[64:128, 1:2], c.gk[l].rearrange("(p o) -> p o", o=1), writes=[r_gqk])
        gprod = sb("gprod", [128, 1], F32); r_gprod = Res()
        s.op("dve", lambda e: e.tensor_tensor(out=gprod[:], in0=gqk[:, 0:1], in1=gqk[:, 1:2], op=ALU.mult), [r_gqk], [r_gprod])
        bones = sb("bones", [128, 128], BF16); r_bones = Res()
        mk = sb("mk_a", [128, 128], F32); r_mk = Res()
        s.dma("sp", mk[:], c.c_masks[:, 0, :], writes=[r_mk])
        s.op("dve", lambda e: e.tensor_copy(bones[:], mk[:]), [r_mk], [r_bones])
        dtb = sb("dtb", [128, 8], F32); r_dtb = Res()
        nega = sb("nega", [128, 8], F32); r_nega = Res()
        s.dma("sp", dtb[:], c.dt_bias[l].partition_broadcast(128), writes=[r_dtb])
        s.dma("sp", nega[:], c.a_log[l].partition_broadcast(128), writes=[r_nega])
        s.op("act", lambda e: e.activation(out=nega[:], in_=nega[:], func=AF.Exp), [r_nega], [r_nega])
        s.op("dve", lambda e: e.tensor_scalar(out=nega[:], in0=nega[:], scalar1=-1.0, scalar2=None, op0=ALU.mult), [r_nega], [r_nega])

        xt = [sb(f"xt{i}", [128, D], F32) for i in range(4)]; r_xt = [Res() for _ in range(4)]
        junk = sb("junkA", [128, D], BF16); r_junk = Res()
        ss = [sb(f"ssA{k}", [128, NT], F32) for k in range(2)]; r_ss = [Res() for _ in range(2)]
        lnv = sb("lnvA", [128, 2 * NT, 2], F32); r_lnv = [Res() for _ in range(2 * NT)]
        hb = [sb(f"hb{i}", [128, D], BF16) for i in range(4)]; r_hb = [Res() for _ in range(4)]
        hT4 = [sb(f"hT4_{i}", [128, 8, 512], BF16) for i in range(2)]; r_hT4 = [Res() for _ in range(2)]
        vst = [sb(f"vst{i}", [128, 8, 65], BF16) for i in range(2)]; r_vst = [Res() for _ in range(2)]
        gz = [sb(f"gz{i}", [128, D], BF16) for i in range(2)]; r_gz = [Res() for _ in range(2)]
        gstash = [sb(f"gstash{k}", [128, NT, 16], F32) for k in range(2)]; r_gst = [Res() for _ in range(2)]
        gproc = sb("gproc", [128, NT, 24], F32); r_gproc = Res()
        fm = [sb(f"fm{i}", [128, 512], BF16) for i in range(4)]; r_fm = [Res() for _ in range(4)]
        sq = [sb(f"sq{i}", [128, 512], BF16) for i in range(2)]; r_sq = [Res() for _ in range(2)]
        rn = [sb(f"rn{i}", [128, 512], F32) for i in range(2)]; r_rn = [Res() for _ in range(2)]
        for i in range(2):
            s.op("pool", lambda e, i=i: e.memset(vst[i][:], 1.0), [], [r_vst[i]])
        cnt = {"fm": 0, "sq": 0, "ev": 0}

        def prepA(sq_i, st):
            sk = sq_i % 2
            h4, r_h4 = hT4[st % 2], r_hT4[st % 2]
            if st == 0:
                s.op("dve", lambda e: e.memset(ss[sk][:], 0.0), [r_ss[sk]], [r_ss[sk]])
            for ti in range(4):
                i = st * 4 + ti
                b = i % 4
                lv = lnv[:, sk * NT + i, :]
                r_lv = r_lnv[sk * NT + i]
                s.dma("sp", xt[b][:], src[sq_i, i * 128:(i + 1) * 128, :], reads=[c.r_xmid[sq_i]], writes=[r_xt[b]])
                yield
                s.op("act", lambda e: e.activation(out=junk[:], in_=xt[b][:], func=AF.Square, accum_out=ss[sk][:, i:i + 1]),
                     [r_xt[b], r_ss[sk]], [r_junk, r_ss[sk]])
                yield
                s.op("act", lambda e: e.activation(out=lv[:, 0:1], in_=ss[sk][:, i:i + 1], func=AF.Ln, bias=EPS, scale=1.0 / D),
                     [r_ss[sk], r_lv], [r_lv])
                yield
                s.op("act", lambda e: e.activation(out=lv[:, 1:2], in_=lv[:, 0:1], func=AF.Exp, scale=-0.5), [r_lv], [r_lv])
                yield
                s.op("dve", lambda e: e.tensor_scalar(out=hb[b][:], in0=xt[b][:], scalar1=lv[:, 1:2], scalar2=None, op0=ALU.mult),
                     [r_xt[b], r_lv], [r_hb[b]])
                yield
            for ti in range(4):
                i = st * 4 + ti
                b = i % 4
                pt, r_pt = c.nextbank()
                ptb = pt[:].bitcast(BF16)
                with s.group("pe"):
                    for dt in range(8):
                        s.op("pe", lambda e, dt=dt: e.transpose(ptb[:, dt * 128:(dt + 1) * 128], hb[b][:, dt * 128:(dt + 1) * 128], c.ident[:]),
                             [r_hb[b], c.r_ident], [r_pt])
                yield
                s.op("dve", lambda e: e.tensor_copy(h4[:, :, ti * 128:(ti + 1) * 128], ptb.rearrange("p (a b) -> p a b", a=8)), [r_pt], [r_h4])
                yield

        def mainA(sq_i, st):
            sk = sq_i % 2
            h4, r_h4 = hT4[st % 2], r_hT4[st % 2]
            for ti in range(4):
                i = st * 4 + ti
                vb = i % 2
                for (c0, cw, kind) in ((1024, 512, "v"), (1536, 512, "az"), (3584, 512, "dz"), (4096, 16, "g")):
                    pp, r_pp = c.nextbank()
                    with s.group("pe"):
                        for dt in range(8):
                            s.op("pe", lambda e, dt=dt: e.matmul(
                                pp[:, 0:cw], lhsT=h4[:, dt, ti * 128:(ti + 1) * 128], rhs=Wb[:, dt, c0:c0 + cw],
                                start=(dt == 0), stop=(dt == 7)), [r_h4, r_Wb], [r_pp])
                    yield
                    if kind == "v":
                        s.op("act", lambda e: e.copy(vst[vb][:, :, 0:64], pp[:].rearrange("p (h d) -> p h d", h=8)), [r_pp], [r_vst[vb]])
                        s.dma("pool", c.v1[sq_i, i * 128:(i + 1) * 128, :], vst[vb][:].rearrange("p h d -> p (h d)"),
                              reads=[r_vst[vb]], writes=[c.r_v1[sq_i]])
                    elif kind == "az":
                        s.op("act", lambda e: e.activation(out=gz[vb][:, 0:512], in_=pp[:], func=AF.Silu), [r_pp], [r_gz[vb]])
                    elif kind == "dz":
                        s.op("act", lambda e: e.activation(out=gz[vb][:, 512:1024], in_=pp[:], func=AF.Silu), [r_pp], [r_gz[vb]])
                        s.dma("pool", c.mix[sq_i, i * 128:(i + 1) * 128, :], gz[vb][:], reads=[r_gz[vb]], writes=[c.r_mix[sq_i]])
                    else:
                        s.op("dve", lambda e: e.tensor_copy(gstash[sk][:, i, :], pp[:, 0:16]), [r_pp], [r_gst[sk]])
                    yield
            for et in range(20):
                if et < 8:
                    c0 = et * 128
                else:
                    c0 = 2048 + (et - 8) * 128
                pp, r_pp = c.nextbank()
                with s.group("pe"):
                    for dt in range(8):
                        s.op("pe", lambda e, dt=dt: e.matmul(
                            pp[:], lhsT=Wb[:, dt, c0:c0 + 128], rhs=h4[:, dt, :], start=(dt == 0), stop=(dt == 7)),
                            [r_h4, r_Wb], [r_pp])
                yield
                fb = cnt["fm"] % 4
                cnt["fm"] += 1
                if et >= 8:
                    eng = "act" if cnt["ev"] % 2 == 0 else "dve"
                    cnt["ev"] += 1
                    if eng == "act":
                        s.op("act", lambda e: e.copy(fm[fb][:], pp[:]), [r_pp], [r_fm[fb]])
                    else:
                        s.op("dve", lambda e: e.tensor_copy(fm[fb][:], pp[:]), [r_pp], [r_fm[fb]])
                    ch0 = (et - 8) * 128
                    s.dma("pool", c.cin[sq_i, ch0:ch0 + 128, 2 + st * 512:2 + (st + 1) * 512], fm[fb][:],
                          reads=[r_fm[fb]], writes=[c.r_cin[sq_i]])
                    yield
                else:
                    qb = cnt["sq"] % 2
                    cnt["sq"] += 1
                    s.op("act", lambda e: e.activation(out=sq[qb][:], in_=pp[:], func=AF.Square), [r_pp], [r_sq[qb]])
                    yield
                    p2, r_p2 = c.nextbank()
                    s.op("pe", lambda e: e.matmul(p2[:], lhsT=bones[:], rhs=sq[qb][:], start=True, stop=True), [r_bones, r_sq[qb]], [r_p2])
                    yield
                    s.op("act", lambda e: e.activation(out=rn[qb][:], in_=p2[:], func=AF.Ln, bias=EPS, scale=1.0 / 64), [r_p2], [r_rn[qb]])
                    yield
                    s.op("act", lambda e: e.activation(out=rn[qb][:], in_=rn[qb][:], func=AF.Exp, scale=-0.5), [r_rn[qb]], [r_rn[qb]])
                    yield
                    if et < 4:
                        s.op("dve", lambda e: e.scalar_tensor_tensor(out=fm[fb][:], in0=pp[:], scalar=0.125, in1=rn[qb][:], op0=ALU.mult, op1=ALU.mult),
                             [r_pp, r_rn[qb]], [r_fm[fb]])
                        s.dma("pool", c.qT[sq_i, et * 128:(et + 1) * 128, st * 512:(st + 1) * 512], fm[fb][:], reads=[r_fm[fb]], writes=[c.r_qT[sq_i]])
                    else:
                        s.op("dve", lambda e: e.scalar_tensor_tensor(out=fm[fb][:], in0=pp[:], scalar=gprod[:, 0:1], in1=rn[qb][:], op0=ALU.mult, op1=ALU.mult),
                             [r_pp, r_rn[qb], r_gprod], [r_fm[fb]])
                        s.dma("pool", c.kT[sq_i, (et - 4) * 128:(et - 3) * 128, st * 512:(st + 1) * 512], fm[fb][:], reads=[r_fm[fb]], writes=[c.r_kT[sq_i]])
                    yield
            if st == 7:
                gs = gstash[sk]
                gv = gs[:].rearrange("p t (d k h) -> p t d k h", d=2, k=2)
                bview = gv[:, :, :, 0, :]
                aview = gv[:, :, :, 1, :]
                beta_o = gproc[:, :, 0:8].rearrange("p t (d h) -> p t d h", d=2)
                g_o = gproc[:, :, 8:16].rearrange("p t (d h) -> p t d h", d=2)
                lnb_o = gproc[:, :, 16:24].rearrange("p t (d h) -> p t d h", d=2)
                s.op("act", lambda e: e.activation(out=beta_o, in_=bview, func=AF.Sigmoid), [r_gst[sk], r_gproc], [r_gproc])
                yield
                s.op("act", lambda e: e.activation(out=lnb_o, in_=beta_o, func=AF.Ln), [r_gproc], [r_gproc])
                yield
                s.op("dve", lambda e: e.tensor_tensor(out=g_o, in0=aview, in1=dtb[:].rearrange("p (d h) -> p d h", d=2).unsqueeze(1).to_broadcast([128, NT, 2, 4]),
                                                      op=ALU.add), [r_gst[sk], r_dtb, r_gproc], [r_gproc])
                yield
                s.op("dve", lambda e: e.tensor_scalar(out=gproc[:, :, 8:16], in0=gproc[:, :, 8:16], scalar1=80.0, scalar2=None, op0=ALU.min), [r_gproc], [r_gproc])
                yield
                s.op("act", lambda e: e.activation(out=gproc[:, :, 8:16], in_=gproc[:, :, 8:16], func=AF.Exp), [r_gproc], [r_gproc])
                yield
                s.op("act", lambda e: e.activation(out=gproc[:, :, 8:16], in_=gproc[:, :, 8:16], func=AF.Ln, bias=1.0), [r_gproc], [r_gproc])
                yield
                s.op("dve", lambda e: e.tensor_tensor(out=gproc[:, :, 8:16], in0=gproc[:, :, 8:16], in1=nega[:].unsqueeze(1).to_broadcast([128, NT, 8]),
                                                      op=ALU.mult), [r_gproc, r_nega], [r_gproc])
                yield
                gdst = c.gates[sq_i].rearrange("(t p) c -> p t c", p=128)
                for qd in range(4):
                    s.dma("sp", gdst[:, qd * 8:(qd + 1) * 8, :], gproc[:, qd * 8:(qd + 1) * 8, :], reads=[r_gproc], writes=[c.r_gates[sq_i]])

        seqs = [(sq_i, st) for sq_i in range(c.n_seq) for st in range(8)]
        _interleave([prepA(*seqs[0])])
        for k, (sq_i, st) in enumerate(seqs):
            gens = [mainA(sq_i, st)]
            if k + 1 < len(seqs):
                gens.append(prepA(*seqs[k + 1]))
            _interleave(gens)


def phase_D(c, l, src, dst):
    nc, s = c.nc, c.s
    with ExitStack() as es:
        def sb(name, shape, dt):
            return es.enter_context(nc.sbuf_tensor(_u(name), shape, dt))
        Wo = sb("Wo", [128, 8, D], BF16); r_Wo = Res()
        stage = [sb(f"wostage{i}", [128, 8, 512], F32) for i in range(2)]
        r_stage = [Res() for _ in range(2)]
        wv = c.w_out[l].rearrange("(t p) e -> p t e", p=128)
        for ci in range(2):
            s.dma("sp", stage[ci][:], wv[:, :, ci * 512:(ci + 1) * 512], writes=[r_stage[ci]])
            s.op("dve" if ci == 0 else "pool", lambda e, ci=ci: e.tensor_copy(Wo[:, :, ci * 512:(ci + 1) * 512], stage[ci][:]), [r_stage[ci]], [r_Wo])
        ND = 3
        xt = [sb(f"xtD{i}", [128, D], F32) for i in range(ND)]; r_xt = [Res() for _ in range(ND)]
        mt = [sb(f"mtD{i}", [128, D], BF16) for i in range(ND)]; r_mt = [Res() for _ in range(ND)]
        mT = [sb(f"mTD{i}", [128, 8, 128], BF16) for i in range(ND)]; r_mT = [Res() for _ in range(ND)]
        yo = [sb(f"yoD{i}", [128, D], F32) for i in range(ND)]; r_yo = [Res() for _ in range(ND)]

        def d_tile(u, b):
            sq_i, i = u
            rows = slice(i * 128, (i + 1) * 128)
            s.dma("sp", xt[b][:], src[sq_i, rows, :], reads=[c.r_xmid[sq_i]], writes=[r_xt[b]])
            s.dma("act", mt[b][:], c.mix[sq_i, rows, :], reads=[c.r_mix[sq_i]], writes=[r_mt[b]])
            yield
            pt, r_pt = c.nextbank()
            ptb = pt[:].bitcast(BF16)
            with s.group("pe"):
                for et in range(8):
                    s.op("pe", lambda e, et=et: e.transpose(ptb[:, et * 128:(et + 1) * 128], mt[b][:, et * 128:(et + 1) * 128], c.ident[:]),
                         [r_mt[b], c.r_ident], [r_pt])
            yield
            s.op("act", lambda e: e.copy(mT[b][:], ptb.rearrange("p (a b) -> p a b", a=8)), [r_pt], [r_mT[b]])
            yield
            for ch in range(2):
                pp, r_pp = c.nextbank()
                with s.group("pe"):
                    for et in range(8):
                        s.op("pe", lambda e, et=et: e.matmul(
                            pp[:], lhsT=mT[b][:, et, :], rhs=Wo[:, et, ch * 512:(ch + 1) * 512], start=(et == 0), stop=(et == 7)),
                            [r_mT[b], r_Wo], [r_pp])
                yield
                s.op("dve", lambda e: e.tensor_tensor(
                    out=yo[b][:, ch * 512:(ch + 1) * 512], in0=pp[:], in1=xt[b][:, ch * 512:(ch + 1) * 512], op=ALU.add),
                    [r_pp, r_xt[b]], [r_yo[b]])
                yield
            wres = [c.r_xmid[sq_i]] if dst is c.xmid else []
            s.dma("pool", dst[sq_i, rows, :], yo[b][:], reads=[r_yo[b]], writes=wres)

        _rolling([(sq_i, i) for sq_i in range(c.n_seq) for i in range(NT)], d_tile, ND)


def _rs(r):
    return min(max(r - 4, 0), 56)


def phase_B(c, l):
    nc, s = c.nc, c.s
    Rj = {}
    for j in range(32):
        rows = [r for r in range(64) if _rs(r) <= 2 * j + 1 and _rs(r) + 7 >= 2 * j]
        Rj[j] = (rows[0], rows[-1])
    c.ring_banks = [0, 1, 2, 3, 4, 5]
    with ExitStack() as es:
        def sb(name, shape, dt):
            return es.enter_context(nc.sbuf_tensor(_u(name), shape, dt))
        Mtab = sb("Mtab", [128, 8, 1024], BF16); r_Mtab = Res()
        with ExitStack() as es2:
            nav = es2.enter_context(nc.sbuf_tensor(_u("nav"), [128, 1024], F32)); r_nav = Res()
            stg = [es2.enter_context(nc.sbuf_tensor(_u(f"rstg{i}"), [128, 1024], F32)) for i in range(2)]
            r_stg = [Res() for _ in range(2)]
            s.dma("sp", nav[:], c.c_navalid, writes=[r_nav])
            for h in range(8):
                b = h % 2
                s.dma("sp", stg[b][:], c.rpbg[l, h], writes=[r_stg[b]])
                s.op("act", lambda e, b=b: e.activation(out=stg[b][:], in_=stg[b][:], func=AF.Exp), [r_stg[b]], [r_stg[b]])
                s.op("dve", lambda e, b=b, h=h: e.tensor_tensor(out=Mtab[:, h, :], in0=stg[b][:], in1=nav[:], op=ALU.mult),
                     [r_stg[b], r_nav], [r_Mtab])
            s.barrier()
        V1 = sb("V1all", [128, NT, 520], BF16); r_V1 = Res()
        mixA = sb("mixA", [128, NT, 512], BF16); r_mixA = Res()
        QT2 = [sb(f"QT2_{i}", [128, L], BF16) for i in range(2)]; r_QT2 = [Res() for _ in range(2)]
        KT2 = [sb(f"KT2_{i}", [128, L], BF16) for i in range(2)]; r_KT2 = [Res() for _ in range(2)]
        NSL = 7
        Es = [[sb(f"E{hh}_{k}", [128, 768], BF16) for k in range(NSL)] for hh in range(2)]
        r_Es = [[Res() for k in range(NSL)] for hh in range(2)]
        rdl = [sb(f"rdB{k}", [128, 2], F32) for k in range(2)]; r_rdl = [Res() for _ in range(2)]
        attl = [sb(f"attB{k}", [128, 2, 64], F32) for k in range(2)]; r_attl = [Res() for _ in range(2)]
        nq = 0
        for sq_i in range(c.n_seq):
            v1v = c.v1[sq_i].rearrange("(j p) c -> p j c", p=128)
            mxv = c.mix[sq_i].rearrange("(j p) c -> p j c", p=128)
            for qd in range(4):
                s.dma("sp", V1[:, qd * 8:(qd + 1) * 8, :], v1v[:, qd * 8:(qd + 1) * 8, :], reads=[c.r_v1[sq_i]], writes=[r_V1])
                s.dma("act", mixA[:, qd * 8:(qd + 1) * 8, :], mxv[:, qd * 8:(qd + 1) * 8, 0:512], reads=[c.r_mix[sq_i]], writes=[r_mixA])
            qbs = {}
            for hp in range(4):
                qbs[hp] = nq % 2
                nq += 1

            def qk_gen(hp, j):
                qb = qbs[hp]
                if j == 0:
                    s.dma("sp", QT2[qb][:], c.qT[sq_i, hp * 128:(hp + 1) * 128, :], reads=[c.r_qT[sq_i]], writes=[r_QT2[qb]])
                    s.dma("sp", KT2[qb][:], c.kT[sq_i, hp * 128:(hp + 1) * 128, :], reads=[c.r_kT[sq_i]], writes=[r_KT2[qb]])
                lo, hi = Rj[j]
                ncols = (hi - lo + 1) * 64
                t0 = lo - 2 * j + 7
                sl = (hp * 32 + j) % NSL
                for hh in range(2):
                    head = 2 * hp + hh
                    ph = slice(hh * 64, hh * 64 + 64)
                    c0 = 0
                    while c0 < ncols:
                        cw = min(512, ncols - c0)
                        pp, r_pp = c.nextbank()
                        s.op("pe", lambda e: e.matmul(
                            pp[:, 0:cw], lhsT=KT2[qb][ph, 128 * j:128 * j + 128], rhs=QT2[qb][ph, 64 * lo + c0:64 * lo + c0 + cw],
                            start=True, stop=True), [r_KT2[qb], r_QT2[qb]], [r_pp])
                        yield
                        s.op("act", lambda e: e.activation(out=Es[hh][sl][:, c0:c0 + cw], in_=pp[:, 0:cw], func=AF.Exp), [r_pp], [r_Es[hh][sl]])
                        yield
                        eng = "dve" if (hh == 0 or (j % 3 == 0)) else "pool"
                        s.op(eng, lambda e: e.tensor_tensor(
                            out=Es[hh][sl][:, c0:c0 + cw], in0=Es[hh][sl][:, c0:c0 + cw],
                            in1=Mtab[:, head, t0 * 64 + c0:t0 * 64 + c0 + cw], op=ALU.mult), [r_Es[hh][sl], r_Mtab], [r_Es[hh][sl]])
                        yield
                        c0 += cw
                    for r in range(lo, hi + 1):
                        for a in range(2):
                            kr = 2 * j + a
                            if not (_rs(r) <= kr <= _rs(r) + 7):
                                s.op("pool", lambda e: e.memset(Es[hh][sl][a * 64:(a + 1) * 64, (r - lo) * 64:(r - lo + 1) * 64], 0.0),
                                     [r_Es[hh][sl]], [r_Es[hh][sl]])
                                yield

            def pv_gen(hp, j):
                for i in range(32):
                    r0, r1 = 2 * i, 2 * i + 1
                    T0 = list(range(_rs(r0) // 2, (_rs(r0) + 7) // 2 + 1))
                    T1 = list(range(_rs(r1) // 2, (_rs(r1) + 7) // 2 + 1))
                    if max(T0[-1], T1[-1]) != j:
                        continue
                    ob, r_ob = c.pb[6 + (i % 2)], c.pr[6 + (i % 2)]
                    common = [jj for jj in T0 if jj in T1]
                    ex0 = [jj for jj in T0 if jj not in T1]
                    ex1 = [jj for jj in T1 if jj not in T0]
                    assert len(common) >= 2
                    plan = [(common[0], 0, 128)] + [(jj, 0, 64) for jj in ex0] + [(jj, 64, 64) for jj in ex1] + [(jj, 0, 128) for jj in common[1:]]
                    for hh in range(2):
                        head = 2 * hp + hh
                        with s.group("pe"):
                            for pi, (jj, p0, m) in enumerate(plan):
                                r = r0 if p0 == 0 else r1
                                blk = (r - Rj[jj][0]) * 64
                                sl = (hp * 32 + jj) % NSL
                                s.op("pe", lambda e: e.matmul(
                                    ob[p0:p0 + m, hh * 66:hh * 66 + 65], lhsT=Es[hh][sl][:, blk:blk + m],
                                    rhs=V1[:, jj, head * 65:head * 65 + 65], start=(pi == 0), stop=(pi == len(plan) - 1)),
                                    [r_Es[hh][sl], r_V1], [r_ob])
                        yield
                    rd, r_rd, att, r_att = rdl[i % 2], r_rdl[i % 2], attl[i % 2], r_attl[i % 2]
                    ov = ob[:, 0:132].rearrange("p (h d) -> p h d", h=2)
                    s.op("dve", lambda e: e.reciprocal(out=rd[:], in_=ov[:, :, 64]), [r_ob, r_rd], [r_rd])
                    yield
                    s.op("dve", lambda e: e.tensor_tensor(out=att[:], in0=ov[:, :, 0:64], in1=rd[:].unsqueeze(2).to_broadcast([128, 2, 64]), op=ALU.mult),
                         [r_ob, r_rd, r_att], [r_att])
                    yield
                    s.op("pool", lambda e: e.tensor_tensor(
                        out=mixA[:, i, hp * 128:(hp + 1) * 128], in0=att[:].rearrange("p h d -> p (h d)"),
                        in1=mixA[:, i, hp * 128:(hp + 1) * 128], op=ALU.mult), [r_att, r_mixA], [r_mixA])
                    yield

            units = [(hp, j) for hp in range(4) for j in range(32)]
            LAG = 2
            for k in range(len(units) + LAG):
                gens = []
                if k - LAG >= 0:
                    gens.append(pv_gen(*units[k - LAG]))
                if k < len(units):
                    gens.append(qk_gen(*units[k]))
                _interleave(gens)
            for qd in range(4):
                s.dma("pool", mxv[:, qd * 8:(qd + 1) * 8, 0:512], mixA[:, qd * 8:(qd + 1) * 8, :], reads=[r_mixA], writes=[c.r_mix[sq_i]])
    c.ring_banks = list(range(8))


def phase_C(c, l):
    nc, s = c.nc, c.s
    NCH = 64
    with ExitStack() as es:
        def sb(name, shape, dt):
            return es.enter_context(nc.sbuf_tensor(_u(name), shape, dt))
        mskf = sb("mskf", [128, 7, 128], F32); r_mskf = Res()
        s.dma("sp", mskf[:], c.c_masks[:, 0:7, :], writes=[r_mskf])
        BD_ONES, BD_GT, BD_LE, BD_LT, BD_GE, CH_A, CH_B = [mskf[:, k, :] for k in range(7)]
        mske = sb("mske", [128, 4, 128], F32); r_mske = Res()
        s.dma("sp", mske[:], c.c_masks[:, 7:11, :], writes=[r_mske])
        Ma = mske[:, 0, :].rearrange("p (d c) -> p d c", d=2)
        Mb = mske[:, 1, :].rearrange("p (d c) -> p d c", d=2)
        Mc = mske[:, 2, :].rearrange("p (d c) -> p d c", d=2)
        EQ = mske[:, 3, 0:64]
        ones_b = sb("ones_b", [128, 128], BF16); r_onesb = Res()
        s.op("pool", lambda e: e.memset(ones_b[:], 1.0), [], [r_onesb])
        dnw = sb("dnw_bc", [128, 128], F32); r_dnw = Res()
        s.dma("sp", dnw[:], c.dn_w[l].partition_broadcast(128), writes=[r_dnw])
        cw = sb("cwC", [128, 12, 5], F32); r_cw = Res()
        s.dma("sp", cw[:], c.conv_w[l].rearrange("(t p) j -> p t j", p=128), writes=[r_cw])

        kqT = sb("kqT", [128, 4, NCH, 128], BF16); r_kqT = Res()
        GT = sb("GTall", [128, NT, 24], F32); r_GT = Res()
        sc = sb("scC", [128, NT, 32], F32); r_scl = [Res() for _ in range(NT)]

        for sq_i in range(c.n_seq):
            gsrc = c.gates[sq_i].rearrange("(t p) c -> p t c", p=128)
            for qd in range(4):
                s.dma("sp", GT[:, qd * 8:(qd + 1) * 8, :], gsrc[:, qd * 8:(qd + 1) * 8, :], reads=[c.r_gates[sq_i]], writes=[r_GT])
            with ExitStack() as e0:
              if "0" in CPARTS:
                  def sb0(name, shape, dt):
                      return e0.enter_context(nc.sbuf_tensor(_u(name), shape, dt))
                  dg = sb0("dgC", [128, 60, 128], BF16); r_dg = Res()
                  for t in range(12):
                      for jx in range(5):
                          if (t * 5 + jx) % 2 == 0:
                              s.op("dve", lambda e, t=t, jx=jx: e.tensor_scalar(out=dg[:, t * 5 + jx, :], in0=c.ident_f[:], scalar1=cw[:, t, jx:jx + 1],
                                                                                scalar2=None, op0=ALU.mult), [c.r_identf, r_cw], [r_dg])
                          else:
                              s.op("act", lambda e, t=t, jx=jx: e.mul(dg[:, t * 5 + jx, :], c.ident_f[:], cw[:, t, jx:jx + 1]), [c.r_identf, r_cw], [r_dg])
                  NS0 = 6
                  T0 = []
                  for k in range(NS0):
                      T = {}
                      for nm, shp, dt in (("xin", [128, 516], BF16), ("yb", [128, 512], F32), ("sqb", [128, 512], BF16), ("rnb", [128, 512], F32),
                                          ("knb", [128, 512], BF16), ("tst", [128, 4, 128], BF16)):
                          T[nm] = sb0(f"{nm}0_{k}", shp, dt)
                          T["r_" + nm] = Res()
                      T0.append(T)

                  def c0_unit(tb, h, qkv, T):
                      xin, yb, sqb, rnb, knb = T["xin"], T["yb"], T["sqb"], T["rnb"], T["knb"]
                      r_xin, r_yb, r_sqb, r_rnb, r_knb = T["r_xin"], T["r_yb"], T["r_sqb"], T["r_rnb"], T["r_knb"]
                      t = qkv * 4 + h
                      ch0 = t * 128
                      lo_c, hi_c = tb * 512, tb * 512 + 516
                      d0, d1 = 0, 516
                      if tb == 0:
                          lo_c, d0 = 2, 2
                          s.op("pool", lambda e: e.memset(xin[:, 0:2], 0.0), [r_xin], [r_xin])
                      if tb == 7:
                          hi_c, d1 = L + 2, 514
                          s.op("pool", lambda e: e.memset(xin[:, 514:516], 0.0), [r_xin], [r_xin])
                      s.dma("sp", xin[:, d0:d1], c.cin[sq_i, ch0:ch0 + 128, lo_c:hi_c], reads=[c.r_cin[sq_i]], writes=[r_xin])
                      yield
                      pp, r_pp = c.nextbank()
                      with s.group("pe"):
                          for jx in range(5):
                              s.op("pe", lambda e, jx=jx: e.matmul(pp[:], lhsT=dg[:, t * 5 + jx, :], rhs=xin[:, jx:jx + 512], start=(jx == 0), stop=(jx == 4)),
                                   [r_dg, r_xin], [r_pp])
                      yield
                      if qkv == 2:
                          s.op("act", lambda e: e.activation(out=knb[:], in_=pp[:], func=AF.Silu), [r_pp, r_knb], [r_knb])
                          yield
                          yield
                          yield
                          yield
                          yield
                          yield
                          pt, r_pt = c.nextbank()
                          ptb = pt[:].bitcast(BF16)
                          with s.group("pe"):
                              for k in range(4):
                                  s.op("pe", lambda e, k=k: e.transpose(ptb[:, k * 128:(k + 1) * 128], knb[:, k * 128:(k + 1) * 128], c.ident[:]),
                                       [r_knb, c.r_ident], [r_pt])
                          yield
                          s.op("dve", lambda e: e.tensor_copy(T["tst"][:], ptb[:, 0:512].rearrange("p (a b) -> p a b", a=4)), [r_pt, T["r_tst"]], [T["r_tst"]])
                          yield
                          s.dma("sp", c.vtd[sq_i].rearrange("(t p) (h d) -> p t h d", p=128, h=4)[:, tb * 4:(tb + 1) * 4, h, :], T["tst"][:],
                                reads=[T["r_tst"]], writes=[c.r_vtd[sq_i]])
                      else:
                          s.op("act", lambda e: e.activation(out=yb[:], in_=pp[:], func=AF.Silu), [r_pp, r_yb], [r_yb])
                          yield
                          s.op("dve", lambda e: e.tensor_tensor(out=sqb[:], in0=yb[:], in1=yb[:], op=ALU.mult), [r_yb, r_sqb], [r_sqb])
                          yield
                          p2, r_p2 = c.nextbank()
                          s.op("pe", lambda e: e.matmul(p2[:], lhsT=ones_b[:], rhs=sqb[:], start=True, stop=True), [r_onesb, r_sqb], [r_p2])
                          yield
                          s.op("act", lambda e: e.activation(out=rnb[:], in_=p2[:], func=AF.Ln, bias=EPS), [r_p2, r_rnb], [r_rnb])
                          yield
                          s.op("act", lambda e: e.activation(out=rnb[:], in_=rnb[:], func=AF.Exp, scale=-0.5), [r_rnb], [r_rnb])
                          yield
                          if qkv == 0:
                              s.op("dve", lambda e: e.scalar_tensor_tensor(
                                  out=kqT[:, h, tb * 8:(tb + 1) * 8, 64:128], in0=yb[:].rearrange("p (n c) -> p n c", n=8), scalar=128 ** -0.5,
                                  in1=rnb[:].rearrange("p (n c) -> p n c", n=8), op0=ALU.mult, op1=ALU.mult), [r_yb, r_rnb], [r_kqT])
                          else:
                              s.op("dve", lambda e: e.tensor_tensor(out=knb[:], in0=yb[:], in1=rnb[:], op=ALU.mult), [r_yb, r_rnb, r_knb], [r_knb])
                              yield
                              s.op("dve", lambda e: e.tensor_copy(kqT[:, h, tb * 8:(tb + 1) * 8, 0:64], knb[:].rearrange("p (n c) -> p n c", n=8)),
                                   [r_knb], [r_kqT])
                              pt, r_pt = c.nextbank()
                              ptb = pt[:].bitcast(BF16)
                              with s.group("pe"):
                                  for k in range(4):
                                      s.op("pe", lambda e, k=k: e.transpose(ptb[:, k * 128:(k + 1) * 128], knb[:, k * 128:(k + 1) * 128], c.ident[:]),
                                           [r_knb, c.r_ident], [r_pt])
                              yield
                              s.op("act", lambda e: e.copy(T["tst"][:], ptb[:, 0:512].rearrange("p (a b) -> p a b", a=4)), [r_pt, T["r_tst"]], [T["r_tst"]])
                              yield
                              s.dma("sp", c.ktd[sq_i].rearrange("(t p) (h d) -> p t h d", p=128, h=4)[:, tb * 4:(tb + 1) * 4, h, :], T["tst"][:],
                                    reads=[T["r_tst"]], writes=[c.r_ktd[sq_i]])

                  units = [(tb, h, qkv) for tb in range(8) for h in range(4) for qkv in range(3)]
                  _rolling(units, lambda u, k: c0_unit(*u, T0[k]), NS0)
                  s.barrier()
            with ExitStack() as e1:
              if "1" in CPARTS or "2" in CPARTS:
                  def sb1(name, shape, dt):
                      return e1.enter_context(nc.sbuf_tensor(_u(name), shape, dt))
                  NSLOT = 2
                  slots = []
                  for k in range(NSLOT):
                      T = {}
                      for nm, shp, dt in (("RG0", [128, 2, 4, 64], F32), ("RG1", [128, 2, 4, 64], F32), ("RB", [128, 2, 4, 64], F32),
                                          ("zc", [128, 32], F32), ("beg", [128, 8], F32),
                                          ("Nbd", [128, 8, 128], BF16), ("NTbd", [128, 8, 128], BF16), ("Pb", [128, 8, 128], BF16),
                                          ("Qb", [128, 8, 128], BF16), ("Zb0", [128, 8, 128], BF16), ("Zb1", [128, 8, 128], BF16),
                                          ("bv", [128, 8, 128], BF16), ("kgb", [128, 8, 128], BF16), ("rec", [128, 4096], BF16),
                                          ("kt", [128, 4, 128], BF16), ("vt", [128, 4, 128], BF16)):
                          T[nm] = sb1(f"{nm}_{k}", shp, dt)
                          T["r_" + nm] = Res()
                      s.op("pool", lambda e, T=T: e.memset(T["Nbd"][:], 0.0), [], [T["r_Nbd"]])
                      s.op("pool", lambda e, T=T: e.memset(T["NTbd"][:], 0.0), [], [T["r_NTbd"]])
                      slots.append(T)
                  bc = lambda ap: ap.unsqueeze(3).to_broadcast([128, 2, 4, 64])
                  bm = lambda m: m.unsqueeze(2).to_broadcast([128, 2, 4, 64])
                  LTm = (BD_GT, BD_LT)
                  LT2m = (BD_LE, BD_GE)
                  done_c1 = set()
                  done_o = [set(), set()]
                  r_zst_t = [Res() for _ in range(NT)]
                  r_ost_t = {(d, n): Res() for d in range(2) for n in range(NCH)}

                  def c1_tile(i, T):
                      RG0, RG1, RB, zc, beg = T["RG0"], T["RG1"], T["RB"], T["zc"], T["beg"]
                      r_RG0, r_RG1, r_RB, r_zc, r_beg = T["r_RG0"], T["r_RG1"], T["r_RB"], T["r_zc"], T["r_beg"]
                      Nbd, NTbd, r_Nbd, r_NTbd = T["Nbd"], T["NTbd"], T["r_Nbd"], T["r_NTbd"]
                      rec, r_rec = T["rec"], T["r_rec"]
                      r_sc = r_scl[i]
                      recv = rec[:].rearrange("p (d x) -> p d x", d=2)
                      kt, vt, r_kt, r_vt = T["kt"], T["vt"], T["r_kt"], T["r_vt"]
                      s.dma("sp", kt[:].rearrange("p h d -> p (h d)"), c.ktd[sq_i, i * 128:(i + 1) * 128, :], reads=[c.r_ktd[sq_i]], writes=[r_kt])
                      s.dma("sp", vt[:].rearrange("p h d -> p (h d)"), c.vtd[sq_i, i * 128:(i + 1) * 128, :], reads=[c.r_vtd[sq_i]], writes=[r_vt])
                      g8 = GT[:, i, 8:16]
                      gv = g8.rearrange("p (d h) -> p d h", d=2)
                      beta8 = GT[:, i, 0:8]
                      beta = beta8.rearrange("p (d h) -> p d h", d=2)
                      s.op("dve", lambda e: e.tensor_tensor(out=RG0[:], in0=bc(gv), in1=bm(Ma), op=ALU.mult), [r_GT, r_mske, r_RG0], [r_RG0])
                      s.op("pool", lambda e: e.tensor_tensor(out=RG1[:], in0=bc(gv), in1=bm(Mb), op=ALU.mult), [r_GT, r_mske, r_RG1], [r_RG1])
                      s.op("pool", lambda e: e.tensor_tensor(out=RB[:], in0=bc(beta), in1=EQ.unsqueeze(1).unsqueeze(1).to_broadcast([128, 2, 4, 64]), op=ALU.mult),
                           [r_GT, r_mske, r_RB], [r_RB])
                      yield
                      D1, r_D1, D2, r_D2, tI, r_tI = RG0, r_RG0, RG1, r_RG1, RB, r_RB
                      pX, r_pX = yield from c.getbank()
                      for d in range(2):
                          s.op("pe", lambda e, d=d: e.matmul(pX[:, d * 256:(d + 1) * 256], lhsT=LTm[d], rhs=RG0[:, d].rearrange("p h c -> p (h c)"),
                                                            start=True, stop=True), [r_mskf, r_RG0], [r_pX])
                      yield
                      s.op("act", lambda e: e.activation(out=D1[:].rearrange("p d h c -> p (d h c)"), in_=pX[:], func=AF.Exp), [r_pX, r_D1], [r_D1])
                      c.relbank(pX)
                      yield
                      pY, r_pY = yield from c.getbank()
                      for d in range(2):
                          s.op("pe", lambda e, d=d: e.matmul(pY[:, d * 256:(d + 1) * 256], lhsT=LT2m[d], rhs=RG1[:, d].rearrange("p h c -> p (h c)"),
                                                            start=True, stop=True), [r_mskf, r_RG1], [r_pY])
                      yield
                      s.op("act", lambda e: e.activation(out=D2[:].rearrange("p d h c -> p (d h c)"), in_=pY[:], func=AF.Exp), [r_pY, r_D2], [r_D2])
                      c.relbank(pY)
                      yield
                      pZ, r_pZ = yield from c.getbank()
                      with s.group("pe"):
                          s.op("pe", lambda e: e.matmul(pZ[:, 0:4], lhsT=BD_LE, rhs=g8[:, 0:4], start=True, stop=True), [r_mskf, r_GT], [r_pZ])
                          s.op("pe", lambda e: e.matmul(pZ[:, 4:8], lhsT=BD_GE, rhs=g8[:, 4:8], start=True, stop=True), [r_mskf, r_GT], [r_pZ])
                          s.op("pe", lambda e: e.matmul(pZ[:, 8:16], lhsT=BD_ONES, rhs=g8, start=True, stop=True), [r_mskf, r_GT], [r_pZ])
                          s.op("pe", lambda e: e.matmul(pZ[:, 16:24], lhsT=CH_A, rhs=g8, start=True, stop=True), [r_mskf, r_GT], [r_pZ])
                          s.op("pe", lambda e: e.matmul(pZ[:, 24:32], lhsT=CH_B, rhs=g8, start=True, stop=True), [r_mskf, r_GT], [r_pZ])
                      yield
                      s.op("act", lambda e: e.copy(zc[:], pZ[:, 0:32]), [r_pZ, r_zc], [r_zc])
                      c.relbank(pZ)
                      yield
                      s.op("dve", lambda e: e.tensor_tensor(out=zc[:, 8:16], in0=zc[:, 8:16], in1=zc[:, 0:8], op=ALU.subtract), [r_zc], [r_zc])
                      yield
                      s.op("act", lambda e: e.activation(out=sc[:, i, :], in_=zc[:], func=AF.Exp), [r_zc, r_sc], [r_sc])
                      yield
                      s.op("dve", lambda e: e.tensor_tensor(out=beg[:], in0=beta8, in1=sc[:, i, 0:8], op=ALU.mult), [r_GT, r_sc, r_beg], [r_beg])
                      s.op("pool", lambda e: e.tensor_tensor(out=recv[:, :, 1280:1536].rearrange("p d (h c) -> p d h c", h=4),
                                                             in0=bc(sc[:, i, 0:8].rearrange("p (d h) -> p d h", d=2)),
                                                             in1=EQ.unsqueeze(1).unsqueeze(1).to_broadcast([128, 2, 4, 64]), op=ALU.mult),
                           [r_sc, r_mske, r_rec], [r_rec])
                      yield
                      pB, r_pB = yield from c.getbank()
                      s.op("pe", lambda e: e.matmul(pB[:], lhsT=BD_ONES, rhs=RB[:].rearrange("p d h c -> p (d h c)"), start=True, stop=True), [r_mskf, r_RB], [r_pB])
                      yield
                      pW, r_pW = yield from c.getbank()
                      with s.group("pe"):
                          for h in range(4):
                              for a in range(2):
                                  n = 2 * i + a
                                  s.op("pe", lambda e, h=h, a=a, n=n: e.matmul(pW[a * 64:(a + 1) * 64, h * 128:(h + 1) * 128], lhsT=kqT[:, h, n, 0:64],
                                                                             rhs=kqT[:, h, n, :], start=True, stop=True), [r_kqT], [r_pW])
                      yield
                      Wv = pW[:].rearrange("p (h x) -> p h x", h=4)
                      KK = Wv[:, :, 0:64].unsqueeze(1).to_broadcast([128, 2, 4, 64])
                      KQ = Wv[:, :, 64:128].unsqueeze(1).to_broadcast([128, 2, 4, 64])
                      s.op("dve", lambda e: e.tensor_tensor(out=tI[:], in0=D1[:], in1=KQ, op=ALU.mult), [r_D1, r_pW, r_tI], [r_tI])
                      yield
                      s.op("dve", lambda e: e.tensor_tensor(out=D1[:], in0=D1[:], in1=KK, op=ALU.mult), [r_D1, r_pW], [r_D1])
                      yield
                      s.op("dve", lambda e: e.tensor_tensor(out=D2[:], in0=D2[:], in1=KK, op=ALU.mult), [r_D2, r_pW], [r_D2])
                      c.relbank(pW)
                      yield
                      s.op("dve", lambda e: e.tensor_tensor(out=D1[:], in0=D1[:], in1=pB[:].rearrange("p (d h c) -> p d h c", d=2, h=4), op=ALU.mult), [r_D1, r_pB], [r_D1])
                      c.relbank(pB)
                      yield
                      s.op("pool", lambda e: e.tensor_tensor(out=D2[:], in0=D2[:], in1=bc(beta), op=ALU.mult), [r_D2, r_GT], [r_D2])
                      yield
                      s.op("pool", lambda e: e.tensor_tensor(out=recv[:, :, 1024:1280].rearrange("p d (h c) -> p d h c", h=4), in0=tI[:], in1=bm(Ma), op=ALU.mult),
                           [r_tI, r_mske, r_rec], [r_rec])
                      for a in range(2):
                          pa = slice(a * 64, (a + 1) * 64)
                          ca = slice(a * 64, (a + 1) * 64)
                          s.op("dve" if a == 0 else "pool", lambda e, pa=pa, ca=ca: e.tensor_tensor(
                              out=Nbd[pa, :, ca].rearrange("p (d h) c -> p d h c", d=2), in0=D1[pa], in1=bm(Mc)[pa], op=ALU.mult),
                              [r_D1, r_mske, r_Nbd], [r_Nbd])
                          s.op("pool" if a == 0 else "dve", lambda e, pa=pa, ca=ca: e.tensor_tensor(
                              out=NTbd[pa, :, ca].rearrange("p (d h) c -> p d h c", d=2), in0=D2[pa], in1=bm(Mb)[pa], op=ALU.mult),
                              [r_D2, r_mske, r_NTbd], [r_NTbd])
                          yield
                      Zb = [T["Zb0"], T["Zb1"]]
                      r_Zb = [T["r_Zb0"], T["r_Zb1"]]
                      s.op("pool", lambda e: e.tensor_tensor(out=Zb[0][:], in0=c.ident[:].unsqueeze(1).to_broadcast([128, 8, 128]), in1=Nbd[:], op=ALU.subtract),
                           [c.r_ident, r_Nbd, r_Zb[0]], [r_Zb[0]])
                      bv, r_bv, kgb, r_kgb = T["bv"], T["r_bv"], T["kgb"], T["r_kgb"]
                      bc128 = lambda ap: ap.rearrange("p (d h) -> p d h", d=2).unsqueeze(3).to_broadcast([128, 2, 4, 128])
                      s.op("pool", lambda e: e.tensor_tensor(out=bv[:].rearrange("p (d h) x -> p d h x", d=2), in0=vt[:].unsqueeze(1).to_broadcast([128, 2, 4, 128]),
                                                             in1=bc128(beta8), op=ALU.mult), [r_vt, r_GT, r_bv], [r_bv])
                      yield
                      s.op("pool", lambda e: e.tensor_tensor(out=kgb[:].rearrange("p (d h) x -> p d h x", d=2), in0=kt[:].unsqueeze(1).to_broadcast([128, 2, 4, 128]),
                                                             in1=bc128(beg[:]), op=ALU.mult), [r_kt, r_beg, r_kgb], [r_kgb])
                      yield
                      s.op("pool", lambda e: e.tensor_tensor(out=recv[:, :, 1536:2048].rearrange("p d (h x) -> p d h x", h=4), in0=kt[:].unsqueeze(1).to_broadcast([128, 2, 4, 128]),
                                                             in1=bc128(sc[:, i, 8:16]), op=ALU.mult), [r_kt, r_sc, r_rec], [r_rec])
                      yield
                      Pseq = [(NTbd, r_NTbd), (T["Pb"], T["r_Pb"])]
                      Qseq = [(Nbd, r_Nbd), (T["Qb"], T["r_Qb"])]
                      zi = 0
                      for lev in range(1, 6):
                          Pc, r_Pc = Pseq[(lev - 1) % 2]
                          Qc, r_Qc = Qseq[(lev - 1) % 2]
                          Pn, r_Pn = Pseq[lev % 2]
                          Qn, r_Qn = Qseq[lev % 2]
                          def mmP(half, pp, r_pp):
                              with s.group("pe"):
                                  for k in range(4):
                                      p = half * 4 + k
                                      s.op("pe", lambda e, p=p, k=k: e.matmul(pp[:, k * 128:(k + 1) * 128], lhsT=Qc[:, p, :], rhs=Pc[:, p, :],
                                                                            start=True, stop=True), [r_Pc, r_Qc], [r_pp])

                          def mmQ(half, pp, r_pp):
                              with s.group("pe"):
                                  for k in range(4):
                                      p = half * 4 + k
                                      s.op("pe", lambda e, p=p, k=k: e.matmul(pp[:, k * 128:(k + 1) * 128], lhsT=Pc[:, p, :], rhs=Qc[:, p, :],
                                                                            start=True, stop=True), [r_Pc, r_Qc], [r_pp])

                          def evP(half, pp, r_pp):
                              s.op("act", lambda e: e.copy(Pn[:, half * 4:(half + 1) * 4, :], pp[:].rearrange("p (k x) -> p k x", k=4)), [r_pp, r_Pn], [r_Pn])
                              c.relbank(pp)

                          def evQ(half, pp, r_pp):
                              s.op("act", lambda e: e.copy(Qn[:, half * 4:(half + 1) * 4, :], pp[:].rearrange("p (k x) -> p k x", k=4)), [r_pp, r_Qn], [r_Qn])
                              c.relbank(pp)
                          pa0, r_pa0 = yield from c.getbank()
                          mmP(0, pa0, r_pa0)
                          yield
                          pa1, r_pa1 = yield from c.getbank()
                          mmP(1, pa1, r_pa1)
                          yield
                          evP(0, pa0, r_pa0)
                          yield
                          if lev < 5:
                              qa0, r_qa0 = yield from c.getbank()
                              mmQ(0, qa0, r_qa0)
                              yield
                          evP(1, pa1, r_pa1)
                          yield
                          if lev < 5:
                              qa1, r_qa1 = yield from c.getbank()
                              mmQ(1, qa1, r_qa1)
                              yield
                              evQ(0, qa0, r_qa0)
                              yield
                              evQ(1, qa1, r_qa1)
                              yield
                          Zc, r_Zc = Zb[zi], r_Zb[zi]
                          Zn, r_Zn = Zb[1 - zi], r_Zb[1 - zi]
                          for half in range(2):
                              pp, r_pp = yield from c.getbank()
                              with s.group("pe"):
                                  for k in range(4):
                                      p = half * 4 + k
                                      s.op("pe", lambda e, p=p, k=k, pp=pp, Zc=Zc, Pn=Pn: e.matmul(pp[:, k * 128:(k + 1) * 128], lhsT=Pn[:, p, :], rhs=Zc[:, p, :],
                                                                                            start=True, stop=True), [r_Pn, r_Zc], [r_pp])
                              yield
                              ppv = pp[:].rearrange("p (k x) -> p k x", k=4)
                              s.op("dve", lambda e, ppv=ppv, half=half, Zn=Zn, Zc=Zc: e.tensor_tensor(out=Zn[:, half * 4:(half + 1) * 4, :], in0=ppv,
                                                                                                     in1=Zc[:, half * 4:(half + 1) * 4, :], op=ALU.add),
                                   [r_pp, r_Zc, r_Zn], [r_Zn])
                              c.relbank(pp)
                              yield
                          zi = 1 - zi
                      Z2, r_Z2 = Zb[zi], r_Zb[zi]
                      for half in range(2):
                          pp, r_pp = yield from c.getbank()
                          with s.group("pe"):
                              for k in range(4):
                                  p = half * 4 + k
                                  s.op("pe", lambda e, p=p, k=k, pp=pp: e.matmul(pp[:, k * 128:(k + 1) * 128], lhsT=Z2[:, p, :], rhs=bv[:, p, :], start=True, stop=True),
                                       [r_Z2, r_bv], [r_pp])
                          yield
                          s.op("act", lambda e, pp=pp, half=half: e.copy(recv[:, half, 0:512], pp[:]), [r_pp, r_rec], [r_rec])
                          c.relbank(pp)
                          yield
                      for half in range(2):
                          pp, r_pp = yield from c.getbank()
                          with s.group("pe"):
                              for k in range(4):
                                  p = half * 4 + k
                                  s.op("pe", lambda e, p=p, k=k, pp=pp: e.matmul(pp[:, k * 128:(k + 1) * 128], lhsT=kgb[:, p, :], rhs=Z2[:, p, :], start=True, stop=True),
                                       [r_kgb, r_Z2], [r_pp])
                          yield
                          s.op("act", lambda e, pp=pp, half=half: e.copy(recv[:, half, 512:1024], pp[:]), [r_pp, r_rec], [r_rec])
                          c.relbank(pp)
                          yield
                      s.dma("pool", c.zst[sq_i, i].rearrange("d p x -> p d x"), recv, reads=[r_rec], writes=[r_zst_t[i]])

                  sb2 = sb1
                  zb = [[sb2(f"zbuf{d}_{k}", [128, 2048], BF16) for k in range(2)] for d in range(2)]
                  r_zb = [[Res() for k in range(2)] for d in range(2)]
                  S = [sb2(f"S{d}", [128, 4, 128], F32) for d in range(2)]; r_S = [Res() for _ in range(2)]
                  Sb = [sb2(f"Sb{d}", [128, 4, 128], BF16) for d in range(2)]; r_Sb = [Res() for _ in range(2)]
                  vnew = [sb2(f"vnew_{d}", [128, 4, 128], BF16) for d in range(2)]; r_vnew = [Res() for _ in range(2)]
                  p1sb = [sb2(f"p1sb_{d}", [128, 4, 128], BF16) for d in range(2)]; r_p1sb = [Res() for _ in range(2)]
                  ot = [sb2(f"ot_{d}", [128, 4, 128], F32) for d in range(2)]; r_ot = [Res() for _ in range(2)]
                  for d in range(2):
                      s.op("pool", lambda e, d=d: e.memset(S[d][:], 0.0), [r_S[d]], [r_S[d]])
                      s.op("pool", lambda e, d=d: e.memset(Sb[d][:], 0.0), [r_Sb[d]], [r_Sb[d]])
                  bc4 = lambda ap: ap.unsqueeze(2).to_broadcast([64, 4, 128])

                  def c2_step(t, d):
                      n = t if d == 0 else NCH - 1 - t
                      i, a = n // 2, n % 2
                      pa = slice(a * 64, (a + 1) * 64)
                      rows = slice(n * 64, (n + 1) * 64)
                      zt, r_zt = zb[d][i % 2], r_zb[d][i % 2]
                      r_sc = r_scl[i]
                      first_of_tile = (a == 0) if d == 0 else (a == 1)
                      if first_of_tile:
                          if t == 0:
                              while i not in done_c1:
                                  yield
                              s.dma("sp", zt[:], c.zst[sq_i, i, d], reads=[r_zst_t[i]], writes=[r_zt])
                          inx = i + 1 if d == 0 else i - 1
                          if 0 <= inx < NT:
                              while inx not in done_c1:
                                  yield
                              s.dma("sp", zb[d][inx % 2][:], c.zst[sq_i, inx, d], reads=[r_zst_t[inx]], writes=[r_zb[d][inx % 2]])
                      U = zt[:, 0:512].rearrange("p (q x) -> p q x", q=4)
                      Wt = zt[:, 512:1024].rearrange("p (q x) -> p q x", q=4)
                      IT = zt[:, 1024:1280].rearrange("p (q x) -> p q x", q=4)
                      DG = zt[:, 1280:1536].rearrange("p (q x) -> p q x", q=4)
                      KD = zt[:, 1536:2048].rearrange("p (q x) -> p q x", q=4)
                      eGt = sc[:, i, 16 + a * 8 + d * 4:16 + a * 8 + (d + 1) * 4]
                      yield
                      s.op("pool", lambda e: e.tensor_tensor(out=S[d][:], in0=S[d][:], in1=eGt.unsqueeze(2).to_broadcast([128, 4, 128]), op=ALU.mult),
                           [r_S[d], r_sc], [r_S[d]])
                      yield
                      pWS, r_pWS = yield from c.getbank()
                      pP1, r_pP1 = yield from c.getbank()
                      with s.group("pe"):
                          for h in range(4):
                              s.op("pe", lambda e, h=h: e.matmul(pWS[pa, h * 128:(h + 1) * 128], lhsT=Wt[:, h, a * 64:(a + 1) * 64], rhs=Sb[d][:, h, :],
                                                                 start=True, stop=True), [r_zt, r_Sb[d]], [r_pWS])
                      yield
                      with s.group("pe"):
                          for h in range(4):
                              s.op("pe", lambda e, h=h: e.matmul(pP1[pa, h * 128:(h + 1) * 128], lhsT=kqT[:, h, n, 64:128], rhs=Sb[d][:, h, :],
                                                                 start=True, stop=True), [r_kqT, r_Sb[d]], [r_pP1])
                      yield
                      v3 = lambda bank: bank[pa, :].rearrange("p (h e) -> p h e", h=4)
                      s.op("dve", lambda e: e.tensor_tensor(out=vnew[d][pa], in0=U[pa, :, :], in1=v3(pWS), op=ALU.subtract),
                           [r_zt, r_pWS, r_vnew[d]], [r_vnew[d]])
                      c.relbank(pWS)
                      yield
                      s.op("act", lambda e: e.copy(p1sb[d][pa], v3(pP1)), [r_pP1, r_p1sb[d]], [r_p1sb[d]])
                      c.relbank(pP1)
                      yield
                      pSU, r_pSU = yield from c.getbank()
                      pP2, r_pP2 = yield from c.getbank()
                      with s.group("pe"):
                          for h in range(4):
                              s.op("pe", lambda e, h=h: e.matmul(pSU[:, h * 128:(h + 1) * 128], lhsT=KD[pa, h, :], rhs=vnew[d][pa, h, :],
                                                                 start=True, stop=True), [r_zt, r_vnew[d]], [r_pSU])
                      yield
                      with s.group("pe"):
                          for h in range(4):
                              s.op("pe", lambda e, h=h: e.matmul(pP2[pa, h * 128:(h + 1) * 128], lhsT=DG[pa, h, :], rhs=p1sb[d][pa, h, :],
                                                                 start=True, stop=False), [r_zt, r_p1sb[d]], [r_pP2])
                              s.op("pe", lambda e, h=h: e.matmul(pP2[pa, h * 128:(h + 1) * 128], lhsT=IT[pa, h, :], rhs=vnew[d][pa, h, :],
                                                                 start=False, stop=True), [r_zt, r_vnew[d]], [r_pP2])
                      yield
                      s.op("dve", lambda e: e.tensor_tensor(out=S[d][:], in0=pSU[:].rearrange("p (h e) -> p h e", h=4), in1=S[d][:], op=ALU.add),
                           [r_pSU, r_S[d]], [r_S[d]])
                      c.relbank(pSU)
                      yield
                      s.op("act", lambda e: e.copy(Sb[d][:], S[d][:]), [r_S[d], r_Sb[d]], [r_Sb[d]])
                      yield
                      s.op("act", lambda e: e.copy(ot[d][pa], v3(pP2)), [r_pP2, r_ot[d]], [r_ot[d]])
                      c.relbank(pP2)
                      yield
                      s.dma("sp", c.ost[sq_i, d, rows, :], ot[d][pa].rearrange("p h e -> p (h e)"), reads=[r_ot[d]], writes=[r_ost_t[(d, n)]])
                      done_o[d].add(n)

                  def c2_dir(d):
                      for t in range(NCH):
                          yield from c2_step(t, d)

                  order = []
                  for k in range(NT // 2):
                      order += [k, NT - 1 - k]

                  def c1_worker(w):
                      for _ in range(w * 45):
                          yield
                      for i in order[w::NSLOT]:
                          yield from c1_tile(i, slots[w])
                          done_c1.add(i)
                  NS3 = 3
                  T3 = []
                  for k in range(NS3):
                      T = {}
                      for nm, shp, dt in (("of", [128, 4, 128], F32), ("ob", [128, 4, 128], F32), ("gt", [128, 4, 128], BF16), ("gd", [128, 4, 128], F32),
                                          ("jk", [128, 128], BF16), ("rs", [128, 8], F32), ("fo", [128, 512], BF16)):
                          T[nm] = sb1(f"{nm}3_{k}", shp, dt)
                          T["r_" + nm] = Res()
                      T3.append(T)
                  ss3 = sb1("ss3", [128, NT, 4], F32); r_ss3 = Res()
                  s.op("pool", lambda e: e.memset(ss3[:], 0.0), [], [r_ss3])

                  def c3_tile(i, T):
                      rows = slice(i * 128, (i + 1) * 128)
                      of, ob, gt, gd, jk, rs, fo = T["of"], T["ob"], T["gt"], T["gd"], T["jk"], T["rs"], T["fo"]
                      s.dma("sp", of[:].rearrange("p h e -> p (h e)"), c.ost[sq_i, 0, rows, :], reads=[r_ost_t[(0, 2 * i)], r_ost_t[(0, 2 * i + 1)]], writes=[T["r_of"]])
                      s.dma("act", ob[:].rearrange("p h e -> p (h e)"), c.ost[sq_i, 1, rows, :], reads=[r_ost_t[(1, 2 * i)], r_ost_t[(1, 2 * i + 1)]], writes=[T["r_ob"]])
                      s.dma("sp", gt[:].rearrange("p h e -> p (h e)"), c.mix[sq_i, rows, 512:1024], reads=[c.r_mix[sq_i]], writes=[T["r_gt"]])
                      yield
                      s.op("dve", lambda e: e.tensor_tensor(out=of[:], in0=of[:], in1=ob[:], op=ALU.add), [T["r_of"], T["r_ob"]], [T["r_of"]])
                      yield
                      s.op("pool", lambda e: e.tensor_tensor(out=gd[:], in0=gt[:], in1=dnw[:].unsqueeze(1).to_broadcast([128, 4, 128]), op=ALU.mult),
                           [T["r_gt"], r_dnw, T["r_gd"]], [T["r_gd"]])
                      yield
                      for h in range(4):
                          s.op("act", lambda e, h=h: e.activation(out=jk[:], in_=of[:, h, :], func=AF.Square, accum_out=ss3[:, i, h:h + 1]),
                               [T["r_of"], T["r_jk"], r_ss3], [T["r_jk"], r_ss3])
                          yield
                      s.op("act", lambda e: e.activation(out=rs[:, 0:4], in_=ss3[:, i, :], func=AF.Ln, bias=EPS, scale=1.0 / 128), [r_ss3, T["r_rs"]], [T["r_rs"]])
                      yield
                      s.op("act", lambda e: e.activation(out=rs[:, 4:8], in_=rs[:, 0:4], func=AF.Exp, scale=-0.5), [T["r_rs"]], [T["r_rs"]])
                      yield
                      s.op("dve", lambda e: e.tensor_tensor(out=of[:], in0=of[:], in1=rs[:, 4:8].unsqueeze(2).to_broadcast([128, 4, 128]), op=ALU.mult),
                           [T["r_of"], T["r_rs"]], [T["r_of"]])
                      yield
                      s.op("pool", lambda e: e.tensor_tensor(out=fo[:].rearrange("p (h e) -> p h e", h=4), in0=of[:], in1=gd[:], op=ALU.mult),
                           [T["r_of"], T["r_gd"], T["r_fo"]], [T["r_fo"]])
                      yield
                      s.dma("sp", c.mix[sq_i, rows, 512:1024], fo[:], reads=[T["r_fo"]], writes=[c.r_mix[sq_i]])


                  c3_order = sorted(range(NT), key=lambda i: (max(2 * i + 1, 63 - 2 * i), i))

                  def c3_worker(w):
                      for i in c3_order[w::NS3]:
                          while not all((2 * i + a) in done_o[d] for d in range(2) for a in range(2)):
                              yield
                          yield from c3_tile(i, T3[w])
                  c.free_banks = list(range(8))
                  _interleave([c1_worker(w) for w in range(NSLOT)] + [c2_dir(0), c2_dir(1)] + [c3_worker(w) for w in range(NS3)],
                              weights=[1] * NSLOT + [2, 2] + [1] * NS3)
                  assert len(c.free_banks) == 8
                  s.barrier()


def _rolling(units, make_gen, nslots):
    pending = list(units)
    active = []
    free = list(range(nslots))
    while pending or active:
        while pending and free:
            k = free.pop(0)
            active.append((make_gen(pending.pop(0), k), k))
        for item in list(active):
            try:
                next(item[0])
            except StopIteration:
                active.remove(item)
                free.append(item[1])


def _interleave(gens, weights=None):
    gens = list(gens)
    wts = {id(g): (weights[k] if weights else 1) for k, g in enumerate(gens)}
    while gens:
        for g in list(gens):
            try:
                for _ in range(wts[id(g)]):
                    next(g)
            except StopIteration:
                gens.remove(g)


def _host_constants():
    ident = np.eye(128, dtype=np.float32)
    masks = np.zeros((128, 16, 128), np.float32)
    p = np.arange(128)[:, None]
    q = np.arange(128)[None, :]
    same = (p // 64 == q // 64)
    masks[:, 0, :] = same
    masks[:, 1, :] = same & (p > q)
    masks[:, 2, :] = same & (p <= q)
    masks[:, 3, :] = same & (p < q)
    masks[:, 4, :] = same & (p >= q)
    masks[:, 5, :] = (p < 64) & (q >= 0)
    masks[:, 6, :] = (p >= 64) & (q >= 0)
    pl = p % 64
    ql = q % 64
    first = q < 64
    masks[:, 7, :] = np.where(first, pl <= ql, pl >= ql)
    masks[:, 8, :] = np.where(first, pl > ql, pl < ql)
    masks[:, 9, :] = np.where(first, pl < ql, pl > ql)
    masks[:, 10, :] = (pl == ql)
    return ident, masks


def _rpb_gather(rpb):
    a = np.arange(2)[:, None, None, None]
    kc = np.arange(64)[None, :, None, None]
    t = np.arange(16)[None, None, :, None]
    qc = np.arange(64)[None, None, None, :]
    i = np.clip(a + 14 - t, 0, 14) + 0 * kc + 0 * qc
    jj = np.clip(kc - qc + 15, 0, 30) + 0 * a + 0 * t
    g = rpb[:, :, i, jj]
    return np.ascontiguousarray(g.reshape(2, 8, 128, 1024))


def _na_valid():
    v = np.zeros((2, 64, 16, 64), np.float32)
    for a in range(2):
        for t in range(16):
            i = a + 14 - t
            if not (0 <= i <= 14):
                continue
            for qc in range(64):
                cs = int(np.clip(qc - 8, 0, 48))
                v[a, cs:cs + 16, t, qc] = 1.0
    return v.reshape(128, 1024)


def make_in_maps(inputs, n_cores=NCORES, n_seq=SEQ_PER_CORE):
    f = lambda a: np.ascontiguousarray(np.asarray(a, dtype=np.float32))
    ident, masks = _host_constants()
    shared = {
        "norm_w": f(inputs["norm_w"]), "w_in": f(inputs["w_in"]),
        "qk_gain_q": f(inputs["qk_gain_q"]), "qk_gain_k": f(inputs["qk_gain_k"]),
        "rpb_g": _rpb_gather(f(inputs["rpb"])),
        "conv_wT": np.ascontiguousarray(f(inputs["conv_w"]).transpose(0, 2, 1)),
        "a_log": f(inputs["a_log"]).reshape(2, 8), "dt_bias": f(inputs["dt_bias"]).reshape(2, 8),
        "dn_norm_w": f(inputs["dn_norm_w"]), "w_out": f(inputs["w_out"]),
        "c_ident": ident, "c_masks": masks, "c_navalid": _na_valid(),
    }
    x = f(inputs["x"])
    maps = []
    for i in range(n_cores):
        m = dict(shared)
        m["x"] = np.ascontiguousarray(x[i * n_seq:(i + 1) * n_seq])
        maps.append(m)
    return maps


def kernel(**inputs):
    nc, c = build_program()
    in_maps = make_in_maps(inputs)
    res = run_bass_kernel_spmd(nc, in_maps, core_ids=list(range(NCORES)))
    out = np.concatenate([np.asarray(r["out"]) for r in res.results], axis=0)
    return out.astype(np.float32)
```
